# Optimizing a Trainium2 kernel written in Bass

```python
import math
import jax
import jax.numpy as jnp
from jax import lax
import numpy as np

D_MODEL = 1024
BATCH = 16
SEQ = 4096
DEPTH = 1
DEC_BATCH = 4
DEC_SEQ = 4096
PAST_LEN = 128

HEAD_DIM = 64
A_Q_HEADS = 8
A_KV_HEADS = 2
A_GROUP = A_Q_HEADS // A_KV_HEADS
WINDOW = 128
BLOCK = 128
B_HEADS = 4
B_DIM = 64
A_Q = A_Q_HEADS * HEAD_DIM
A_KV = A_KV_HEADS * HEAD_DIM
B_QK = B_HEADS * 2 * B_DIM
B_V = B_HEADS * 2 * B_DIM
A_OUT = A_Q
B_OUT = B_V
IN_SPLITS = (A_Q, A_Q + A_KV, A_Q + 2 * A_KV, A_Q + 2 * A_KV + B_QK, A_Q + 2 * A_KV + 2 * B_QK, A_Q + 2 * A_KV + 2 * B_QK + B_V, A_Q + 2 * A_KV + 2 * B_QK + B_V + D_MODEL)
IN_TOTAL = A_Q + 2 * A_KV + 2 * B_QK + B_V + 2 * D_MODEL
D_FF = ((8 * D_MODEL + 3 * 256 - 1) // (3 * 256)) * 256
ROPE_THETA = 10000.0
ALPHA = (2.0 * DEPTH) ** 0.25
BETA = (8.0 * DEPTH) ** -0.25
LN_EPS = 1e-5
RMS_EPS = 1e-5
NEG = -1e30

kernel_name = 'hybrid_gated_window_diff_encoder'


def lambda_init(layer):
    return 0.8 - 0.6 * math.exp(-0.3 * layer)


def layer_norm(x, g, b):
    xf = x.astype(jnp.float32)
    xc = xf - xf.mean(-1, keepdims=True)
    var = (xc * xc).mean(-1, keepdims=True)
    return (xc * lax.rsqrt(var + LN_EPS) * g.astype(jnp.float32) + b.astype(jnp.float32)).astype(x.dtype)


def head_rmsnorm(x, w):
    xf = x.astype(jnp.float32)
    y = xf * lax.rsqrt((xf * xf).mean(-1, keepdims=True) + RMS_EPS) * w.astype(jnp.float32)
    return y.astype(x.dtype)


def rope_tables(seq_len, dim):
    inv = 1.0 / (ROPE_THETA ** (jnp.arange(0, dim, 2, dtype=jnp.float32) / dim))
    ang = jnp.arange(seq_len, dtype=jnp.float32)[:, None] * inv[None, :]
    return jnp.cos(ang), jnp.sin(ang)


def apply_rope(x, cos, sin):
    shp = (cos.shape[0],) + (1,) * (x.ndim - 3) + (cos.shape[1],)
    c = cos.reshape(shp).astype(x.dtype)
    s = sin.reshape(shp).astype(x.dtype)
    x1, x2 = jnp.split(x, 2, axis=-1)
    return jnp.concatenate([x1 * c - x2 * s, x2 * c + x1 * s], axis=-1)


def window_gqa(q, k, v, sink):
    bsz, seq = q.shape[0], q.shape[1]
    nb = seq // BLOCK
    qb = q.reshape(bsz, nb, BLOCK, A_KV_HEADS, A_GROUP, HEAD_DIM)
    pad = ((0, 0), (1, 1), (0, 0), (0, 0), (0, 0))
    kp = jnp.pad(k.reshape(bsz, nb, BLOCK, A_KV_HEADS, HEAD_DIM), pad)
    vp = jnp.pad(v.reshape(bsz, nb, BLOCK, A_KV_HEADS, HEAD_DIM), pad)
    kb = jnp.concatenate([kp[:, :-2], kp[:, 1:-1], kp[:, 2:]], axis=2)
    vb = jnp.concatenate([vp[:, :-2], vp[:, 1:-1], vp[:, 2:]], axis=2)
    s = jnp.einsum('bnqhgd,bnkhd->bnhgqk', qb, kb).astype(jnp.float32) * (HEAD_DIM ** -0.5)
    blk = jnp.arange(nb)
    qpos = blk[:, None] * BLOCK + jnp.arange(BLOCK)[None, :]
    kpos = (blk[:, None] - 1) * BLOCK + jnp.arange(3 * BLOCK)[None, :]
    kp3 = kpos[:, None, :]
    valid = (jnp.abs(qpos[:, :, None] - kp3) <= WINDOW) & (kp3 >= 0) & (kp3 < seq)
    s = jnp.where(valid[None, :, None, None], s, NEG)
    sk = sink.astype(jnp.float32).reshape(1, 1, A_KV_HEADS, A_GROUP, 1, 1)
    m = jnp.maximum(s.max(-1, keepdims=True), sk)
    e = jnp.exp(s - m)
    p = e / (e.sum(-1, keepdims=True) + jnp.exp(sk - m))
    o = jnp.einsum('bnhgqk,bnkhd->bnqhgd', p.astype(v.dtype), vb)
    return o.reshape(bsz, seq, A_Q)


def diff_attention(q, k, v, lam):
    bsz, seq = q.shape[0], q.shape[1]
    nb = seq // BLOCK
    qb = q.reshape(bsz, nb, BLOCK, B_HEADS, 2, B_DIM).transpose(1, 0, 2, 3, 4, 5)
    scale = B_DIM ** -0.5

    def one_block(qblk):
        s = jnp.einsum('bqhcd,bkhcd->bhcqk', qblk, k).astype(jnp.float32) * scale
        p = jax.nn.softmax(s, axis=-1)
        a = p[:, :, 0] - lam * p[:, :, 1]
        return jnp.einsum('bhqk,bkhe->bqhe', a.astype(v.dtype), v)

    o = lax.map(one_block, qb)
    return o.transpose(1, 0, 2, 3, 4).reshape(bsz, seq, B_HEADS, 2 * B_DIM)


def encoder_layer(x, c, cos, sin, layer, w_ada, b_ada, w_in, sink_logit, lam_q1, lam_k1, lam_q2, lam_k2, subln_w, w_a, w_b, w_o, ln1_g, ln1_b, w_gu, w_down, ln2_g, ln2_b):
    bsz, seq, _ = x.shape
    mod = (jax.nn.silu(c) @ w_ada + b_ada).reshape(bsz, 6, 1, D_MODEL)
    shift_m, scale_m, gate_m = mod[:, 0], mod[:, 1], mod[:, 2]
    shift_f, scale_f, gate_f = mod[:, 3], mod[:, 4], mod[:, 5]

    h = x * (1.0 + scale_m) + shift_m
    qa, ka, va, qd, kd, vd, ga, gb = jnp.split(h @ w_in, IN_SPLITS, axis=-1)
    qa = apply_rope(qa.reshape(bsz, seq, A_Q_HEADS, HEAD_DIM), cos, sin)
    ka = apply_rope(ka.reshape(bsz, seq, A_KV_HEADS, HEAD_DIM), cos, sin)
    va = va.reshape(bsz, seq, A_KV_HEADS, HEAD_DIM)
    a_out = window_gqa(qa, ka, va, sink_logit) @ w_a

    lam_i = lambda_init(layer)
    lam = (jnp.exp(jnp.sum(lam_q1.astype(jnp.float32) * lam_k1.astype(jnp.float32)))
           - jnp.exp(jnp.sum(lam_q2.astype(jnp.float32) * lam_k2.astype(jnp.float32))) + lam_i)
    qd = apply_rope(qd.reshape(bsz, seq, B_HEADS, 2, B_DIM), cos, sin)
    kd = apply_rope(kd.reshape(bsz, seq, B_HEADS, 2, B_DIM), cos, sin)
    vd = vd.reshape(bsz, seq, B_HEADS, 2 * B_DIM)
    od = head_rmsnorm(diff_attention(qd, kd, vd, lam), subln_w) * (1.0 - lam_i)
    b_out = od.reshape(bsz, seq, B_OUT) @ w_b

    merged = jax.nn.sigmoid(ga) * a_out + jax.nn.sigmoid(gb) * b_out
    x = layer_norm(ALPHA * x + gate_m * (merged @ w_o), ln1_g, ln1_b)

    h = x * (1.0 + scale_f) + shift_f
    g, u = jnp.split(h @ w_gu, 2, axis=-1)
    f = (jax.nn.silu(g) * u) @ w_down
    return layer_norm(ALPHA * x + gate_f * f, ln2_g, ln2_b)


def run_trunk(x, c, w_ada, b_ada, w_in, sink_logit, lam_q1, lam_k1, lam_q2, lam_k2, subln_w, w_a, w_b, w_o, ln1_g, ln1_b, w_gu, w_down, ln2_g, ln2_b):
    cos, sin = rope_tables(x.shape[1], HEAD_DIM)
    for l in range(DEPTH):
        x = encoder_layer(x, c, cos, sin, l, w_ada[l], b_ada[l], w_in[l], sink_logit[l], lam_q1[l], lam_k1[l], lam_q2[l], lam_k2[l], subln_w[l], w_a[l], w_b[l], w_o[l], ln1_g[l], ln1_b[l], w_gu[l], w_down[l], ln2_g[l], ln2_b[l])
    return x


def setup_inputs(seed: int = 0) -> dict:
    key = jax.random.key(seed)
    ks = jax.random.split(key, 22)

    def nrm(k, shape, s):
        return jax.random.normal(k, shape, jnp.float32) * s

    return {
        'x_prompt': nrm(ks[0], (BATCH, SEQ, D_MODEL), 1.0),
        'x_sample': nrm(ks[1], (DEC_BATCH, DEC_SEQ, D_MODEL), 1.0),
        'c_prompt': nrm(ks[2], (BATCH, D_MODEL), 1.0),
        'c_sample': nrm(ks[3], (DEC_BATCH, D_MODEL), 1.0),
        'w_ada': nrm(ks[4], (DEPTH, D_MODEL, 6 * D_MODEL), 0.5 * D_MODEL ** -0.5),
        'b_ada': nrm(ks[5], (DEPTH, 6 * D_MODEL), 0.01),
        'w_in': nrm(ks[6], (DEPTH, D_MODEL, IN_TOTAL), D_MODEL ** -0.5),
        'sink_logit': nrm(ks[7], (DEPTH, A_Q_HEADS), 0.5),
        'lam_q1': nrm(ks[8], (DEPTH, B_DIM), 0.1),
        'lam_k1': nrm(ks[9], (DEPTH, B_DIM), 0.1),
        'lam_q2': nrm(ks[10], (DEPTH, B_DIM), 0.1),
        'lam_k2': nrm(ks[11], (DEPTH, B_DIM), 0.1),
        'subln_w': 1.0 + nrm(ks[12], (DEPTH, 2 * B_DIM), 0.02),
        'w_a': nrm(ks[13], (DEPTH, A_OUT, D_MODEL), BETA * A_OUT ** -0.5),
        'w_b': nrm(ks[14], (DEPTH, B_OUT, D_MODEL), BETA * B_OUT ** -0.5),
        'w_o': nrm(ks[15], (DEPTH, D_MODEL, D_MODEL), BETA * D_MODEL ** -0.5),
        'ln1_g': 1.0 + nrm(ks[16], (DEPTH, D_MODEL), 0.02),
        'ln1_b': nrm(ks[17], (DEPTH, D_MODEL), 0.02),
        'w_gu': nrm(ks[18], (DEPTH, D_MODEL, 2 * D_FF), D_MODEL ** -0.5),
        'w_down': nrm(ks[19], (DEPTH, D_FF, D_MODEL), BETA * D_FF ** -0.5),
        'ln2_g': 1.0 + nrm(ks[20], (DEPTH, D_MODEL), 0.02),
        'ln2_b': nrm(ks[21], (DEPTH, D_MODEL), 0.02),
    }


def reference(x_prompt, x_sample, c_prompt, c_sample, w_ada, b_ada, w_in, sink_logit, lam_q1, lam_k1, lam_q2, lam_k2, subln_w, w_a, w_b, w_o, ln1_g, ln1_b, w_gu, w_down, ln2_g, ln2_b):
    y_prompt = run_trunk(x_prompt, c_prompt, w_ada, b_ada, w_in, sink_logit, lam_q1, lam_k1, lam_q2, lam_k2, subln_w, w_a, w_b, w_o, ln1_g, ln1_b, w_gu, w_down, ln2_g, ln2_b)
    y_sample = run_trunk(x_sample, c_sample, w_ada, b_ada, w_in, sink_logit, lam_q1, lam_k1, lam_q2, lam_k2, subln_w, w_a, w_b, w_o, ln1_g, ln1_b, w_gu, w_down, ln2_g, ln2_b)
    return (y_prompt, y_sample)
```

```python
from contextlib import ExitStack
import types
import numpy as np
import concourse.bass as bass
import concourse.mybir as mybir
from concourse.bass_utils import run_bass_kernel_spmd

F32 = mybir.dt.float32
BF16 = mybir.dt.bfloat16
AF = mybir.ActivationFunctionType
ALU = mybir.AluOpType

D = 1024
DFF = 2816
NFF = 22
ALPHA = 2.0 ** 0.25
LAM_INIT = 0.2
EPS = 1e-5


def _freeze(fn):
    if fn.__closure__ is None:
        return fn
    cells = []
    for c in fn.__closure__:
        try:
            cells.append(types.CellType(c.cell_contents))
        except ValueError:
            cells.append(c)
    return types.FunctionType(fn.__code__, fn.__globals__, fn.__name__, fn.__defaults__, tuple(cells))


class Prog:
    ENG = ("pe", "act", "dve", "pool", "sp")

    def __init__(self):
        self.q = {e: [] for e in self.ENG}
        self.last_w = {}
        self.readers = {}
        self.chans = []

    def op(self, eng, fn, r=(), w=(), chan=None):
        idx = len(self.q[eng])
        deps = set()
        for b in r:
            lw = self.last_w.get(b)
            if lw is not None:
                deps.add(lw)
        for b in w:
            lw = self.last_w.get(b)
            if lw is not None:
                deps.add(lw)
            rd = self.readers.get(b)
            if rd:
                deps.update(rd.values())
        if chan is not None and chan not in self.chans:
            self.chans.append(chan)
        self.q[eng].append(dict(fn=_freeze(fn), deps=deps, chan=chan, signal=False, ticket=None))
        me = (eng, idx)
        for b in r:
            key = eng if chan is None else ("dma", chan)
            self.readers.setdefault(b, {})[key] = me
        for b in w:
            self.last_w[b] = me
            self.readers[b] = {}
        return me

    def emit(self, nc, stack, final_wait_chans=()):
        q = self.q
        for e in self.ENG:
            for ins in q[e]:
                for (e2, i2) in ins["deps"]:
                    q[e2][i2]["signal"] = True
        esem = {e: stack.enter_context(nc.semaphore("s_" + e)) for e in self.ENG if e != "sp"}
        csem = {c: stack.enter_context(nc.semaphore("c_" + str(c))) for c in self.chans}
        ccount = {c: 0 for c in self.chans}
        for e in self.ENG:
            cnt = 0
            for ins in q[e]:
                if ins["chan"] is not None:
                    ccount[ins["chan"]] += 16
                    ins["ticket"] = ccount[ins["chan"]]
                elif ins["signal"]:
                    cnt += 1
                    ins["ticket"] = cnt
        block = stack.enter_context(nc.Block())

        def run(e, eng):
            waited = {}
            for ins in q[e]:
                need = {}
                for (e2, i2) in ins["deps"]:
                    d = q[e2][i2]
                    if d["chan"] is not None:
                        s = csem[d["chan"]]
                        key = ("c", d["chan"])
                    else:
                        if e2 == e and e == "pe":
                            continue
                        s = esem[e2]
                        key = ("e", e2)
                    v = d["ticket"]
                    if v > need.get(key, (None, 0))[1]:
                        need[key] = (s, v)
                for key, (s, v) in need.items():
                    if waited.get(key, 0) >= v:
                        continue
                    eng.wait_ge(s, v)
                    waited[key] = v
                r = ins["fn"](eng)
                if ins["chan"] is not None:
                    r.then_inc(csem[ins["chan"]], 16)
                elif ins["signal"]:
                    r.then_inc(esem[e], 1)
            if e == "pool":
                for c in final_wait_chans:
                    if ccount.get(c, 0) > 0:
                        eng.wait_ge(csem[c], ccount[c])

        @block.tensor
        def _(eng):
            run("pe", eng)

        @block.scalar
        def _(eng):
            run("act", eng)

        @block.vector
        def _(eng):
            run("dve", eng)

        @block.gpsimd
        def _(eng):
            run("pool", eng)

        @block.sync
        def _(eng):
            run("sp", eng)


class _Stop(Exception):
    pass


def _ck(name):
    return None


def build(jobs):
    NJ = len(jobs)
    nc = bass.Bass("TRN2", target_bir_lowering=False)
    P = Prog()

    def din(name, shape, dtype=F32):
        return nc.dram_tensor(name, shape, dtype, kind="ExternalInput").ap()

    def dscr(name, shape, dtype=BF16):
        return nc.dram_tensor(name, shape, dtype, kind="Internal").ap()

    xj = [din(f"xin{j}", [jobs[j][0] * 512, D]) for j in range(NJ)]
    ropej = [din(f"rope{j}", [2, 128, jobs[j][0] * 512]) for j in range(NJ)]
    yj = [nc.dram_tensor(f"yout{j}", [jobs[j][1] * 512, D], F32, kind="ExternalOutput").ap() for j in range(NJ)]
    cT_d = din("cTin", [128, 8, NJ])
    wada_d = din("wada", [24, 128, 8, 256])
    bcol_d = din("bcolin", [128, 48])
    wshapes = dict(win1=[2, 128, 8, 640], win2=[2, 128, 8, 512], wab=[4, 128, 4, 512], wgab=[4, 128, 8, 512],
                   wo=[2, 128, 8, 512], wgu=[11, 128, 8, 512], wdn=[4, 128, 11, 512])
    wf = {k: din(k, v) for k, v in wshapes.items()}
    wb16 = {k: dscr(k + "_b", v) for k, v in wshapes.items()}
    ident_d = din("identc", [128, 128])
    pt_d = din("ptm", [128, 128])
    mask_d = din("maskc", [128, 2, 128])
    maskn_d = din("masknc", [128, 2, 128])
    sink_d = din("sink", [1, 8])
    lamv_d = din("lamv", [1, 256])
    subw_d = din("subwin", [1, 128])
    subwc_d = din("subwcol", [128, 1])
    lnrow_d = din("lnrow", [4, D])
    lncol_d = din("lncolin", [128, 2, 8])

    NKT = max(j[0] for j in jobs)
    SM = NKT * 512
    NBM = NKT * 4

    with ExitStack() as st:
        def sb(name, shape, dtype):
            return st.enter_context(nc.sbuf_tensor(name, shape, dtype))

        KdT = sb("KdT", [128, 4, SM], BF16)
        Vd = sb("Vd", [128, NBM, 516], BF16)
        KaT = sb("KaT", [128, SM], BF16)
        Va = sb("Va", [128, NBM, 130], BF16)
        NRING = 3
        ring = [sb(f"ring{i}", [128, 5632], BF16) for i in range(NRING)]
        x1 = sb("x1", [128, 4, D], F32)
        xs = [sb(f"xs{i}", [128, D], F32) for i in range(2)]
        cs = sb("cs", [128, 2, 512], F32)
        lnv = [sb(f"lnv{i}", [128, D], F32) for i in range(2)]
        gm = sb("gm", [128, D], BF16)
        gf = sb("gf", [128, D], BF16)
        arA = sb("arA", [128, 11264], BF16)
        arB = sb("arB", [128, 4096], BF16)
        arC = sb("arC", [128, 4096], BF16)
        arD = sb("arD", [128, 4096], BF16)
        ident = sb("ident", [128, 128], F32)
        ones = sb("ones", [128, 128], F32)
        PT = sb("PT", [128, 128], BF16)
        esink = sb("esink", [128, 8], F32)
        lamt = sb("lamt", [128, 256], F32)
        lsm = sb("lsm", [128, 8], F32)
        mhalf = sb("mhalf", [128, 16], F32)
        cT = sb("cT", [128, 8, NJ], F32)
        siluT = sb("siluT", [128, 8, NJ], F32)
        modT = sb("modT", [128, 48, NJ], F32)
        bcol = sb("bcol", [128, 48], F32)
        lncol = sb("lncol", [128, 2, 8], F32)
        cols = sb("cols", [128, NJ, 4, 8], F32)
        zt = sb("zt", [128, 512], BF16)
        ones_bf = sb("ones_bf", [128, 128], BF16)
        ident_bf = sb("ident_bf", [128, 128], BF16)
        maskn = sb("maskn", [128, 2, 128], BF16)
        subwc = sb("subwc", [128, 1], F32)
        stt = sb("stt", [128, 64], F32)
        bst = sb("bst", [128, 2, 6], F32)
        psum = [st.enter_context(nc.psum_tensor(f"ps{i}", [128, 2, 512], F32)) for i in range(4)]

        def bank(b):
            return psum[b // 2][:, b % 2, :]

        def bkey(b):
            return f"ps{b}"

        def pg(ar, name, p0, n, dtype=BF16):
            ap = ar[:, p0 * 512:(p0 + n) * 512]
            if dtype == F32:
                ap = ap.bitcast(F32)
            return ap, [(name, p) for p in range(p0, p0 + n)]

        actT = lambda f: pg(arA, "A", f, 1)
        QdT = lambda c: pg(arA, "A", c, 1)
        QaT_all = arA[:, 4 * 512:8 * 512].rearrange("p (c t) -> p c t", c=4)
        QaT = lambda c: pg(arA, "A", 4 + c, 1)
        Oatmp = pg(arA, "A", 8, 2, F32)
        Odtmp = pg(arA, "A", 10, 2, F32)
        OaT = lambda k: pg(arA, "A", 12 + k, 1)
        OdT = lambda k: pg(arA, "A", 16 + k, 1)
        hT_all = arB[:, :].rearrange("p (k t) -> p k t", k=8)
        hkeys = lambda k, s=None: [("B", k, ss) for ss in (range(4) if s is None else [s])]
        mergedT = lambda oc: pg(arC, "C", oc, 1)
        Ed = lambda i: pg(arC, "C", 2 * i, 2)
        Ew = lambda i: pg(arC, "C", 6 + i, 1)
        tgm = lambda i: pg(arC, "C", 2 * i, 2, F32)
        wm = lambda i: pg(arC, "C", 4 + 2 * i, 2, F32)
        qb_ = pg(arD, "D", 0, 1)
        t1_ = pg(arD, "D", 1, 2, F32)
        t2_ = pg(arD, "D", 3, 2, F32)
        ta_ = pg(arD, "D", 0, 2, F32)
        tb_ = pg(arD, "D", 2, 2, F32)
        m1_ = pg(arD, "D", 4, 2, F32)
        m2_ = pg(arD, "D", 6, 2, F32)
        tmp_ = pg(arD, "D", 0, 4, F32)
        xn_ = pg(arD, "D", 4, 4, F32)

        sched = []
        ring_state = dict(issued=0, used=0)
        released = set()

        def ring_pump():
            while ring_state["issued"] < len(sched):
                n = ring_state["issued"]
                if n >= NRING and (n - NRING) not in released:
                    break
                slot = n % NRING
                wname, wi = sched[n]
                src = wb16[wname][wi].rearrange("p k n -> p (k n)")
                ne = src.shape[1]
                P.op("sp", lambda e, slot=slot, src=src, ne=ne: e.dma_start(out=ring[slot][:, 0:ne], in_=src),
                     r=[("scr", wname, wi)], w=[("ring", slot)], chan=("ring", slot))
                ring_state["issued"] += 1

        def ring_next():
            i = ring_state["used"]
            ring_state["used"] += 1
            ring_pump()
            assert ring_state["issued"] > i, "ring deadlock: too many pieces held"
            return ring[i % NRING], ("ring", i % NRING), i

        def ring_rel(i):
            released.add(i)
            ring_pump()

        def piece(name, i):
            return (name, i)

        for j in range(NJ):
            nkt, nqt = jobs[j]
            sched += [piece("win1", 0), piece("win1", 1)]
            for t in range(nqt):
                sched += [piece("win2", 0), piece("win2", 1)]
                for q in range(4):
                    sched += [piece("wab", q), piece("wgab", q)]
                sched += [piece("wo", 0), piece("wo", 1)]
                sched += [piece("wgu", i) for i in range(11)]
                sched += [piece("wdn", i) for i in range(4)]

        def cast_weights(names, extra_r=()):
            for k in names:
                for i in range(wshapes[k][0]):
                    P.op("pool", lambda e, k=k, i=i: e.dma_start(out=wb16[k][i], in_=wf[k][i]), r=list(extra_r), w=[("scr", k, i)], chan=("cast", k, i))
        cast_weights(("win1", "win2", "wab", "wgab", "wo", "wgu", "wdn"))
        ld = lambda out, in_, keys: P.op("sp", lambda e: e.dma_start(out=out, in_=in_), w=keys, chan="const")
        ld(ident[:], ident_d, ["ident"])
        ld(cT[:], cT_d, ["cT"])
        ld(bcol[:], bcol_d, ["bcol"])
        ld(lncol[:], lncol_d, ["lncol"])
        ld(esink[:], sink_d.partition_broadcast(128), ["esink"])
        ld(lamt[:], lamv_d.partition_broadcast(128), ["lamt"])
        ld(subwc[:], subwc_d, ["subwc"])
        P.op("pool", lambda e: e.dma_start(out=PT[:], in_=pt_d), w=["PT"], chan="constc")
        P.op("pool", lambda e: e.dma_start(out=ident_bf[:], in_=ident_d), w=["ident_bf"], chan="constc")
        P.op("pool", lambda e: e.dma_start(out=maskn[:], in_=maskn_d), w=["maskn"], chan="constc")
        P.op("pool", lambda e: e.memset(ones[:], 1.0), w=["ones"])
        P.op("pool", lambda e: e.memset(zt[:], 0.0), w=["zt"])
        P.op("pool", lambda e: e.memset(ones_bf[:], 1.0), w=["ones_bf"])
        P.op("pool", lambda e: e.memset(mhalf[:], -0.5), w=["mhalf"])
        P.op("pool", lambda e: e.memset(Vd[:], 1.0), w=["Vd"])
        P.op("pool", lambda e: e.memset(Va[:], 1.0), w=["Va"])
        P.op("act", lambda e: e.activation(out=esink[:], in_=esink[:], func=AF.Exp), r=["esink"], w=["esink"])
        P.op("dve", lambda e: e.tensor_tensor(out=lamt[:, 0:64], in0=lamt[:, 0:64], in1=lamt[:, 64:128], op=ALU.mult), r=["lamt"], w=["lamt"])
        P.op("dve", lambda e: e.tensor_tensor(out=lamt[:, 128:192], in0=lamt[:, 128:192], in1=lamt[:, 192:256], op=ALU.mult), r=["lamt"], w=["lamt"])
        P.op("act", lambda e: e.activation(out=lamt[:, 64:128], in_=lamt[:, 0:64], func=AF.Identity, accum_out=lsm[:, 0:1]), r=["lamt"], w=["lamt", "lsm"])
        P.op("act", lambda e: e.activation(out=lamt[:, 192:256], in_=lamt[:, 128:192], func=AF.Identity, accum_out=lsm[:, 1:2]), r=["lamt"], w=["lamt", "lsm"])
        P.op("act", lambda e: e.activation(out=lsm[:, 2:4], in_=lsm[:, 0:2], func=AF.Exp), r=["lsm"], w=["lsm"])
        P.op("dve", lambda e: e.tensor_tensor(out=lsm[:, 4:5], in0=lsm[:, 3:4], in1=lsm[:, 2:3], op=ALU.subtract), r=["lsm"], w=["lsm"])
        P.op("dve", lambda e: e.tensor_scalar_add(out=lsm[:, 5:6], in0=lsm[:, 4:5], scalar1=-LAM_INIT), r=["lsm"], w=["lsm"])
        neglam = lsm[:, 5:6]
        P.op("dve", lambda e: e.tensor_scalar_mul(out=subwc[:], in0=subwc[:], scalar1=1.0 - LAM_INIT), r=["subwc"], w=["subwc"])
        P.op("act", lambda e: e.activation(out=siluT[:], in_=cT[:], func=AF.Tanh, scale=0.5), r=["cT"], w=["siluT"])
        P.op("dve", lambda e: e.scalar_tensor_tensor(out=siluT[:], in0=siluT[:], scalar=1.0, in1=cT[:], op0=ALU.add, op1=ALU.mult),
             r=["siluT", "cT"], w=["siluT"])
        P.op("dve", lambda e: e.tensor_scalar_mul(out=siluT[:], in0=siluT[:], scalar1=0.5), r=["siluT"], w=["siluT"])
        wstage = [arA[:, i * 4096:(i + 1) * 4096].bitcast(F32).rearrange("p (k n) -> p k n", k=8) for i in range(2)]
        wstk = [[("A", p) for p in range(8 * i, 8 * i + 8)] for i in range(2)]
        mps = psum[0][:, 0, 0:48 * NJ].rearrange("p (c j) -> p c j", j=NJ)
        for pc in range(24):
            i = pc % 2
            P.op("sp", lambda e, pc=pc, i=i: e.dma_start(out=wstage[i], in_=wada_d[pc]), w=wstk[i], chan=("wst", i))
            for cc in range(2):
                ch = pc * 2 + cc
                for kk in range(8):
                    P.op("pe", lambda e, i=i, cc=cc, kk=kk, ch=ch: e.matmul(mps[:, ch, :], lhsT=wstage[i][:, kk, cc * 128:(cc + 1) * 128],
                                                                          rhs=siluT[:, kk, :], start=(kk == 0), stop=(kk == 7)),
                         r=wstk[i] + ["siluT"], w=[bkey(0)])
        for j in range(NJ):
            P.op("dve", lambda e, j=j: e.tensor_tensor(out=modT[:, :, j], in0=mps[:, :, j], in1=bcol[:], op=ALU.add),
                 r=[bkey(0), "bcol"], w=["modT"])
        for j in range(NJ):
            P.op("dve", lambda e, j=j: e.tensor_scalar_add(out=cols[:, j, 0, :], in0=modT[:, 8:16, j], scalar1=1.0), r=["modT"], w=["cols"])
            P.op("dve", lambda e, j=j: e.tensor_copy(out=cols[:, j, 1, :], in_=modT[:, 0:8, j]), r=["modT"], w=["cols"])
            P.op("dve", lambda e, j=j: e.scalar_tensor_tensor(out=cols[:, j, 2, :], in0=modT[:, 32:40, j], scalar=1.0, in1=lncol[:, 0, :],
                                                              op0=ALU.add, op1=ALU.mult), r=["modT", "lncol"], w=["cols"])
            P.op("dve", lambda e, j=j: e.scalar_tensor_tensor(out=cols[:, j, 3, :], in0=modT[:, 32:40, j], scalar=1.0, in1=lncol[:, 1, :],
                                                              op0=ALU.add, op1=ALU.mult), r=["modT", "lncol"], w=["cols"])
            P.op("dve", lambda e, j=j: e.tensor_tensor(out=cols[:, j, 3, :], in0=cols[:, j, 3, :], in1=modT[:, 24:32, j], op=ALU.add),
                 r=["modT", "cols"], w=["cols"])

        rr = dict(ps=0)
        rope_rr = dict(n=0)
        acc_rr = dict(n=0)

        def next_bank(lo=0, hi=8):
            b = lo + rr["ps"] % (hi - lo)
            rr["ps"] += 1
            return b

        alt = dict(n=0)

        def evac_eng():
            alt["n"] += 1
            return "act" if alt["n"] % 2 else "dve"

        def load_x_sub(j, row0, slot):
            P.op("sp", lambda e: e.dma_start(out=xs[slot][:], in_=xj[j][row0:row0 + 128, :]), w=[("xs", slot)], chan=("xs", slot))

        xs_rr = dict(n=0)
        xpre = {}

        def prefetch_x(j, t):
            for s in range(2):
                slot = xs_rr["n"] % 2
                xs_rr["n"] += 1
                load_x_sub(j, t * 512 + s * 128, slot)
                xpre[(j, t, s)] = slot

        def make_hT(j, t):
            for s in range(4):
                if (j, t, s) in xpre:
                    slot = xpre.pop((j, t, s))
                else:
                    slot = xs_rr["n"] % 2
                    xs_rr["n"] += 1
                    load_x_sub(j, t * 512 + s * 128, slot)
                _ck("p1_x")
                for kg in range(2):
                    b = next_bank()
                    for kk in range(4):
                        k = kg * 4 + kk
                        if kk == 1:
                            _ck("p1_tr")
                        P.op("pe", lambda e, b=b, kk=kk, k=k, slot=slot: e.transpose(bank(b)[:, kk * 128:(kk + 1) * 128],
                                                                                 xs[slot][:, k * 128:(k + 1) * 128], ident[:]),
                             r=[("xs", slot), "ident"], w=[bkey(b)])
                    for kk in range(4):
                        k = kg * 4 + kk
                        if kk == 1:
                            _ck("p1_ev1")
                        if kk == 2:
                            _ck("p1_ev2")
                        eng = "act"
                        out = hT_all[:, k, s * 128:(s + 1) * 128]
                        src = bank(b)[:, kk * 128:(kk + 1) * 128]
                        if eng == "act":
                            P.op("act", lambda e, out=out, src=src, k=k: e.activation(out=out, in_=src, func=AF.Identity,
                                                                                   bias=cols[:, j, 1, k:k + 1], scale=cols[:, j, 0, k:k + 1]),
                                 r=[bkey(b), "cols"], w=hkeys(k, s))
                        else:
                            P.op("dve", lambda e, out=out, src=src, k=k: e.tensor_scalar(out=out, in0=src, scalar1=cols[:, j, 0, k:k + 1],
                                                                                      scalar2=cols[:, j, 1, k:k + 1], op0=ALU.mult, op1=ALU.add),
                                 r=[bkey(b), "cols"], w=hkeys(k, s))

        def load_cs(j, t):
            P.op("sp", lambda e: e.dma_start(out=cs[:], in_=ropej[j][:, :, t * 512:(t + 1) * 512].rearrange("c p t -> p c t")),
                 w=["cs"], chan="cs")

        def mm_fm(b, wv, col0, rhs_fn, nk, rkeys):
            for k in range(nk):
                P.op("pe", lambda e, k=k: e.matmul(bank(b), lhsT=wv[:, k, col0:col0 + 128], rhs=rhs_fn(k), start=(k == 0), stop=(k == nk - 1)),
                     r=rkeys(k), w=[bkey(b)])

        rope_pend = []

        def rope_chunk(b, out_ap, out_keys):
            rope_rr["n"] += 1
            odd = rope_rr["n"] % 2
            if odd:
                qb, qbk = qb_
                t1, t1k = t1_
                t2, t2k = t2_
            else:
                qb, qbk = pg(arC, "C", 0, 1)
                t1, t1k = pg(arC, "C", 1, 2, F32)
                t2, t2k = pg(arC, "C", 3, 2, F32)
            P.op("act", lambda e: e.activation(out=qb, in_=bank(b), func=AF.Copy), r=[bkey(b)], w=qbk)

            def tail():
                b2 = next_bank()
                P.op("pe", lambda e: e.matmul(bank(b2), lhsT=PT[:], rhs=qb, start=True, stop=True), r=qbk + ["PT"], w=[bkey(b2)])
                P.op("dve", lambda e: e.tensor_tensor(out=t1, in0=bank(b), in1=cs[:, 0, :], op=ALU.mult), r=[bkey(b), "cs"] + qbk, w=t1k)
                P.op("dve", lambda e: e.tensor_tensor(out=t2, in0=bank(b2), in1=cs[:, 1, :], op=ALU.mult), r=[bkey(b2), "cs"], w=t2k)
                P.op("pool" if odd else "dve", lambda e: e.tensor_tensor(out=out_ap, in0=t1, in1=t2, op=ALU.add), r=t1k + t2k, w=out_keys)
            rope_flush()
            rope_pend.append(tail)

        def rope_flush():
            while rope_pend:
                rope_pend.pop(0)()

        def job_body(j):
            nkt, nqt = jobs[j]
            NB = nkt * 4
            _ck("prologue")
            for (v0, gt, gkey) in ((16, gm, "gm"), (40, gf, "gf")):
                for half in range(2):
                    b = next_bank()
                    for kk in range(4):
                        k = half * 4 + kk
                        dg = xs[kk % 2][:, 0:128]
                        P.op("dve", lambda e, dg=dg, k=k, v0=v0: e.tensor_scalar_mul(out=dg, in0=ident[:], scalar1=modT[:, v0 + k, j:j + 1]),
                             r=["ident", "modT"], w=[("xs", kk % 2)])
                        P.op("pe", lambda e, dg=dg, b=b, kk=kk: e.matmul(bank(b)[:, kk * 128:(kk + 1) * 128], lhsT=ones[:], rhs=dg, start=True, stop=True),
                             r=[("xs", kk % 2), "ones"], w=[bkey(b)])
                    P.op("act", lambda e, b=b, gt=gt, half=half: e.activation(out=gt[:, half * 512:(half + 1) * 512], in_=bank(b), func=AF.Identity, scale=0.5),
                         r=[bkey(b)], w=[gkey])

            _ck("gates")
            w1a, w1ak, w1ai = ring_next()
            w1b, w1bk, w1bi = ring_next()
            w1av = w1a[:, 0:5120].rearrange("p (k n) -> p k n", k=8)
            w1bv = w1b[:, 0:5120].rearrange("p (k n) -> p k n", k=8)
            _ck("p1_ring")
            for t in range(nkt):
                load_cs(j, t)
                _ck("p1_cs")
                make_hT(j, t)
                _ck("p1_hT")
                for c in range(5):
                    b = next_bank()
                    mm_fm(b, w1av, c * 128, lambda k: hT_all[:, k, :], 8, lambda k: [w1ak] + hkeys(k))
                    if c < 4:
                        rope_chunk(b, KdT[:, c, t * 512:(t + 1) * 512], [("Kd", c, t)])
                    else:
                        rope_chunk(b, KaT[:, t * 512:(t + 1) * 512], [("Ka", t)])
                rope_flush()
                _ck("p1_k")
                for s in range(4):
                    blk = t * 4 + s
                    b = next_bank()
                    for k in range(8):
                        P.op("pe", lambda e, b=b, k=k, s=s: e.matmul(bank(b), lhsT=hT_all[:, k, s * 128:(s + 1) * 128], rhs=w1bv[:, k, 0:512],
                                                                   start=(k == 0), stop=(k == 7)), r=[w1bk] + hkeys(k, s), w=[bkey(b)])
                    eng = evac_eng()
                    outv = Vd[:, blk, :].rearrange("p (h e) -> p h e", e=129)[:, :, 0:128]
                    srcv = bank(b).rearrange("p (h e) -> p h e", e=128)
                    if eng == "act":
                        P.op("act", lambda e, outv=outv, srcv=srcv: e.activation(out=outv, in_=srcv, func=AF.Copy), r=[bkey(b), "Vd"], w=[("Vd", blk)])
                    else:
                        P.op("dve", lambda e, outv=outv, srcv=srcv: e.tensor_copy(out=outv, in_=srcv), r=[bkey(b), "Vd"], w=[("Vd", blk)])
                    b2 = next_bank()
                    for k in range(8):
                        P.op("pe", lambda e, b2=b2, k=k, s=s: e.matmul(bank(b2)[:, 0:128], lhsT=hT_all[:, k, s * 128:(s + 1) * 128], rhs=w1bv[:, k, 512:640],
                                                                     start=(k == 0), stop=(k == 7)), r=[w1bk] + hkeys(k, s), w=[bkey(b2)])
                    outv2 = Va[:, blk, :].rearrange("p (h e) -> p h e", e=65)[:, :, 0:64]
                    srcv2 = bank(b2)[:, 0:128].rearrange("p (h e) -> p h e", e=64)
                    P.op("dve", lambda e, outv2=outv2, srcv2=srcv2: e.tensor_copy(out=outv2, in_=srcv2), r=[bkey(b2), "Va"], w=[("Va", blk)])

            ring_rel(w1ai)
            ring_rel(w1bi)
            _ck("phase1")
            for t in range(nqt):
                load_cs(j, t)
                make_hT(j, t)
                for pi, qdst in ((0, QdT), (1, QaT)):
                    wt, wk, wi_ = ring_next()
                    wv = wt[:, 0:4096].rearrange("p (k n) -> p k n", k=8)
                    for c in range(4):
                        b = next_bank()
                        mm_fm(b, wv, c * 128, lambda k: hT_all[:, k, :], 8, lambda k: [wk] + hkeys(k))
                        oap, okeys = qdst(c)
                        rope_chunk(b, oap, okeys)
                    ring_rel(wi_)
                rope_flush()
                _ck("s2")
                oaT_all = arA[:, 12 * 512:16 * 512].rearrange("p (k t) -> p k t", k=4)
                steps = []
                for s in range(4):
                    n = t * 4 + s
                    for g in range(2):
                        js = [jb for jb in (n - 1, n, n + 1) if 0 <= jb < NB]
                        for ji, jb in enumerate(js):
                            steps.append((s, n, g, ji, jb, len(js)))
                w3 = dict(ab=None, accv=None)

                def w_front(i, st_):
                    s, n, g, ji, jb, nj = st_
                    sbk = next_bank(0, 4)
                    P.op("pe", lambda e: e.matmul(
                        bank(sbk).rearrange("p (c q) -> p c q", c=4), lhsT=KaT[g * 64:(g + 1) * 64, jb * 128:(jb + 1) * 128],
                        rhs=QaT_all[g * 64:(g + 1) * 64, :, s * 128:(s + 1) * 128], start=True, stop=(jb == n)),
                        r=[("Ka", jb // 4)] + [("A", 4 + c) for c in range(4)], w=[bkey(sbk)])
                    if jb != n:
                        mi = 0 if jb < n else 1
                        P.op("pe", lambda e: e.matmul(
                            bank(sbk).rearrange("p (c q) -> p c q", c=4), lhsT=ident_bf[:],
                            rhs=maskn[:, mi:mi + 1, :].to_broadcast([128, 4, 128]), start=False, stop=True),
                            r=["ident_bf", "maskn"], w=[bkey(sbk)])
                    ew, ewk = Ew(i % 2)
                    P.op("act", lambda e: e.activation(out=ew, in_=bank(sbk), func=AF.Exp, scale=0.125), r=[bkey(sbk)], w=ewk)
                    return ew, ewk

                oat_bufs = [Oatmp, pg(arD, "D", 0, 2, F32)]

                def w_back(i, st_, ew, ewk):
                    s, n, g, ji, jb, nj = st_
                    oat, oatk = oat_bufs[s % 2]
                    if ji == 0:
                        ab = next_bank(4, 8)
                        w3["ab"] = ab
                        w3["accv"] = bank(ab)[:, 0:260].rearrange("p (h e) -> p h e", e=65)
                        P.op("pe", lambda e: e.matmul(bank(ab)[:, 0:260], lhsT=zt[:, 0:128], rhs=zt[:, 0:260], start=True, stop=False),
                             r=["zt"], w=[bkey(ab)])
                    ab = w3["ab"]
                    accv = w3["accv"]
                    for hh in range(4):
                        P.op("pe", lambda e, hh=hh: e.matmul(accv[:, hh, :], lhsT=ew[:, hh * 128:(hh + 1) * 128], rhs=Va[:, jb, g * 65:(g + 1) * 65],
                                                              start=False, stop=(ji == nj - 1)), r=ewk + [("Va", jb)], w=[bkey(ab)])
                    if ji == nj - 1:
                        d0 = 40 + 8 * (g % 2)
                        P.op("dve", lambda e: e.tensor_tensor(out=stt[:, d0:d0 + 4], in0=accv[:, :, 64], in1=esink[:, 4 * g:4 * g + 4], op=ALU.add),
                             r=[bkey(ab), "esink"], w=[("stt_a", g)])
                        P.op("dve", lambda e: e.reciprocal(out=stt[:, d0 + 4:d0 + 8], in_=stt[:, d0:d0 + 4]), r=[("stt_a", g)], w=[("stt_b", g)])
                        P.op("dve", lambda e: e.tensor_tensor(
                            out=oat.rearrange("p (h e) -> p h e", e=128)[:, :, g * 64:(g + 1) * 64], in0=accv[:, :, 0:64],
                            in1=stt[:, d0 + 4:d0 + 8].unsqueeze(2).to_broadcast([128, 4, 64]), op=ALU.mult),
                            r=[bkey(ab), ("stt_b", g)], w=oatk)

                def w_tr(s):
                    oat, oatk = oat_bufs[s % 2]
                    b = next_bank(0, 4)
                    for k in range(4):
                        P.op("pe", lambda e, k=k: e.transpose(bank(b)[:, k * 128:(k + 1) * 128], oat[:, k * 128:(k + 1) * 128], ident[:]),
                             r=oatk + ["ident"], w=[bkey(b)])
                    P.op("dve", lambda e: e.tensor_copy(out=oaT_all[:, :, s * 128:(s + 1) * 128], in_=bank(b).rearrange("p (k q) -> p k q", k=4)),
                         r=[bkey(b)], w=[("A", 12 + k) for k in range(4)])

                prev = None
                tr_pend = None
                for i in range(len(steps) + 1):
                    cur = None
                    if i < len(steps):
                        ew, ewk = w_front(i, steps[i])
                        cur = (i, steps[i], ew, ewk)
                    if tr_pend is not None:
                        w_tr(tr_pend)
                        tr_pend = None
                    if prev is not None:
                        w_back(*prev)
                        ps_, pst = prev[0], prev[1]
                        if pst[2] == 1 and pst[3] == pst[5] - 1:
                            tr_pend = pst[0]
                    prev = cur
                if tr_pend is not None:
                    w_tr(tr_pend)
                _ck("s3")
                odt, odtk = Odtmp
                rcb, rck = pg(arD, "D", 0, 2, F32)
                tt_, ttk = pg(arD, "D", 2, 2, F32)
                sqb, sqk = pg(arD, "D", 4, 2, F32)
                veb, vek = pg(arD, "D", 6, 2, F32)
                npair = NB // 2
                items = [(h, c, jp) for h in range(4) for c in range(2) for jp in range(npair)]
                grp = {}

                def d_group(h, c):
                    if (h, c) not in grp:
                        pa = 2 + (acc_rr["n"] % 2)
                        acc_rr["n"] += 1
                        grp[(h, c)] = (psum[pa][:, 0, :], psum[pa][:, 1, :], bkey(pa * 2), bkey(pa * 2 + 1))
                    return grp[(h, c)]

                def d_front(idx):
                    h, c, jp = items[idx]
                    sp_ = idx % 2
                    for jj in range(2):
                        jb = 2 * jp + jj
                        P.op("pe", lambda e, jj=jj, jb=jb: e.matmul(
                            psum[sp_][:, jj, :], lhsT=KdT[c * 64:(c + 1) * 64, h, jb * 128:(jb + 1) * 128],
                            rhs=arA[c * 64:(c + 1) * 64, h * 512:(h + 1) * 512], start=True, stop=True),
                            r=[("Kd", h, jb // 4), ("A", h)], w=[bkey(sp_ * 2 + jj)])
                    ed, edk = Ed(idx % 3)
                    P.op("act", lambda e: e.activation(out=ed.rearrange("p (a q) -> p a q", a=2), in_=psum[sp_][:, :, :], func=AF.Exp, scale=0.125),
                         r=[bkey(sp_ * 2), bkey(sp_ * 2 + 1)], w=edk)
                    return ed, edk

                def d_rms(h):
                    P.op("act", lambda e: e.activation(out=sqb, in_=odt, func=AF.Square), r=odtk, w=sqk)
                    b = next_bank(0, 4)
                    P.op("pe", lambda e: e.matmul(bank(b), lhsT=ones[:], rhs=sqb, start=True, stop=True), r=sqk + ["ones"], w=[bkey(b)])
                    P.op("dve", lambda e: e.tensor_scalar(out=veb, in0=bank(b), scalar1=1.0 / 128.0, scalar2=EPS, op0=ALU.mult, op1=ALU.add),
                         r=[bkey(b)], w=vek)
                    I32 = mybir.dt.int32
                    yb, ybk = rcb, rck
                    P.op("dve", lambda e: e.tensor_single_scalar(out=yb.bitcast(I32), in_=veb.bitcast(I32), scalar=1, op=ALU.arith_shift_right), r=vek, w=ybk)
                    P.op("dve", lambda e: e.tensor_scalar(out=yb.bitcast(I32), in0=yb.bitcast(I32), scalar1=-1, scalar2=0x5f3759df, op0=ALU.mult, op1=ALU.add),
                         r=ybk, w=ybk)
                    for it in range(2):
                        P.op("dve", lambda e: e.tensor_tensor(out=tt_, in0=veb, in1=yb, op=ALU.mult), r=vek + ybk, w=ttk)
                        P.op("dve", lambda e: e.tensor_tensor(out=tt_, in0=tt_, in1=yb, op=ALU.mult), r=ttk + ybk, w=ttk)
                        P.op("dve", lambda e: e.tensor_scalar(out=tt_, in0=tt_, scalar1=-0.5, scalar2=1.5, op0=ALU.mult, op1=ALU.add), r=ttk, w=ttk)
                        P.op("dve", lambda e: e.tensor_tensor(out=yb, in0=yb, in1=tt_, op=ALU.mult), r=ttk + ybk, w=ybk)
                    P.op("dve", lambda e: e.tensor_tensor(out=sqb, in0=odt, in1=yb, op=ALU.mult), r=odtk + ybk, w=sqk)
                    oap, oak = OdT(h)
                    P.op("dve", lambda e: e.tensor_scalar_mul(out=oap, in0=sqb, scalar1=subwc[:, 0:1]), r=sqk + ["subwc"], w=oak)

                esum_buf = [pg(arC, "C", 6, 1), pg(arC, "C", 7, 1)]
                sum_pend = []

                def d_sum_flush():
                    while sum_pend:
                        es, esk, accS, kS, first, last = sum_pend.pop(0)
                        P.op("pe", lambda e: e.matmul(accS, lhsT=ones_bf[:], rhs=es, start=first, stop=last), r=esk + ["ones_bf"], w=[kS])

                def d_back(idx, ed, edk):
                    h, c, jp = items[idx]
                    accO, accS, kO, kS = d_group(h, c)
                    es, esk = esum_buf[idx % 2]
                    d_sum_flush()
                    P.op("pool", lambda e: e.tensor_tensor(out=es, in0=ed[:, 0:512], in1=ed[:, 512:1024], op=ALU.add), r=edk, w=esk)
                    for jj in range(2):
                        jb = 2 * jp + jj
                        P.op("pe", lambda e, jj=jj, jb=jb: e.matmul(accO, lhsT=Vd[:, jb, h * 129:h * 129 + 128], rhs=ed[:, jj * 512:(jj + 1) * 512],
                                                                  start=(jb == 0), stop=(jb == NB - 1)), r=edk + [("Vd", jb)], w=[kO])
                    sum_pend.append((es, esk, accS, kS, jp == 0, jp == npair - 1))
                    if jp == npair - 1:
                        d_sum_flush()
                    if jp == npair - 1:
                        P.op("dve", lambda e: e.reciprocal(out=rcb, in_=accS), r=[kS], w=rck)
                        if c == 0:
                            P.op("dve", lambda e: e.tensor_tensor(out=odt, in0=accO, in1=rcb, op=ALU.mult), r=[kO] + rck, w=odtk)
                        else:
                            P.op("dve", lambda e: e.tensor_tensor(out=tt_, in0=accO, in1=rcb, op=ALU.mult), r=[kO] + rck, w=ttk)
                            P.op("dve", lambda e: e.scalar_tensor_tensor(out=odt, in0=tt_, scalar=neglam, in1=odt, op0=ALU.mult, op1=ALU.add),
                                 r=ttk + odtk + ["lsm"], w=odtk)
                            return h
                    return None

                dpend = None
                rms_q = []
                for idx in range(len(items) + 1):
                    cur = None
                    if idx < len(items):
                        ed, edk = d_front(idx)
                        cur = (idx, ed, edk)
                    while rms_q and rms_q[0][0] <= idx:
                        d_rms(rms_q.pop(0)[1])
                    if dpend is not None:
                        hdone = d_back(*dpend)
                        if hdone is not None:
                            rms_q.append((idx + 3, hdone))
                    dpend = cur
                while rms_q:
                    d_rms(rms_q.pop(0)[1])
                _ck("s4")
                for qtr in range(4):
                    wabt, wabk, wabi = ring_next()
                    gabt, gabk, gabi = ring_next()
                    wabv = wabt[:, 0:2048].rearrange("p (k n) -> p k n", k=4)
                    gabv = gabt[:, 0:4096].rearrange("p (k n) -> p k n", k=8)
                    for cc in range(2):
                        oc = qtr * 2 + cc
                        bA = next_bank()
                        mm_fm(bA, wabv, cc * 128, lambda k: OaT(k)[0], 4, lambda k: [wabk] + OaT(k)[1])
                        bB = next_bank()
                        mm_fm(bB, wabv, 256 + cc * 128, lambda k: OdT(k)[0], 4, lambda k: [wabk] + OdT(k)[1])
                        bGa = next_bank()
                        mm_fm(bGa, gabv, cc * 128, lambda k: hT_all[:, k, :], 8, lambda k: [gabk] + hkeys(k))
                        bGb = next_bank()
                        mm_fm(bGb, gabv, 256 + cc * 128, lambda k: hT_all[:, k, :], 8, lambda k: [gabk] + hkeys(k))
                        ta, tak = ta_
                        tb, tbk = tb_
                        m1, m1k = m1_
                        m2, m2k = m2_
                        P.op("act", lambda e, bGa=bGa: e.activation(out=ta, in_=bank(bGa), func=AF.Tanh, scale=0.5), r=[bkey(bGa)], w=tak)
                        P.op("act", lambda e, bGb=bGb: e.activation(out=tb, in_=bank(bGb), func=AF.Tanh, scale=0.5), r=[bkey(bGb)], w=tbk)
                        P.op("dve", lambda e, bA=bA: e.scalar_tensor_tensor(out=m1, in0=ta, scalar=1.0, in1=bank(bA), op0=ALU.add, op1=ALU.mult),
                             r=tak + [bkey(bA)], w=m1k)
                        P.op("dve", lambda e, bB=bB: e.scalar_tensor_tensor(out=m2, in0=tb, scalar=1.0, in1=bank(bB), op0=ALU.add, op1=ALU.mult),
                             r=tbk + [bkey(bB)], w=m2k)
                        mo, mok = mergedT(oc)
                        P.op("pool", lambda e, mo=mo: e.tensor_tensor(out=mo, in0=m1, in1=m2, op=ALU.add), r=m1k + m2k, w=mok)
                    ring_rel(wabi)
                    ring_rel(gabi)
                _ck("s6")
                wo0, wo0k, wo0i = ring_next()
                wo1, wo1k, wo1i = ring_next()
                wov = [wo0[:, 0:4096].rearrange("p (k n) -> p k n", k=8), wo1[:, 0:4096].rearrange("p (k n) -> p k n", k=8)]
                wok = [wo0k, wo1k]
                P.op("sp", lambda e: e.dma_start(out=lnv[0][:], in_=lnrow_d[0:1, :].partition_broadcast(128)), w=[("lnv", 0)], chan=("lnv", 0))
                P.op("sp", lambda e: e.dma_start(out=lnv[1][:], in_=lnrow_d[1:2, :].partition_broadcast(128)), w=[("lnv", 1)], chan=("lnv", 1))
                tmps = [pg(arD, "D", 0, 4, F32), pg(arD, "D", 4, 4, F32)]
                xns = [pg(arA, "A", 0, 4, F32), pg(arA, "A", 4, 4, F32)]

                def s7_front(s):
                    tmp, tmpk = tmps[s % 2]
                    xn, xnk = xns[s % 2]
                    slot = xs_rr["n"] % 2
                    xs_rr["n"] += 1
                    load_x_sub(j, t * 512 + s * 128, slot)
                    for half in range(2):
                        b = next_bank()
                        for k in range(8):
                            P.op("pe", lambda e, b=b, k=k, s=s, half=half: e.matmul(bank(b), lhsT=arC[:, k * 512 + s * 128:k * 512 + (s + 1) * 128],
                                                                                  rhs=wov[half][:, k, :], start=(k == 0), stop=(k == 7)),
                                 r=[wok[half], ("C", k)], w=[bkey(b)])
                        P.op("dve", lambda e, b=b, half=half, tmp=tmp: e.tensor_tensor(out=tmp[:, half * 512:(half + 1) * 512], in0=bank(b),
                                                                            in1=gm[:, half * 512:(half + 1) * 512], op=ALU.mult),
                             r=[bkey(b), "gm"], w=tmpk)
                    P.op("dve", lambda e, s=s, slot=slot, tmp=tmp: e.scalar_tensor_tensor(out=x1[:, s, :], in0=xs[slot][:], scalar=ALPHA, in1=tmp,
                                                                                  op0=ALU.mult, op1=ALU.add), r=[("xs", slot)] + tmpk, w=[("x1", s)])
                    for hh in range(2):
                        P.op("dve", lambda e, s=s, hh=hh: e.bn_stats(out=bst[:, hh, :], in_=x1[:, s, hh * 512:(hh + 1) * 512]), r=[("x1", s)], w=["bst"])
                    P.op("dve", lambda e: e.bn_aggr(out=stt[:, 32:34], in_=bst[:]), r=["bst"], w=["stt_mv"])
                    P.op("pool", lambda e: e.tensor_scalar_add(out=stt[:, 34:35], in0=stt[:, 33:34], scalar1=EPS), r=["stt_mv"], w=["stt_ve2"])
                    P.op("pool", lambda e: e.tensor_tensor(out=stt[:, 35:36], in0=stt[:, 34:35], in1=mhalf[:, 0:1], op=ALU.pow), r=["stt_ve2", "mhalf"], w=["stt_rs2"])
                    P.op("dve", lambda e, s=s, xn=xn: e.tensor_scalar(out=xn, in0=x1[:, s, :], scalar1=stt[:, 32:33], scalar2=stt[:, 35:36],
                                                               op0=ALU.subtract, op1=ALU.mult), r=[("x1", s), "stt_mv", "stt_rs2"], w=xnk)
                    P.op("pool", lambda e, s=s, xn=xn: e.tensor_tensor(out=x1[:, s, :], in0=xn, in1=lnv[0][:], op=ALU.mult), r=xnk + [("lnv", 0)], w=[("x1", s)])
                    P.op("pool", lambda e, s=s: e.tensor_tensor(out=x1[:, s, :], in0=x1[:, s, :], in1=lnv[1][:], op=ALU.add), r=[("x1", s), ("lnv", 1)], w=[("x1", s)])

                def s7_back(s):
                    xn, xnk = xns[s % 2]
                    for kg in range(2):
                        b = next_bank()
                        for kk in range(4):
                            k = kg * 4 + kk
                            P.op("pe", lambda e, b=b, kk=kk, k=k, xn=xn: e.transpose(bank(b)[:, kk * 128:(kk + 1) * 128], xn[:, k * 128:(k + 1) * 128], ident[:]),
                                 r=xnk + ["ident"], w=[bkey(b)])
                        for kk in range(4):
                            k = kg * 4 + kk
                            out = hT_all[:, k, s * 128:(s + 1) * 128]
                            src = bank(b)[:, kk * 128:(kk + 1) * 128]
                            P.op("act", lambda e, out=out, src=src, k=k: e.activation(out=out, in_=src, func=AF.Identity,
                                                                                   bias=cols[:, j, 3, k:k + 1], scale=cols[:, j, 2, k:k + 1]),
                                 r=[bkey(b), "cols"], w=hkeys(k, s))

                for s in range(5):
                    if s < 4:
                        s7_front(s)
                    if s >= 1:
                        s7_back(s - 1)
                ring_rel(wo0i)
                ring_rel(wo1i)
                _ck("s7")
                for pi in range(11):
                    wt, wk, wi_ = ring_next()
                    wv = wt[:, 0:4096].rearrange("p (k n) -> p k n", k=8)
                    for pp in range(2):
                        fi = 2 * pi + pp
                        bG = next_bank()
                        mm_fm(bG, wv, (2 * pp) * 128, lambda k: hT_all[:, k, :], 8, lambda k: [wk] + hkeys(k))
                        bU = next_bank()
                        mm_fm(bU, wv, (2 * pp + 1) * 128, lambda k: hT_all[:, k, :], 8, lambda k: [wk] + hkeys(k))
                        tg, tgk = tgm(fi % 2)
                        ww, wwk = wm(fi % 2)
                        P.op("act", lambda e, bG=bG, tg=tg: e.activation(out=tg, in_=bank(bG), func=AF.Tanh, scale=0.5), r=[bkey(bG)], w=tgk)
                        P.op("dve", lambda e, bG=bG, tg=tg, ww=ww: e.scalar_tensor_tensor(out=ww, in0=tg, scalar=1.0, in1=bank(bG), op0=ALU.add, op1=ALU.mult),
                             r=tgk + [bkey(bG)], w=wwk)
                        ao, aok = actT(fi)
                        P.op("dve", lambda e, bU=bU, ww=ww, ao=ao: e.tensor_tensor(out=ao, in0=ww, in1=bank(bU), op=ALU.mult), r=wwk + [bkey(bU)], w=aok)
                    ring_rel(wi_)
                _ck("s8")
                if t + 1 < nqt:
                    prefetch_x(j, t + 1)
                P.op("sp", lambda e: e.dma_start(out=lnv[0][:], in_=lnrow_d[2:3, :].partition_broadcast(128)), w=[("lnv", 0)], chan=("lnv", 0))
                P.op("sp", lambda e: e.dma_start(out=lnv[1][:], in_=lnrow_d[3:4, :].partition_broadcast(128)), w=[("lnv", 1)], chan=("lnv", 1))
                for half in range(2):
                    bs = [4 * half + s for s in range(4)]
                    for kp in range(2):
                        wt, wk, wi_ = ring_next()
                        wv = wt[:, 0:5632].rearrange("p (k n) -> p k n", k=11)
                        for s in range(4):
                            for kk in range(11):
                                f = kp * 11 + kk
                                P.op("pe", lambda e, s=s, kk=kk, f=f, wv=wv, bs=bs: e.matmul(bank(bs[s]), lhsT=arA[:, f * 512 + s * 128:f * 512 + (s + 1) * 128],
                                                                                      rhs=wv[:, kk, :], start=(f == 0), stop=(f == NFF - 1)),
                                     r=[wk, ("A", f)], w=[bkey(bs[s])])
                        ring_rel(wi_)
                    for s in range(4):
                        m1, m1k = m1_
                        P.op("dve", lambda e, s=s, half=half, bs=bs: e.tensor_tensor(out=m1, in0=bank(bs[s]), in1=gf[:, half * 512:(half + 1) * 512], op=ALU.mult),
                             r=[bkey(bs[s]), "gf"], w=m1k)
                        P.op("dve", lambda e, s=s, half=half: e.scalar_tensor_tensor(out=x1[:, s, half * 512:(half + 1) * 512], in0=x1[:, s, half * 512:(half + 1) * 512],
                                                                                      scalar=ALPHA, in1=m1, op0=ALU.mult, op1=ALU.add), r=[("x1", s)] + m1k, w=[("x1", s)])
                for s in range(4):
                    for hh in range(2):
                        P.op("dve", lambda e, s=s, hh=hh: e.bn_stats(out=bst[:, hh, :], in_=x1[:, s, hh * 512:(hh + 1) * 512]), r=[("x1", s)], w=["bst"])
                    P.op("dve", lambda e: e.bn_aggr(out=stt[:, 32:34], in_=bst[:]), r=["bst"], w=["stt_mv"])
                    P.op("pool", lambda e: e.tensor_scalar_add(out=stt[:, 34:35], in0=stt[:, 33:34], scalar1=EPS), r=["stt_mv"], w=["stt_ve2"])
                    P.op("pool", lambda e: e.tensor_tensor(out=stt[:, 35:36], in0=stt[:, 34:35], in1=mhalf[:, 0:1], op=ALU.pow), r=["stt_ve2", "mhalf"], w=["stt_rs2"])
                    P.op("dve", lambda e, s=s: e.tensor_scalar(out=x1[:, s, :], in0=x1[:, s, :], scalar1=stt[:, 32:33], scalar2=stt[:, 35:36],
                                                               op0=ALU.subtract, op1=ALU.mult), r=[("x1", s), "stt_mv", "stt_rs2"], w=[("x1", s)])
                    P.op("pool", lambda e, s=s: e.tensor_tensor(out=x1[:, s, :], in0=x1[:, s, :], in1=lnv[0][:], op=ALU.mult), r=[("x1", s), ("lnv", 0)], w=[("x1", s)])
                    P.op("pool", lambda e, s=s: e.tensor_tensor(out=x1[:, s, :], in0=x1[:, s, :], in1=lnv[1][:], op=ALU.add), r=[("x1", s), ("lnv", 1)], w=[("x1", s)])
                    P.op("pool", lambda e, s=s: e.dma_start(out=yj[j][t * 512 + s * 128:t * 512 + (s + 1) * 128, :], in_=x1[:, s, :]),
                         r=[("x1", s)], chan=("st", s))
        try:
            for j in range(NJ):
                job_body(j)
        except _Stop:
            pass
        P.emit(nc, st, final_wait_chans=[("st", s) for s in range(4)])
    return nc


def _tile_w(w, kc, ncols):
    K, N = w.shape
    return np.ascontiguousarray(w.reshape(K // 128, 128, N // ncols, ncols).transpose(2, 1, 0, 3))


def _rope_tables(S, reverse=False):
    inv = 1.0 / (10000.0 ** (np.arange(0, 64, 2, dtype=np.float32) / 64.0))
    pos = np.arange(S, dtype=np.float32)
    if reverse:
        pos = pos[::-1]
    ang = pos[None, :].astype(np.float32) * inv[:, None].astype(np.float32)
    ang = ang.astype(np.float32)
    idx = (np.arange(128) % 64) % 32
    return np.ascontiguousarray(np.stack([np.cos(ang)[idx], np.sin(ang)[idx]], 0).astype(np.float32))


def _const_inputs():
    ident = np.eye(128, dtype=np.float32)
    pt = np.zeros((128, 128), np.float32)
    for m in range(128):
        if (m % 64) < 32:
            pt[m + 32, m] = -1.0
        else:
            pt[m - 32, m] = 1.0
    ki = np.arange(128)[:, None]
    qi = np.arange(128)[None, :]
    masks = np.stack([(qi <= ki), (ki <= qi)], 1).astype(np.float32)
    return dict(identc=ident, ptm=pt, maskc=np.ascontiguousarray(masks), masknc=np.ascontiguousarray((masks - 1.0) * 30000.0))


def _weight_inputs(w_ada, b_ada, w_in, sink_logit, lam_q1, lam_k1, lam_q2, lam_k2, subln_w, w_a, w_b, w_o,
                   ln1_g, ln1_b, w_gu, w_down, ln2_g, ln2_b):
    w_in = w_in[0]
    qa, ka, va = w_in[:, 0:512], w_in[:, 512:640], w_in[:, 640:768]
    qd, kd, vd = w_in[:, 768:1280], w_in[:, 1280:1792], w_in[:, 1792:2304]
    ga, gb = w_in[:, 2304:3328], w_in[:, 3328:4352]
    perm = np.concatenate([np.concatenate([np.arange(c * 64, c * 64 + 64), np.arange((c + 4) * 64, (c + 4) * 64 + 64)]) for c in range(4)])
    qa_p = qa[:, perm]
    win1 = np.stack([_tile_w(np.concatenate([kd, ka], 1), 8, 640)[0], _tile_w(np.concatenate([vd, va], 1), 8, 640)[0]], 0)
    win2 = np.concatenate([_tile_w(qd, 8, 512), _tile_w(qa_p, 8, 512)], 0)
    wa_p = w_a[0][perm, :]
    wab = _tile_w(np.concatenate([np.concatenate([wa_p[:, q * 256:(q + 1) * 256], w_b[0][:, q * 256:(q + 1) * 256]], 1) for q in range(4)], 1), 4, 512)
    wgab = _tile_w(np.concatenate([np.concatenate([ga[:, q * 256:(q + 1) * 256], gb[:, q * 256:(q + 1) * 256]], 1) for q in range(4)], 1), 8, 512)
    wo = _tile_w(w_o[0], 8, 512)
    g, u = w_gu[0][:, :DFF], w_gu[0][:, DFF:]
    gu = np.concatenate([np.concatenate([g[:, f * 128:(f + 1) * 128], u[:, f * 128:(f + 1) * 128]], 1) for f in range(NFF)], 1)
    wgu = _tile_w(gu, 8, 512)
    wd = w_down[0]
    wdn = np.stack([np.ascontiguousarray(wd[kp * 1408:(kp + 1) * 1408, half * 512:(half + 1) * 512].reshape(11, 128, 512).transpose(1, 0, 2))
                    for half in range(2) for kp in range(2)], 0)
    wada = _tile_w(w_ada[0], 8, 256)
    bcol = np.ascontiguousarray(b_ada[0].reshape(48, 128).T)
    lamv = np.concatenate([lam_q1[0], lam_k1[0], lam_q2[0], lam_k2[0]])[None, :]
    lnrow = np.stack([ln1_g[0], ln1_b[0], ln2_g[0], ln2_b[0]], 0)
    lncol = np.ascontiguousarray(np.stack([ln1_g[0].reshape(8, 128).T, ln1_b[0].reshape(8, 128).T], 1))
    f = lambda a: np.ascontiguousarray(a, dtype=np.float32)
    return dict(win1=f(win1), win2=f(win2), wab=f(wab), wgab=f(wgab), wo=f(wo), wgu=f(wgu), wdn=f(wdn), wada=f(wada), bcolin=f(bcol),
                sink=f(sink_logit), lamv=f(lamv), subwin=f(subln_w), subwcol=f(subln_w[0][:, None]), lnrow=f(lnrow), lncolin=f(lncol))


_NC_CACHE = {}


def run_jobs(core_jobs, jobs_cfg, weights):
    key = tuple(jobs_cfg)
    if key not in _NC_CACHE:
        _NC_CACHE[key] = build(list(jobs_cfg))
    nc = _NC_CACHE[key]
    shared = dict(weights)
    shared.update(_const_inputs())
    in_maps = []
    for cj in core_jobs:
        m = dict(shared)
        cs_ = np.stack([c for (_, c, _) in cj], 0)
        m["cTin"] = np.ascontiguousarray(cs_.reshape(len(cj), 8, 128).transpose(2, 1, 0)).astype(np.float32)
        for j, (x, c, rev) in enumerate(cj):
            m[f"xin{j}"] = np.ascontiguousarray(x, dtype=np.float32)
            m[f"rope{j}"] = _rope_tables(x.shape[0], rev)
        in_maps.append(m)
    res = run_bass_kernel_spmd(nc, in_maps, core_ids=list(range(len(core_jobs))))
    return [[r[f"yout{j}"] for j in range(len(jobs_cfg))] for r in res.results]


def kernel(x_prompt, x_sample, c_prompt, c_sample, w_ada, b_ada, w_in, sink_logit, lam_q1, lam_k1, lam_q2, lam_k2,
           subln_w, w_a, w_b, w_o, ln1_g, ln1_b, w_gu, w_down, ln2_g, ln2_b):
    a = lambda v: np.asarray(v)
    weights = _weight_inputs(a(w_ada), a(b_ada), a(w_in), a(sink_logit), a(lam_q1), a(lam_k1), a(lam_q2), a(lam_k2), a(subln_w),
                             a(w_a), a(w_b), a(w_o), a(ln1_g), a(ln1_b), a(w_gu), a(w_down), a(ln2_g), a(ln2_b))
    xs_ = [a(x_prompt)[i] for i in range(16)] + [a(x_sample)[i] for i in range(4)]
    cs_ = [a(c_prompt)[i] for i in range(16)] + [a(c_sample)[i] for i in range(4)]
    core_jobs = []
    plan = []
    for c in range(8):
        halves = list(range(5 * c, 5 * c + 5))
        seqs = sorted(set(h // 2 for h in halves))
        full = [s for s in seqs if (2 * s in halves and 2 * s + 1 in halves)]
        part = [h for h in halves if (h // 2) not in full]
        assert len(full) == 2 and len(part) == 1
        hp = part[0]
        rev = (hp % 2 == 1)
        sp_ = hp // 2
        xp = xs_[sp_][::-1] if rev else xs_[sp_]
        core_jobs.append([(xs_[full[0]], cs_[full[0]], False), (xs_[full[1]], cs_[full[1]], False), (xp, cs_[sp_], rev)])
        plan.append((full, sp_, rev))
    outs = run_jobs(core_jobs, ((8, 8), (8, 8), (8, 4)), weights)
    y = np.zeros((20, 4096, D), np.float32)
    for c in range(8):
        full, sp_, rev = plan[c]
        y[full[0]] = outs[c][0]
        y[full[1]] = outs[c][1]
        if rev:
            y[sp_, 2048:] = outs[c][2][::-1]
        else:
            y[sp_, :2048] = outs[c][2]
    return (y[:16], y[16:])
```

```python
from contextlib import ExitStack
import types
import numpy as np
import concourse.bass as bass
import concourse.mybir as mybir
from concourse.bass_utils import run_bass_kernel_spmd

F32 = mybir.dt.float32
BF16 = mybir.dt.bfloat16
AF = mybir.ActivationFunctionType
ALU = mybir.AluOpType

D = 1024
DFF = 2816
NFF = 22
ALPHA = 2.0 ** 0.25
LAM_INIT = 0.2
EPS = 1e-5


def _freeze(fn):
    if fn.__closure__ is None:
        return fn
    cells = []
    for c in fn.__closure__:
        try:
            cells.append(types.CellType(c.cell_contents))
        except ValueError:
            cells.append(c)
    return types.FunctionType(fn.__code__, fn.__globals__, fn.__name__, fn.__defaults__, tuple(cells))


class Prog:
    ENG = ("pe", "act", "dve", "pool", "sp")

    def __init__(self):
        self.q = {e: [] for e in self.ENG}
        self.last_w = {}
        self.readers = {}
        self.chans = []

    def op(self, eng, fn, r=(), w=(), chan=None):
        idx = len(self.q[eng])
        deps = set()
        for b in r:
            lw = self.last_w.get(b)
            if lw is not None:
                deps.add(lw)
        for b in w:
            lw = self.last_w.get(b)
            if lw is not None:
                deps.add(lw)
            rd = self.readers.get(b)
            if rd:
                deps.update(rd.values())
        if chan is not None and chan not in self.chans:
            self.chans.append(chan)
        self.q[eng].append(dict(fn=_freeze(fn), deps=deps, chan=chan, signal=False, ticket=None))
        me = (eng, idx)
        for b in r:
            key = eng if chan is None else ("dma", chan)
            self.readers.setdefault(b, {})[key] = me
        for b in w:
            self.last_w[b] = me
            self.readers[b] = {}
        return me

    def emit(self, nc, stack, final_wait_chans=()):
        q = self.q
        for e in self.ENG:
            for ins in q[e]:
                for (e2, i2) in ins["deps"]:
                    q[e2][i2]["signal"] = True
        esem = {e: stack.enter_context(nc.semaphore("s_" + e)) for e in self.ENG if e != "sp"}
        csem = {c: stack.enter_context(nc.semaphore("c_" + str(c))) for c in self.chans}
        ccount = {c: 0 for c in self.chans}
        for e in self.ENG:
            cnt = 0
            for ins in q[e]:
                if ins["chan"] is not None:
                    ccount[ins["chan"]] += 16
                    ins["ticket"] = ccount[ins["chan"]]
                elif ins["signal"]:
                    cnt += 1
                    ins["ticket"] = cnt
        block = stack.enter_context(nc.Block())

        def run(e, eng):
            waited = {}
            for ins in q[e]:
                need = {}
                for (e2, i2) in ins["deps"]:
                    d = q[e2][i2]
                    if d["chan"] is not None:
                        s = csem[d["chan"]]
                        key = ("c", d["chan"])
                    else:
                        if e2 == e and e == "pe":
                            continue
                        s = esem[e2]
                        key = ("e", e2)
                    v = d["ticket"]
                    if v > need.get(key, (None, 0))[1]:
                        need[key] = (s, v)
                for key, (s, v) in need.items():
                    if waited.get(key, 0) >= v:
                        continue
                    eng.wait_ge(s, v)
                    waited[key] = v
                r = ins["fn"](eng)
                if ins["chan"] is not None:
                    r.then_inc(csem[ins["chan"]], 16)
                elif ins["signal"]:
                    r.then_inc(esem[e], 1)
            if e == "pool":
                for c in final_wait_chans:
                    if ccount.get(c, 0) > 0:
                        eng.wait_ge(csem[c], ccount[c])

        @block.tensor
        def _(eng):
            run("pe", eng)

        @block.scalar
        def _(eng):
            run("act", eng)

        @block.vector
        def _(eng):
            run("dve", eng)

        @block.gpsimd
        def _(eng):
            run("pool", eng)

        @block.sync
        def _(eng):
            run("sp", eng)


class _Stop(Exception):
    pass


def _ck(name):
    return None


def build(jobs):
    NJ = len(jobs)
    nc = bass.Bass("TRN2", target_bir_lowering=False)
    P = Prog()

    def din(name, shape, dtype=F32):
        return nc.dram_tensor(name, shape, dtype, kind="ExternalInput").ap()

    def dscr(name, shape, dtype=BF16):
        return nc.dram_tensor(name, shape, dtype, kind="Internal").ap()

    xj = [din(f"xin{j}", [jobs[j][0] * 512, D]) for j in range(NJ)]
    ropej = [din(f"rope{j}", [2, 128, jobs[j][0] * 512]) for j in range(NJ)]
    yj = [nc.dram_tensor(f"yout{j}", [jobs[j][1] * 512, D], F32, kind="ExternalOutput").ap() for j in range(NJ)]
    cT_d = din("cTin", [128, 8, NJ])
    wada_d = din("wada", [24, 128, 8, 256])
    bcol_d = din("bcolin", [128, 48])
    wshapes = dict(win1=[2, 128, 8, 640], win2=[2, 128, 8, 512], wab=[4, 128, 4, 512], wgab=[4, 128, 8, 512],
                   wo=[2, 128, 8, 512], wgu=[11, 128, 8, 512], wdn=[4, 128, 11, 512])
    wf = {k: din(k, v) for k, v in wshapes.items()}
    wb16 = {k: dscr(k + "_b", v) for k, v in wshapes.items()}
    ident_d = din("identc", [128, 128])
    pt_d = din("ptm", [128, 128])
    mask_d = din("maskc", [128, 2, 128])
    maskn_d = din("masknc", [128, 2, 128])
    sink_d = din("sink", [1, 8])
    lamv_d = din("lamv", [1, 256])
    subw_d = din("subwin", [1, 128])
    subwc_d = din("subwcol", [128, 1])
    lnrow_d = din("lnrow", [4, D])
    lncol_d = din("lncolin", [128, 2, 8])

    NKT = max(j[0] for j in jobs)
    SM = NKT * 512
    NBM = NKT * 4

    with ExitStack() as st:
        def sb(name, shape, dtype):
            return st.enter_context(nc.sbuf_tensor(name, shape, dtype))

        KdT = sb("KdT", [128, 4, SM], BF16)
        Vd = sb("Vd", [128, NBM, 516], BF16)
        KaT = sb("KaT", [128, SM], BF16)
        Va = sb("Va", [128, NBM, 192], BF16)
        NRING = 3
        ring = [sb(f"ring{i}", [128, 5632], BF16) for i in range(NRING)]
        x1 = sb("x1", [128, 4, D], F32)
        xs = [sb(f"xs{i}", [128, D], F32) for i in range(2)]
        cs = sb("cs", [128, 2, 512], F32)
        lnv = [sb(f"lnv{i}", [128, D], F32) for i in range(2)]
        lamt = lnv[0][:, 0:256]
        gm = sb("gm", [128, D], BF16)
        gf = sb("gf", [128, D], BF16)
        arA = sb("arA", [128, 11264], BF16)
        arB = sb("arB", [128, 4096], BF16)
        arC = sb("arC", [128, 4096], BF16)
        arD = sb("arD", [128, 4096], BF16)
        ident = sb("ident", [128, 128], F32)
        ones = sb("ones", [128, 128], F32)
        PT = sb("PT", [128, 128], BF16)
        esink = sb("esink", [128, 8], F32)
        lsm = sb("lsm", [128, 8], F32)
        mhalf = sb("mhalf", [128, 16], F32)
        cT = sb("cT", [128, 8, NJ], F32)
        siluT = sb("siluT", [128, 8, NJ], F32)
        modT = sb("modT", [128, 48, NJ], F32)
        bcol = sb("bcol", [128, 48], F32)
        lncol = sb("lncol", [128, 2, 8], F32)
        cols = sb("cols", [128, NJ, 4, 8], F32)
        onesP = sb("onesP", [128, 192], BF16)
        esink2 = sb("esink2", [128, 4], F32)
        ones_bf = sb("ones_bf", [128, 128], BF16)
        ident_bf = sb("ident_bf", [128, 128], BF16)
        maskn = sb("maskn", [128, 2, 128], BF16)
        subwc = sb("subwc", [128, 1], F32)
        stt = sb("stt", [128, 64], F32)
        bst = sb("bst", [128, 2, 6], F32)
        psum = [st.enter_context(nc.psum_tensor(f"ps{i}", [128, 2, 512], F32)) for i in range(4)]

        def bank(b):
            return psum[b // 2][:, b % 2, :]

        def bkey(b):
            return f"ps{b}"

        def pg(ar, name, p0, n, dtype=BF16):
            ap = ar[:, p0 * 512:(p0 + n) * 512]
            if dtype == F32:
                ap = ap.bitcast(F32)
            return ap, [(name, p) for p in range(p0, p0 + n)]

        actT = lambda f: pg(arA, "A", f, 1)
        QdT = lambda c: pg(arA, "A", c, 1)
        QaT_all = arA[:, 4 * 512:8 * 512].rearrange("p (c t) -> p c t", c=4)
        QaT = lambda c: pg(arA, "A", 4 + c, 1)
        Oatmp = pg(arA, "A", 8, 2, F32)
        Odtmp = pg(arA, "A", 10, 2, F32)
        OaT = lambda k: pg(arA, "A", 12 + k, 1)
        OdT = lambda k: pg(arA, "A", 16 + k, 1)
        hT_all = arB[:, :].rearrange("p (k t) -> p k t", k=8)
        hkeys = lambda k, s=None: [("B", k, ss) for ss in (range(4) if s is None else [s])]
        mergedT = lambda oc: pg(arC, "C", oc, 1)
        Ed = lambda i: pg(arC, "C", 2 * i, 2)
        Ew = lambda i: pg(arC, "C", 6 + i, 1)
        tgm = lambda i: pg(arC, "C", 2 * i, 2, F32)
        wm = lambda i: pg(arC, "C", 4 + 2 * i, 2, F32)
        qb_ = pg(arD, "D", 0, 1)
        t1_ = pg(arD, "D", 1, 2, F32)
        t2_ = pg(arD, "D", 3, 2, F32)
        ta_ = pg(arD, "D", 0, 2, F32)
        tb_ = pg(arD, "D", 2, 2, F32)
        m1_ = pg(arD, "D", 4, 2, F32)
        m2_ = pg(arD, "D", 6, 2, F32)
        tmp_ = pg(arD, "D", 0, 4, F32)
        xn_ = pg(arD, "D", 4, 4, F32)

        sched = []
        ring_state = dict(issued=0, used=0)
        released = set()

        def ring_pump():
            while ring_state["issued"] < len(sched):
                n = ring_state["issued"]
                if n >= NRING and (n - NRING) not in released:
                    break
                slot = n % NRING
                wname, wi = sched[n]
                src = wb16[wname][wi].rearrange("p k n -> p (k n)")
                ne = src.shape[1]
                P.op("sp", lambda e, slot=slot, src=src, ne=ne: e.dma_start(out=ring[slot][:, 0:ne], in_=src),
                     r=[("scr", wname, wi)], w=[("ring", slot)], chan=("ring", slot))
                ring_state["issued"] += 1

        def ring_next():
            i = ring_state["used"]
            ring_state["used"] += 1
            ring_pump()
            assert ring_state["issued"] > i, "ring deadlock: too many pieces held"
            return ring[i % NRING], ("ring", i % NRING), i

        def ring_rel(i):
            released.add(i)
            ring_pump()

        def piece(name, i):
            return (name, i)

        for j in range(NJ):
            nkt, nqt = jobs[j]
            sched += [piece("win1", 0), piece("win1", 1)]
            for t in range(nqt):
                sched += [piece("win2", 0), piece("win2", 1)]
                for q in range(4):
                    sched += [piece("wab", q), piece("wgab", q)]
                sched += [piece("wo", 0), piece("wo", 1)]
                sched += [piece("wgu", i) for i in range(11)]
                sched += [piece("wdn", i) for i in range(4)]

        def cast_weights(names, extra_r=()):
            for k in names:
                for i in range(wshapes[k][0]):
                    P.op("pool", lambda e, k=k, i=i: e.dma_start(out=wb16[k][i], in_=wf[k][i]), r=list(extra_r), w=[("scr", k, i)], chan=("cast", k, i))
        cast_weights(("win1", "win2", "wab", "wgab", "wo", "wgu", "wdn"))
        ld = lambda out, in_, keys: P.op("sp", lambda e: e.dma_start(out=out, in_=in_), w=keys, chan="const")
        ld(ident[:], ident_d, ["ident"])
        ld(cT[:], cT_d, ["cT"])
        ld(bcol[:], bcol_d, ["bcol"])
        ld(lncol[:], lncol_d, ["lncol"])
        ld(esink[:], sink_d.partition_broadcast(128), ["esink"])
        ld(lamt, lamv_d.partition_broadcast(128), [("lnv", 0)])
        ld(subwc[:], subwc_d, ["subwc"])
        P.op("pool", lambda e: e.dma_start(out=PT[:], in_=pt_d), w=["PT"], chan="constc")
        P.op("pool", lambda e: e.dma_start(out=ident_bf[:], in_=ident_d), w=["ident_bf"], chan="constc")
        P.op("pool", lambda e: e.dma_start(out=maskn[:], in_=maskn_d), w=["maskn"], chan="constc")
        P.op("pool", lambda e: e.memset(ones[:], 1.0), w=["ones"])
        P.op("pool", lambda e: e.memset(onesP[:], 1.0), w=["onesP"])
        P.op("pool", lambda e: e.memset(onesP[:, 64:128], 0.0), r=["onesP"], w=["onesP"])
        P.op("pool", lambda e: e.memset(ones_bf[:], 1.0), w=["ones_bf"])
        P.op("pool", lambda e: e.memset(mhalf[:], -0.5), w=["mhalf"])
        P.op("pool", lambda e: e.memset(Vd[:], 1.0), w=["Vd"])
        P.op("pool", lambda e: e.memset(Va[:], 0.0), w=["Va"])
        P.op("act", lambda e: e.activation(out=esink[:], in_=esink[:], func=AF.Exp), r=["esink"], w=["esink"])
        P.op("dve", lambda e: e.tensor_copy(out=esink2[0:64, :], in_=esink[0:64, 0:4]), r=["esink"], w=["esink2"])
        P.op("dve", lambda e: e.tensor_copy(out=esink2[64:128, :], in_=esink[64:128, 4:8]), r=["esink", "esink2"], w=["esink2"])
        P.op("dve", lambda e: e.tensor_tensor(out=lamt[:, 0:64], in0=lamt[:, 0:64], in1=lamt[:, 64:128], op=ALU.mult), r=[("lnv", 0)], w=[("lnv", 0)])
        P.op("dve", lambda e: e.tensor_tensor(out=lamt[:, 128:192], in0=lamt[:, 128:192], in1=lamt[:, 192:256], op=ALU.mult), r=[("lnv", 0)], w=[("lnv", 0)])
        P.op("act", lambda e: e.activation(out=lamt[:, 64:128], in_=lamt[:, 0:64], func=AF.Identity, accum_out=lsm[:, 0:1]), r=[("lnv", 0)], w=[("lnv", 0), "lsm"])
        P.op("act", lambda e: e.activation(out=lamt[:, 192:256], in_=lamt[:, 128:192], func=AF.Identity, accum_out=lsm[:, 1:2]), r=[("lnv", 0)], w=[("lnv", 0), "lsm"])
        P.op("act", lambda e: e.activation(out=lsm[:, 2:4], in_=lsm[:, 0:2], func=AF.Exp), r=["lsm"], w=["lsm"])
        P.op("dve", lambda e: e.tensor_tensor(out=lsm[:, 4:5], in0=lsm[:, 3:4], in1=lsm[:, 2:3], op=ALU.subtract), r=["lsm"], w=["lsm"])
        P.op("dve", lambda e: e.tensor_scalar_add(out=lsm[:, 5:6], in0=lsm[:, 4:5], scalar1=-LAM_INIT), r=["lsm"], w=["lsm"])
        neglam = lsm[:, 5:6]
        P.op("dve", lambda e: e.tensor_scalar_mul(out=subwc[:], in0=subwc[:], scalar1=1.0 - LAM_INIT), r=["subwc"], w=["subwc"])
        P.op("act", lambda e: e.activation(out=siluT[:], in_=cT[:], func=AF.Tanh, scale=0.5), r=["cT"], w=["siluT"])
        P.op("dve", lambda e: e.scalar_tensor_tensor(out=siluT[:], in0=siluT[:], scalar=1.0, in1=cT[:], op0=ALU.add, op1=ALU.mult),
             r=["siluT", "cT"], w=["siluT"])
        P.op("dve", lambda e: e.tensor_scalar_mul(out=siluT[:], in0=siluT[:], scalar1=0.5), r=["siluT"], w=["siluT"])
        wstage = [arA[:, i * 4096:(i + 1) * 4096].bitcast(F32).rearrange("p (k n) -> p k n", k=8) for i in range(2)]
        wstk = [[("A", p) for p in range(8 * i, 8 * i + 8)] for i in range(2)]
        mps = psum[0][:, 0, 0:48 * NJ].rearrange("p (c j) -> p c j", j=NJ)
        for pc in range(24):
            i = pc % 2
            P.op("sp", lambda e, pc=pc, i=i: e.dma_start(out=wstage[i], in_=wada_d[pc]), w=wstk[i], chan=("wst", i))
            for cc in range(2):
                ch = pc * 2 + cc
                for kk in range(8):
                    P.op("pe", lambda e, i=i, cc=cc, kk=kk, ch=ch: e.matmul(mps[:, ch, :], lhsT=wstage[i][:, kk, cc * 128:(cc + 1) * 128],
                                                                          rhs=siluT[:, kk, :], start=(kk == 0), stop=(kk == 7)),
                         r=wstk[i] + ["siluT"], w=[bkey(0)])
        for j in range(NJ):
            P.op("dve", lambda e, j=j: e.tensor_tensor(out=modT[:, :, j], in0=mps[:, :, j], in1=bcol[:], op=ALU.add),
                 r=[bkey(0), "bcol"], w=["modT"])
        for j in range(NJ):
            P.op("dve", lambda e, j=j: e.tensor_scalar_add(out=cols[:, j, 0, :], in0=modT[:, 8:16, j], scalar1=1.0), r=["modT"], w=["cols"])
            P.op("dve", lambda e, j=j: e.tensor_copy(out=cols[:, j, 1, :], in_=modT[:, 0:8, j]), r=["modT"], w=["cols"])
            P.op("dve", lambda e, j=j: e.scalar_tensor_tensor(out=cols[:, j, 2, :], in0=modT[:, 32:40, j], scalar=1.0, in1=lncol[:, 0, :],
                                                              op0=ALU.add, op1=ALU.mult), r=["modT", "lncol"], w=["cols"])
            P.op("dve", lambda e, j=j: e.scalar_tensor_tensor(out=cols[:, j, 3, :], in0=modT[:, 32:40, j], scalar=1.0, in1=lncol[:, 1, :],
                                                              op0=ALU.add, op1=ALU.mult), r=["modT", "lncol"], w=["cols"])
            P.op("dve", lambda e, j=j: e.tensor_tensor(out=cols[:, j, 3, :], in0=cols[:, j, 3, :], in1=modT[:, 24:32, j], op=ALU.add),
                 r=["modT", "cols"], w=["cols"])

        rr = dict(ps=0)
        rope_rr = dict(n=0)
        acc_rr = dict(n=0)

        def next_bank(lo=0, hi=8):
            b = lo + rr["ps"] % (hi - lo)
            rr["ps"] += 1
            return b

        alt = dict(n=0)

        def evac_eng():
            alt["n"] += 1
            return "act" if alt["n"] % 2 else "dve"

        def load_x_sub(j, row0, slot):
            P.op("sp", lambda e: e.dma_start(out=xs[slot][:], in_=xj[j][row0:row0 + 128, :]), w=[("xs", slot)], chan=("xs", slot))

        xs_rr = dict(n=0)
        xpre = {}

        def prefetch_x(j, t):
            for s in range(2):
                slot = xs_rr["n"] % 2
                xs_rr["n"] += 1
                load_x_sub(j, t * 512 + s * 128, slot)
                xpre[(j, t, s)] = slot

        def make_hT(j, t):
            for s in range(4):
                if (j, t, s) in xpre:
                    slot = xpre.pop((j, t, s))
                else:
                    slot = xs_rr["n"] % 2
                    xs_rr["n"] += 1
                    load_x_sub(j, t * 512 + s * 128, slot)
                _ck("p1_x")
                for kg in range(2):
                    b = next_bank()
                    for kk in range(4):
                        k = kg * 4 + kk
                        if kk == 1:
                            _ck("p1_tr")
                        P.op("pe", lambda e, b=b, kk=kk, k=k, slot=slot: e.transpose(bank(b)[:, kk * 128:(kk + 1) * 128],
                                                                                 xs[slot][:, k * 128:(k + 1) * 128], ident[:]),
                             r=[("xs", slot), "ident"], w=[bkey(b)])
                    for kk in range(4):
                        k = kg * 4 + kk
                        if kk == 1:
                            _ck("p1_ev1")
                        if kk == 2:
                            _ck("p1_ev2")
                        eng = "act"
                        out = hT_all[:, k, s * 128:(s + 1) * 128]
                        src = bank(b)[:, kk * 128:(kk + 1) * 128]
                        if eng == "act":
                            P.op("act", lambda e, out=out, src=src, k=k: e.activation(out=out, in_=src, func=AF.Identity,
                                                                                   bias=cols[:, j, 1, k:k + 1], scale=cols[:, j, 0, k:k + 1]),
                                 r=[bkey(b), "cols"], w=hkeys(k, s))
                        else:
                            P.op("dve", lambda e, out=out, src=src, k=k: e.tensor_scalar(out=out, in0=src, scalar1=cols[:, j, 0, k:k + 1],
                                                                                      scalar2=cols[:, j, 1, k:k + 1], op0=ALU.mult, op1=ALU.add),
                                 r=[bkey(b), "cols"], w=hkeys(k, s))

        def load_cs(j, t):
            P.op("sp", lambda e: e.dma_start(out=cs[:], in_=ropej[j][:, :, t * 512:(t + 1) * 512].rearrange("c p t -> p c t")),
                 w=["cs"], chan="cs")

        def mm_fm(b, wv, col0, rhs_fn, nk, rkeys):
            for k in range(nk):
                P.op("pe", lambda e, k=k: e.matmul(bank(b), lhsT=wv[:, k, col0:col0 + 128], rhs=rhs_fn(k), start=(k == 0), stop=(k == nk - 1)),
                     r=rkeys(k), w=[bkey(b)])

        rope_pend = []

        def rope_chunk(b, out_ap, out_keys):
            rope_rr["n"] += 1
            odd = rope_rr["n"] % 2
            if odd:
                qb, qbk = qb_
                t1, t1k = t1_
                t2, t2k = t2_
            else:
                qb, qbk = pg(arC, "C", 0, 1)
                t1, t1k = pg(arC, "C", 1, 2, F32)
                t2, t2k = pg(arC, "C", 3, 2, F32)
            P.op("act", lambda e: e.activation(out=qb, in_=bank(b), func=AF.Copy), r=[bkey(b)], w=qbk)

            def tail():
                b2 = next_bank()
                P.op("pe", lambda e: e.matmul(bank(b2), lhsT=PT[:], rhs=qb, start=True, stop=True), r=qbk + ["PT"], w=[bkey(b2)])
                P.op("dve", lambda e: e.tensor_tensor(out=t1, in0=bank(b), in1=cs[:, 0, :], op=ALU.mult), r=[bkey(b), "cs"] + qbk, w=t1k)
                P.op("dve", lambda e: e.tensor_tensor(out=t2, in0=bank(b2), in1=cs[:, 1, :], op=ALU.mult), r=[bkey(b2), "cs"], w=t2k)
                P.op("pool" if odd else "dve", lambda e: e.tensor_tensor(out=out_ap, in0=t1, in1=t2, op=ALU.add), r=t1k + t2k, w=out_keys)
            rope_flush()
            rope_pend.append(tail)

        def rope_flush():
            while rope_pend:
                rope_pend.pop(0)()

        def job_body(j):
            nkt, nqt = jobs[j]
            NB = nkt * 4
            _ck("prologue")
            for (v0, gt, gkey) in ((16, gm, "gm"), (40, gf, "gf")):
                for half in range(2):
                    b = next_bank()
                    for kk in range(4):
                        k = half * 4 + kk
                        dg = xs[kk % 2][:, 0:128]
                        P.op("dve", lambda e, dg=dg, k=k, v0=v0: e.tensor_scalar_mul(out=dg, in0=ident[:], scalar1=modT[:, v0 + k, j:j + 1]),
                             r=["ident", "modT"], w=[("xs", kk % 2)])
                        P.op("pe", lambda e, dg=dg, b=b, kk=kk: e.matmul(bank(b)[:, kk * 128:(kk + 1) * 128], lhsT=ones[:], rhs=dg, start=True, stop=True),
                             r=[("xs", kk % 2), "ones"], w=[bkey(b)])
                    P.op("act", lambda e, b=b, gt=gt, half=half: e.activation(out=gt[:, half * 512:(half + 1) * 512], in_=bank(b), func=AF.Identity, scale=0.5),
                         r=[bkey(b)], w=[gkey])

            _ck("gates")
            w1a, w1ak, w1ai = ring_next()
            w1b, w1bk, w1bi = ring_next()
            w1av = w1a[:, 0:5120].rearrange("p (k n) -> p k n", k=8)
            w1bv = w1b[:, 0:5120].rearrange("p (k n) -> p k n", k=8)
            _ck("p1_ring")
            for t in range(nkt):
                load_cs(j, t)
                _ck("p1_cs")
                make_hT(j, t)
                _ck("p1_hT")
                for c in range(5):
                    b = next_bank()
                    mm_fm(b, w1av, c * 128, lambda k: hT_all[:, k, :], 8, lambda k: [w1ak] + hkeys(k))
                    if c < 4:
                        rope_chunk(b, KdT[:, c, t * 512:(t + 1) * 512], [("Kd", c, t)])
                    else:
                        rope_chunk(b, KaT[:, t * 512:(t + 1) * 512], [("Ka", t)])
                rope_flush()
                _ck("p1_k")
                for s in range(4):
                    blk = t * 4 + s
                    b = next_bank()
                    for k in range(8):
                        P.op("pe", lambda e, b=b, k=k, s=s: e.matmul(bank(b), lhsT=hT_all[:, k, s * 128:(s + 1) * 128], rhs=w1bv[:, k, 0:512],
                                                                   start=(k == 0), stop=(k == 7)), r=[w1bk] + hkeys(k, s), w=[bkey(b)])
                    eng = evac_eng()
                    outv = Vd[:, blk, :].rearrange("p (h e) -> p h e", e=129)[:, :, 0:128]
                    srcv = bank(b).rearrange("p (h e) -> p h e", e=128)
                    if eng == "act":
                        P.op("act", lambda e, outv=outv, srcv=srcv: e.activation(out=outv, in_=srcv, func=AF.Copy), r=[bkey(b), "Vd"], w=[("Vd", blk)])
                    else:
                        P.op("dve", lambda e, outv=outv, srcv=srcv: e.tensor_copy(out=outv, in_=srcv), r=[bkey(b), "Vd"], w=[("Vd", blk)])
                    b2 = next_bank()
                    for k in range(8):
                        P.op("pe", lambda e, b2=b2, k=k, s=s: e.matmul(bank(b2)[:, 0:128], lhsT=hT_all[:, k, s * 128:(s + 1) * 128], rhs=w1bv[:, k, 512:640],
                                                                     start=(k == 0), stop=(k == 7)), r=[w1bk] + hkeys(k, s), w=[bkey(b2)])
                    for g in range(2):
                        P.op("dve", lambda e, b2=b2, g=g, blk=blk: e.tensor_copy(out=Va[:, blk, g * 128:g * 128 + 64], in_=bank(b2)[:, g * 64:(g + 1) * 64]),
                             r=[bkey(b2), "Va"], w=[("Va", blk)])
            ring_rel(w1ai)
            ring_rel(w1bi)
            _ck("phase1")
            for t in range(nqt):
                load_cs(j, t)
                make_hT(j, t)
                for pi, qdst in ((0, QdT), (1, QaT)):
                    wt, wk, wi_ = ring_next()
                    wv = wt[:, 0:4096].rearrange("p (k n) -> p k n", k=8)
                    for c in range(4):
                        b = next_bank()
                        mm_fm(b, wv, c * 128, lambda k: hT_all[:, k, :], 8, lambda k: [wk] + hkeys(k))
                        oap, okeys = qdst(c)
                        rope_chunk(b, oap, okeys)
                    ring_rel(wi_)
                rope_flush()
                _ck("s2")
                oaT_all = arA[:, 12 * 512:16 * 512].rearrange("p (k t) -> p k t", k=4)
                den_sb, denk = pg(arD, "D", 0, 2, F32)
                steps = []
                for s in range(4):
                    n = t * 4 + s
                    lst = [(g, jb) for g in range(2) for jb in (n - 1, n, n + 1) if 0 <= jb < NB]
                    for li, (g, jb) in enumerate(lst):
                        steps.append((s, n, g, jb, li == 0, li == len(lst) - 1))

                def w_front(i, st_):
                    s, n, g, jb, first, last = st_
                    sbk = next_bank(0, 4)
                    P.op("pe", lambda e: e.matmul(
                        bank(sbk).rearrange("p (c q) -> p c q", c=4), lhsT=KaT[g * 64:(g + 1) * 64, jb * 128:(jb + 1) * 128],
                        rhs=QaT_all[g * 64:(g + 1) * 64, :, s * 128:(s + 1) * 128], start=True, stop=(jb == n)),
                        r=[("Ka", jb // 4)] + [("A", 4 + c) for c in range(4)], w=[bkey(sbk)])
                    if jb != n:
                        mi = 0 if jb < n else 1
                        P.op("pe", lambda e: e.matmul(
                            bank(sbk).rearrange("p (c q) -> p c q", c=4), lhsT=ident_bf[:],
                            rhs=maskn[:, mi:mi + 1, :].to_broadcast([128, 4, 128]), start=False, stop=True),
                            r=["ident_bf", "maskn"], w=[bkey(sbk)])
                    ew, ewk = Ew(i % 2)
                    P.op("act", lambda e: e.activation(out=ew, in_=bank(sbk), func=AF.Exp, scale=0.125), r=[bkey(sbk)], w=ewk)
                    return ew, ewk

                def w_back(i, st_, ew, ewk):
                    s, n, g, jb, first, last = st_
                    pa = 2 + (s % 2)
                    accO, accS = psum[pa][:, 0, :], psum[pa][:, 1, :]
                    kO, kS = bkey(pa * 2), bkey(pa * 2 + 1)
                    P.op("pe", lambda e: e.matmul(accO, lhsT=Va[:, jb, g * 64:g * 64 + 128], rhs=ew, start=first, stop=last),
                         r=ewk + [("Va", jb)], w=[kO])
                    P.op("pe", lambda e: e.matmul(accS, lhsT=onesP[:, g * 64:g * 64 + 128], rhs=ew, start=first, stop=last),
                         r=ewk + ["onesP"], w=[kS])
                    if last:
                        dv = den_sb.rearrange("p (h q) -> p h q", h=4)
                        P.op("dve", lambda e: e.tensor_tensor(out=dv, in0=accS.rearrange("p (h q) -> p h q", h=4),
                                                              in1=esink2[:, :].unsqueeze(2).to_broadcast([128, 4, 128]), op=ALU.add),
                             r=[kS, "esink2"], w=denk)
                        P.op("dve", lambda e: e.reciprocal(out=den_sb, in_=den_sb), r=denk, w=denk)
                        P.op("dve", lambda e: e.tensor_tensor(out=oaT_all[:, :, s * 128:(s + 1) * 128], in0=accO.rearrange("p (h q) -> p h q", h=4),
                                                              in1=dv, op=ALU.mult),
                             r=[kO] + denk, w=[("A", 12 + k) for k in range(4)])

                prev = None
                for i in range(len(steps) + 1):
                    cur = None
                    if i < len(steps):
                        ew, ewk = w_front(i, steps[i])
                        cur = (i, steps[i], ew, ewk)
                    if prev is not None:
                        w_back(*prev)
                    prev = cur
                odt, odtk = Odtmp
                rcb, rck = pg(arD, "D", 0, 2, F32)
                tt_, ttk = pg(arD, "D", 2, 2, F32)
                sqb, sqk = pg(arD, "D", 4, 2, F32)
                veb, vek = pg(arD, "D", 6, 2, F32)
                npair = NB // 2
                items = [(h, c, jp) for h in range(4) for c in range(2) for jp in range(npair)]
                grp = {}

                def d_group(h, c):
                    if (h, c) not in grp:
                        pa = 2 + (acc_rr["n"] % 2)
                        acc_rr["n"] += 1
                        grp[(h, c)] = (psum[pa][:, 0, :], psum[pa][:, 1, :], bkey(pa * 2), bkey(pa * 2 + 1))
                    return grp[(h, c)]

                def d_front(idx):
                    h, c, jp = items[idx]
                    sp_ = idx % 2
                    for jj in range(2):
                        jb = 2 * jp + jj
                        P.op("pe", lambda e, jj=jj, jb=jb: e.matmul(
                            psum[sp_][:, jj, :], lhsT=KdT[c * 64:(c + 1) * 64, h, jb * 128:(jb + 1) * 128],
                            rhs=arA[c * 64:(c + 1) * 64, h * 512:(h + 1) * 512], start=True, stop=True),
                            r=[("Kd", h, jb // 4), ("A", h)], w=[bkey(sp_ * 2 + jj)])
                    ed, edk = Ed(idx % 3)
                    P.op("act", lambda e: e.activation(out=ed.rearrange("p (a q) -> p a q", a=2), in_=psum[sp_][:, :, :], func=AF.Exp, scale=0.125),
                         r=[bkey(sp_ * 2), bkey(sp_ * 2 + 1)], w=edk)
                    return ed, edk

                def d_rms(h):
                    P.op("act", lambda e: e.activation(out=sqb, in_=odt, func=AF.Square), r=odtk, w=sqk)
                    b = next_bank(0, 4)
                    P.op("pe", lambda e: e.matmul(bank(b), lhsT=ones[:], rhs=sqb, start=True, stop=True), r=sqk + ["ones"], w=[bkey(b)])
                    P.op("dve", lambda e: e.tensor_scalar(out=veb, in0=bank(b), scalar1=1.0 / 128.0, scalar2=EPS, op0=ALU.mult, op1=ALU.add),
                         r=[bkey(b)], w=vek)
                    I32 = mybir.dt.int32
                    yb, ybk = rcb, rck
                    P.op("dve", lambda e: e.tensor_single_scalar(out=yb.bitcast(I32), in_=veb.bitcast(I32), scalar=1, op=ALU.arith_shift_right), r=vek, w=ybk)
                    P.op("dve", lambda e: e.tensor_scalar(out=yb.bitcast(I32), in0=yb.bitcast(I32), scalar1=-1, scalar2=0x5f3759df, op0=ALU.mult, op1=ALU.add),
                         r=ybk, w=ybk)
                    for it in range(2):
                        P.op("dve", lambda e: e.tensor_tensor(out=tt_, in0=veb, in1=yb, op=ALU.mult), r=vek + ybk, w=ttk)
                        P.op("dve", lambda e: e.tensor_tensor(out=tt_, in0=tt_, in1=yb, op=ALU.mult), r=ttk + ybk, w=ttk)
                        P.op("dve", lambda e: e.tensor_scalar(out=tt_, in0=tt_, scalar1=-0.5, scalar2=1.5, op0=ALU.mult, op1=ALU.add), r=ttk, w=ttk)
                        P.op("dve", lambda e: e.tensor_tensor(out=yb, in0=yb, in1=tt_, op=ALU.mult), r=ttk + ybk, w=ybk)
                    P.op("dve", lambda e: e.tensor_tensor(out=sqb, in0=odt, in1=yb, op=ALU.mult), r=odtk + ybk, w=sqk)
                    oap, oak = OdT(h)
                    P.op("dve", lambda e: e.tensor_scalar_mul(out=oap, in0=sqb, scalar1=subwc[:, 0:1]), r=sqk + ["subwc"], w=oak)

                esum_buf = [pg(arC, "C", 6, 1), pg(arC, "C", 7, 1)]
                sum_pend = []

                def d_sum_flush():
                    while sum_pend:
                        es, esk, accS, kS, first, last = sum_pend.pop(0)
                        P.op("pe", lambda e: e.matmul(accS, lhsT=ones_bf[:], rhs=es, start=first, stop=last), r=esk + ["ones_bf"], w=[kS])

                def d_back(idx, ed, edk):
                    h, c, jp = items[idx]
                    accO, accS, kO, kS = d_group(h, c)
                    es, esk = esum_buf[idx % 2]
                    d_sum_flush()
                    P.op("pool", lambda e: e.tensor_tensor(out=es, in0=ed[:, 0:512], in1=ed[:, 512:1024], op=ALU.add), r=edk, w=esk)
                    for jj in range(2):
                        jb = 2 * jp + jj
                        P.op("pe", lambda e, jj=jj, jb=jb: e.matmul(accO, lhsT=Vd[:, jb, h * 129:h * 129 + 128], rhs=ed[:, jj * 512:(jj + 1) * 512],
                                                                  start=(jb == 0), stop=(jb == NB - 1)), r=edk + [("Vd", jb)], w=[kO])
                    sum_pend.append((es, esk, accS, kS, jp == 0, jp == npair - 1))
                    if jp == npair - 1:
                        d_sum_flush()
                    if jp == npair - 1:
                        P.op("dve", lambda e: e.reciprocal(out=rcb, in_=accS), r=[kS], w=rck)
                        if c == 0:
                            P.op("dve", lambda e: e.tensor_tensor(out=odt, in0=accO, in1=rcb, op=ALU.mult), r=[kO] + rck, w=odtk)
                        else:
                            P.op("dve", lambda e: e.tensor_tensor(out=tt_, in0=accO, in1=rcb, op=ALU.mult), r=[kO] + rck, w=ttk)
                            P.op("dve", lambda e: e.scalar_tensor_tensor(out=odt, in0=tt_, scalar=neglam, in1=odt, op0=ALU.mult, op1=ALU.add),
                                 r=ttk + odtk + ["lsm"], w=odtk)
                            return h
                    return None

                dpend = None
                rms_q = []
                for idx in range(len(items) + 1):
                    cur = None
                    if idx < len(items):
                        ed, edk = d_front(idx)
                        cur = (idx, ed, edk)
                    while rms_q and rms_q[0][0] <= idx:
                        d_rms(rms_q.pop(0)[1])
                    if dpend is not None:
                        hdone = d_back(*dpend)
                        if hdone is not None:
                            rms_q.append((idx + 3, hdone))
                    dpend = cur
                while rms_q:
                    d_rms(rms_q.pop(0)[1])
                _ck("s4")
                for qtr in range(4):
                    wabt, wabk, wabi = ring_next()
                    gabt, gabk, gabi = ring_next()
                    wabv = wabt[:, 0:2048].rearrange("p (k n) -> p k n", k=4)
                    gabv = gabt[:, 0:4096].rearrange("p (k n) -> p k n", k=8)
                    for cc in range(2):
                        oc = qtr * 2 + cc
                        bA = next_bank()
                        mm_fm(bA, wabv, cc * 128, lambda k: OaT(k)[0], 4, lambda k: [wabk] + OaT(k)[1])
                        bB = next_bank()
                        mm_fm(bB, wabv, 256 + cc * 128, lambda k: OdT(k)[0], 4, lambda k: [wabk] + OdT(k)[1])
                        bGa = next_bank()
                        mm_fm(bGa, gabv, cc * 128, lambda k: hT_all[:, k, :], 8, lambda k: [gabk] + hkeys(k))
                        bGb = next_bank()
                        mm_fm(bGb, gabv, 256 + cc * 128, lambda k: hT_all[:, k, :], 8, lambda k: [gabk] + hkeys(k))
                        ta, tak = ta_
                        tb, tbk = tb_
                        m1, m1k = m1_
                        m2, m2k = m2_
                        P.op("act", lambda e, bGa=bGa: e.activation(out=ta, in_=bank(bGa), func=AF.Tanh, scale=0.5), r=[bkey(bGa)], w=tak)
                        P.op("act", lambda e, bGb=bGb: e.activation(out=tb, in_=bank(bGb), func=AF.Tanh, scale=0.5), r=[bkey(bGb)], w=tbk)
                        P.op("dve", lambda e, bA=bA: e.scalar_tensor_tensor(out=m1, in0=ta, scalar=1.0, in1=bank(bA), op0=ALU.add, op1=ALU.mult),
                             r=tak + [bkey(bA)], w=m1k)
                        P.op("dve", lambda e, bB=bB: e.scalar_tensor_tensor(out=m2, in0=tb, scalar=1.0, in1=bank(bB), op0=ALU.add, op1=ALU.mult),
                             r=tbk + [bkey(bB)], w=m2k)
                        mo, mok = mergedT(oc)
                        P.op("pool", lambda e, mo=mo: e.tensor_tensor(out=mo, in0=m1, in1=m2, op=ALU.add), r=m1k + m2k, w=mok)
                    ring_rel(wabi)
                    ring_rel(gabi)
                _ck("s6")
                wo0, wo0k, wo0i = ring_next()
                wo1, wo1k, wo1i = ring_next()
                wov = [wo0[:, 0:4096].rearrange("p (k n) -> p k n", k=8), wo1[:, 0:4096].rearrange("p (k n) -> p k n", k=8)]
                wok = [wo0k, wo1k]
                P.op("sp", lambda e: e.dma_start(out=lnv[0][:], in_=lnrow_d[0:1, :].partition_broadcast(128)), w=[("lnv", 0)], chan=("lnv", 0))
                P.op("sp", lambda e: e.dma_start(out=lnv[1][:], in_=lnrow_d[1:2, :].partition_broadcast(128)), w=[("lnv", 1)], chan=("lnv", 1))
                tmps = [pg(arD, "D", 0, 4, F32), pg(arD, "D", 4, 4, F32)]
                xns = [pg(arA, "A", 0, 4, F32), pg(arA, "A", 4, 4, F32)]

                def s7_front(s):
                    tmp, tmpk = tmps[s % 2]
                    xn, xnk = xns[s % 2]
                    slot = xs_rr["n"] % 2
                    xs_rr["n"] += 1
                    load_x_sub(j, t * 512 + s * 128, slot)
                    for half in range(2):
                        b = next_bank()
                        for k in range(8):
                            P.op("pe", lambda e, b=b, k=k, s=s, half=half: e.matmul(bank(b), lhsT=arC[:, k * 512 + s * 128:k * 512 + (s + 1) * 128],
                                                                                  rhs=wov[half][:, k, :], start=(k == 0), stop=(k == 7)),
                                 r=[wok[half], ("C", k)], w=[bkey(b)])
                        P.op("dve", lambda e, b=b, half=half, tmp=tmp: e.tensor_tensor(out=tmp[:, half * 512:(half + 1) * 512], in0=bank(b),
                                                                            in1=gm[:, half * 512:(half + 1) * 512], op=ALU.mult),
                             r=[bkey(b), "gm"], w=tmpk)
                    P.op("dve", lambda e, s=s, slot=slot, tmp=tmp: e.scalar_tensor_tensor(out=x1[:, s, :], in0=xs[slot][:], scalar=ALPHA, in1=tmp,
                                                                                  op0=ALU.mult, op1=ALU.add), r=[("xs", slot)] + tmpk, w=[("x1", s)])
                    for hh in range(2):
                        P.op("dve", lambda e, s=s, hh=hh: e.bn_stats(out=bst[:, hh, :], in_=x1[:, s, hh * 512:(hh + 1) * 512]), r=[("x1", s)], w=["bst"])
                    P.op("dve", lambda e: e.bn_aggr(out=stt[:, 32:34], in_=bst[:]), r=["bst"], w=["stt_mv"])
                    P.op("pool", lambda e: e.tensor_scalar_add(out=stt[:, 34:35], in0=stt[:, 33:34], scalar1=EPS), r=["stt_mv"], w=["stt_ve2"])
                    P.op("pool", lambda e: e.tensor_tensor(out=stt[:, 35:36], in0=stt[:, 34:35], in1=mhalf[:, 0:1], op=ALU.pow), r=["stt_ve2", "mhalf"], w=["stt_rs2"])
                    P.op("dve", lambda e, s=s, xn=xn: e.tensor_scalar(out=xn, in0=x1[:, s, :], scalar1=stt[:, 32:33], scalar2=stt[:, 35:36],
                                                               op0=ALU.subtract, op1=ALU.mult), r=[("x1", s), "stt_mv", "stt_rs2"], w=xnk)
                    P.op("pool", lambda e, s=s, xn=xn: e.tensor_tensor(out=x1[:, s, :], in0=xn, in1=lnv[0][:], op=ALU.mult), r=xnk + [("lnv", 0)], w=[("x1", s)])
                    P.op("pool", lambda e, s=s: e.tensor_tensor(out=x1[:, s, :], in0=x1[:, s, :], in1=lnv[1][:], op=ALU.add), r=[("x1", s), ("lnv", 1)], w=[("x1", s)])

                def s7_back(s):
                    xn, xnk = xns[s % 2]
                    for kg in range(2):
                        b = next_bank()
                        for kk in range(4):
                            k = kg * 4 + kk
                            P.op("pe", lambda e, b=b, kk=kk, k=k, xn=xn: e.transpose(bank(b)[:, kk * 128:(kk + 1) * 128], xn[:, k * 128:(k + 1) * 128], ident[:]),
                                 r=xnk + ["ident"], w=[bkey(b)])
                        for kk in range(4):
                            k = kg * 4 + kk
                            out = hT_all[:, k, s * 128:(s + 1) * 128]
                            src = bank(b)[:, kk * 128:(kk + 1) * 128]
                            P.op("act", lambda e, out=out, src=src, k=k: e.activation(out=out, in_=src, func=AF.Identity,
                                                                                   bias=cols[:, j, 3, k:k + 1], scale=cols[:, j, 2, k:k + 1]),
                                 r=[bkey(b), "cols"], w=hkeys(k, s))

                for s in range(5):
                    if s < 4:
                        s7_front(s)
                    if s >= 1:
                        s7_back(s - 1)
                ring_rel(wo0i)
                ring_rel(wo1i)
                _ck("s7")
                for pi in range(11):
                    wt, wk, wi_ = ring_next()
                    wv = wt[:, 0:4096].rearrange("p (k n) -> p k n", k=8)
                    for pp in range(2):
                        fi = 2 * pi + pp
                        bG = next_bank()
                        mm_fm(bG, wv, (2 * pp) * 128, lambda k: hT_all[:, k, :], 8, lambda k: [wk] + hkeys(k))
                        bU = next_bank()
                        mm_fm(bU, wv, (2 * pp + 1) * 128, lambda k: hT_all[:, k, :], 8, lambda k: [wk] + hkeys(k))
                        tg, tgk = tgm(fi % 2)
                        ww, wwk = wm(fi % 2)
                        P.op("act", lambda e, bG=bG, tg=tg: e.activation(out=tg, in_=bank(bG), func=AF.Tanh, scale=0.5), r=[bkey(bG)], w=tgk)
                        P.op("dve", lambda e, bG=bG, tg=tg, ww=ww: e.scalar_tensor_tensor(out=ww, in0=tg, scalar=1.0, in1=bank(bG), op0=ALU.add, op1=ALU.mult),
                             r=tgk + [bkey(bG)], w=wwk)
                        ao, aok = actT(fi)
                        P.op("dve", lambda e, bU=bU, ww=ww, ao=ao: e.tensor_tensor(out=ao, in0=ww, in1=bank(bU), op=ALU.mult), r=wwk + [bkey(bU)], w=aok)
                    ring_rel(wi_)
                _ck("s8")
                if t + 1 < nqt:
                    prefetch_x(j, t + 1)
                P.op("sp", lambda e: e.dma_start(out=lnv[0][:], in_=lnrow_d[2:3, :].partition_broadcast(128)), w=[("lnv", 0)], chan=("lnv", 0))
                P.op("sp", lambda e: e.dma_start(out=lnv[1][:], in_=lnrow_d[3:4, :].partition_broadcast(128)), w=[("lnv", 1)], chan=("lnv", 1))
                for half in range(2):
                    bs = [4 * half + s for s in range(4)]
                    for kp in range(2):
                        wt, wk, wi_ = ring_next()
                        wv = wt[:, 0:5632].rearrange("p (k n) -> p k n", k=11)
                        for s in range(4):
                            for kk in range(11):
                                f = kp * 11 + kk
                                P.op("pe", lambda e, s=s, kk=kk, f=f, wv=wv, bs=bs: e.matmul(bank(bs[s]), lhsT=arA[:, f * 512 + s * 128:f * 512 + (s + 1) * 128],
                                                                                      rhs=wv[:, kk, :], start=(f == 0), stop=(f == NFF - 1)),
                                     r=[wk, ("A", f)], w=[bkey(bs[s])])
                        ring_rel(wi_)
                    for s in range(4):
                        m1, m1k = m1_
                        P.op("dve", lambda e, s=s, half=half, bs=bs: e.tensor_tensor(out=m1, in0=bank(bs[s]), in1=gf[:, half * 512:(half + 1) * 512], op=ALU.mult),
                             r=[bkey(bs[s]), "gf"], w=m1k)
                        P.op("dve", lambda e, s=s, half=half: e.scalar_tensor_tensor(out=x1[:, s, half * 512:(half + 1) * 512], in0=x1[:, s, half * 512:(half + 1) * 512],
                                                                                      scalar=ALPHA, in1=m1, op0=ALU.mult, op1=ALU.add), r=[("x1", s)] + m1k, w=[("x1", s)])
                for s in range(4):
                    for hh in range(2):
                        P.op("dve", lambda e, s=s, hh=hh: e.bn_stats(out=bst[:, hh, :], in_=x1[:, s, hh * 512:(hh + 1) * 512]), r=[("x1", s)], w=["bst"])
                    P.op("dve", lambda e: e.bn_aggr(out=stt[:, 32:34], in_=bst[:]), r=["bst"], w=["stt_mv"])
                    P.op("pool", lambda e: e.tensor_scalar_add(out=stt[:, 34:35], in0=stt[:, 33:34], scalar1=EPS), r=["stt_mv"], w=["stt_ve2"])
                    P.op("pool", lambda e: e.tensor_tensor(out=stt[:, 35:36], in0=stt[:, 34:35], in1=mhalf[:, 0:1], op=ALU.pow), r=["stt_ve2", "mhalf"], w=["stt_rs2"])
                    P.op("dve", lambda e, s=s: e.tensor_scalar(out=x1[:, s, :], in0=x1[:, s, :], scalar1=stt[:, 32:33], scalar2=stt[:, 35:36],
                                                               op0=ALU.subtract, op1=ALU.mult), r=[("x1", s), "stt_mv", "stt_rs2"], w=[("x1", s)])
                    P.op("pool", lambda e, s=s: e.tensor_tensor(out=x1[:, s, :], in0=x1[:, s, :], in1=lnv[0][:], op=ALU.mult), r=[("x1", s), ("lnv", 0)], w=[("x1", s)])
                    P.op("pool", lambda e, s=s: e.tensor_tensor(out=x1[:, s, :], in0=x1[:, s, :], in1=lnv[1][:], op=ALU.add), r=[("x1", s), ("lnv", 1)], w=[("x1", s)])
                    P.op("pool", lambda e, s=s: e.dma_start(out=yj[j][t * 512 + s * 128:t * 512 + (s + 1) * 128, :], in_=x1[:, s, :]),
                         r=[("x1", s)], chan=("st", s))
        try:
            for j in range(NJ):
                job_body(j)
        except _Stop:
            pass
        P.emit(nc, st, final_wait_chans=[("st", s) for s in range(4)])
    return nc


def _tile_w(w, kc, ncols):
    K, N = w.shape
    return np.ascontiguousarray(w.reshape(K // 128, 128, N // ncols, ncols).transpose(2, 1, 0, 3))


def _rope_tables(S, reverse=False):
    inv = 1.0 / (10000.0 ** (np.arange(0, 64, 2, dtype=np.float32) / 64.0))
    pos = np.arange(S, dtype=np.float32)
    if reverse:
        pos = pos[::-1]
    ang = pos[None, :].astype(np.float32) * inv[:, None].astype(np.float32)
    ang = ang.astype(np.float32)
    idx = (np.arange(128) % 64) % 32
    return np.ascontiguousarray(np.stack([np.cos(ang)[idx], np.sin(ang)[idx]], 0).astype(np.float32))


def _const_inputs():
    ident = np.eye(128, dtype=np.float32)
    pt = np.zeros((128, 128), np.float32)
    for m in range(128):
        if (m % 64) < 32:
            pt[m + 32, m] = -1.0
        else:
            pt[m - 32, m] = 1.0
    ki = np.arange(128)[:, None]
    qi = np.arange(128)[None, :]
    masks = np.stack([(qi <= ki), (ki <= qi)], 1).astype(np.float32)
    return dict(identc=ident, ptm=pt, maskc=np.ascontiguousarray(masks), masknc=np.ascontiguousarray((masks - 1.0) * 30000.0))


def _weight_inputs(w_ada, b_ada, w_in, sink_logit, lam_q1, lam_k1, lam_q2, lam_k2, subln_w, w_a, w_b, w_o,
                   ln1_g, ln1_b, w_gu, w_down, ln2_g, ln2_b):
    w_in = w_in[0]
    qa, ka, va = w_in[:, 0:512], w_in[:, 512:640], w_in[:, 640:768]
    qd, kd, vd = w_in[:, 768:1280], w_in[:, 1280:1792], w_in[:, 1792:2304]
    ga, gb = w_in[:, 2304:3328], w_in[:, 3328:4352]
    perm = np.concatenate([np.concatenate([np.arange(c * 64, c * 64 + 64), np.arange((c + 4) * 64, (c + 4) * 64 + 64)]) for c in range(4)])
    qa_p = qa[:, perm]
    win1 = np.stack([_tile_w(np.concatenate([kd, ka], 1), 8, 640)[0], _tile_w(np.concatenate([vd, va], 1), 8, 640)[0]], 0)
    win2 = np.concatenate([_tile_w(qd, 8, 512), _tile_w(qa_p, 8, 512)], 0)
    wa_p = w_a[0][perm, :]
    wab = _tile_w(np.concatenate([np.concatenate([wa_p[:, q * 256:(q + 1) * 256], w_b[0][:, q * 256:(q + 1) * 256]], 1) for q in range(4)], 1), 4, 512)
    wgab = _tile_w(np.concatenate([np.concatenate([ga[:, q * 256:(q + 1) * 256], gb[:, q * 256:(q + 1) * 256]], 1) for q in range(4)], 1), 8, 512)
    wo = _tile_w(w_o[0], 8, 512)
    g, u = w_gu[0][:, :DFF], w_gu[0][:, DFF:]
    gu = np.concatenate([np.concatenate([g[:, f * 128:(f + 1) * 128], u[:, f * 128:(f + 1) * 128]], 1) for f in range(NFF)], 1)
    wgu = _tile_w(gu, 8, 512)
    wd = w_down[0]
    wdn = np.stack([np.ascontiguousarray(wd[kp * 1408:(kp + 1) * 1408, half * 512:(half + 1) * 512].reshape(11, 128, 512).transpose(1, 0, 2))
                    for half in range(2) for kp in range(2)], 0)
    wada = _tile_w(w_ada[0], 8, 256)
    bcol = np.ascontiguousarray(b_ada[0].reshape(48, 128).T)
    lamv = np.concatenate([lam_q1[0], lam_k1[0], lam_q2[0], lam_k2[0]])[None, :]
    lnrow = np.stack([ln1_g[0], ln1_b[0], ln2_g[0], ln2_b[0]], 0)
    lncol = np.ascontiguousarray(np.stack([ln1_g[0].reshape(8, 128).T, ln1_b[0].reshape(8, 128).T], 1))
    f = lambda a: np.ascontiguousarray(a, dtype=np.float32)
    return dict(win1=f(win1), win2=f(win2), wab=f(wab), wgab=f(wgab), wo=f(wo), wgu=f(wgu), wdn=f(wdn), wada=f(wada), bcolin=f(bcol),
                sink=f(sink_logit), lamv=f(lamv), subwin=f(subln_w), subwcol=f(subln_w[0][:, None]), lnrow=f(lnrow), lncolin=f(lncol))


_NC_CACHE = {}


def run_jobs(core_jobs, jobs_cfg, weights):
    key = tuple(jobs_cfg)
    if key not in _NC_CACHE:
        _NC_CACHE[key] = build(list(jobs_cfg))
    nc = _NC_CACHE[key]
    shared = dict(weights)
    shared.update(_const_inputs())
    in_maps = []
    for cj in core_jobs:
        m = dict(shared)
        cs_ = np.stack([c for (_, c, _) in cj], 0)
        m["cTin"] = np.ascontiguousarray(cs_.reshape(len(cj), 8, 128).transpose(2, 1, 0)).astype(np.float32)
        for j, (x, c, rev) in enumerate(cj):
            m[f"xin{j}"] = np.ascontiguousarray(x, dtype=np.float32)
            m[f"rope{j}"] = _rope_tables(x.shape[0], rev)
        in_maps.append(m)
    res = run_bass_kernel_spmd(nc, in_maps, core_ids=list(range(len(core_jobs))))
    return [[r[f"yout{j}"] for j in range(len(jobs_cfg))] for r in res.results]


def kernel(x_prompt, x_sample, c_prompt, c_sample, w_ada, b_ada, w_in, sink_logit, lam_q1, lam_k1, lam_q2, lam_k2,
           subln_w, w_a, w_b, w_o, ln1_g, ln1_b, w_gu, w_down, ln2_g, ln2_b):
    a = lambda v: np.asarray(v)
    weights = _weight_inputs(a(w_ada), a(b_ada), a(w_in), a(sink_logit), a(lam_q1), a(lam_k1), a(lam_q2), a(lam_k2), a(subln_w),
                             a(w_a), a(w_b), a(w_o), a(ln1_g), a(ln1_b), a(w_gu), a(w_down), a(ln2_g), a(ln2_b))
    xs_ = [a(x_prompt)[i] for i in range(16)] + [a(x_sample)[i] for i in range(4)]
    cs_ = [a(c_prompt)[i] for i in range(16)] + [a(c_sample)[i] for i in range(4)]
    core_jobs = []
    plan = []
    for c in range(8):
        halves = list(range(5 * c, 5 * c + 5))
        seqs = sorted(set(h // 2 for h in halves))
        full = [s for s in seqs if (2 * s in halves and 2 * s + 1 in halves)]
        part = [h for h in halves if (h // 2) not in full]
        assert len(full) == 2 and len(part) == 1
        hp = part[0]
        rev = (hp % 2 == 1)
        sp_ = hp // 2
        xp = xs_[sp_][::-1] if rev else xs_[sp_]
        core_jobs.append([(xs_[full[0]], cs_[full[0]], False), (xs_[full[1]], cs_[full[1]], False), (xp, cs_[sp_], rev)])
        plan.append((full, sp_, rev))
    outs = run_jobs(core_jobs, ((8, 8), (8, 8), (8, 4)), weights)
    y = np.zeros((20, 4096, D), np.float32)
    for c in range(8):
        full, sp_, rev = plan[c]
        y[full[0]] = outs[c][0]
        y[full[1]] = outs[c][1]
        if rev:
            y[sp_, 2048:] = outs[c][2][::-1]
        else:
            y[sp_, :2048] = outs[c][2]
    return (y[:16], y[16:])
```

```python
from contextlib import ExitStack
import types
import numpy as np
import concourse.bass as bass
import concourse.mybir as mybir
from concourse.bass_utils import run_bass_kernel_spmd

F32 = mybir.dt.float32
BF16 = mybir.dt.bfloat16
AF = mybir.ActivationFunctionType
ALU = mybir.AluOpType

D = 1024
DFF = 2816
NFF = 22
ALPHA = 2.0 ** 0.25
LAM_INIT = 0.2
EPS = 1e-5


def _freeze(fn):
    if fn.__closure__ is None:
        return fn
    cells = []
    for c in fn.__closure__:
        try:
            cells.append(types.CellType(c.cell_contents))
        except ValueError:
            cells.append(c)
    return types.FunctionType(fn.__code__, fn.__globals__, fn.__name__, fn.__defaults__, tuple(cells))


class Prog:
    ENG = ("pe", "act", "dve", "pool", "sp")

    def __init__(self):
        self.q = {e: [] for e in self.ENG}
        self.last_w = {}
        self.readers = {}
        self.chans = []

    def op(self, eng, fn, r=(), w=(), chan=None):
        idx = len(self.q[eng])
        deps = set()
        for b in r:
            lw = self.last_w.get(b)
            if lw is not None:
                deps.add(lw)
        for b in w:
            lw = self.last_w.get(b)
            if lw is not None:
                deps.add(lw)
            rd = self.readers.get(b)
            if rd:
                deps.update(rd.values())
        if chan is not None and chan not in self.chans:
            self.chans.append(chan)
        self.q[eng].append(dict(fn=_freeze(fn), deps=deps, chan=chan, signal=False, ticket=None))
        me = (eng, idx)
        for b in r:
            key = eng if chan is None else ("dma", chan)
            self.readers.setdefault(b, {})[key] = me
        for b in w:
            self.last_w[b] = me
            self.readers[b] = {}
        return me

    def emit(self, nc, stack, final_wait_chans=()):
        q = self.q
        for e in self.ENG:
            for ins in q[e]:
                for (e2, i2) in ins["deps"]:
                    q[e2][i2]["signal"] = True
        esem = {e: stack.enter_context(nc.semaphore("s_" + e)) for e in self.ENG if e != "sp"}
        csem = {c: stack.enter_context(nc.semaphore("c_" + str(c))) for c in self.chans}
        ccount = {c: 0 for c in self.chans}
        for e in self.ENG:
            cnt = 0
            for ins in q[e]:
                if ins["chan"] is not None:
                    ccount[ins["chan"]] += 16
                    ins["ticket"] = ccount[ins["chan"]]
                elif ins["signal"]:
                    cnt += 1
                    ins["ticket"] = cnt
        block = stack.enter_context(nc.Block())

        def run(e, eng):
            waited = {}
            for ins in q[e]:
                need = {}
                for (e2, i2) in ins["deps"]:
                    d = q[e2][i2]
                    if d["chan"] is not None:
                        s = csem[d["chan"]]
                        key = ("c", d["chan"])
                    else:
                        if e2 == e and e == "pe":
                            continue
                        s = esem[e2]
                        key = ("e", e2)
                    v = d["ticket"]
                    if v > need.get(key, (None, 0))[1]:
                        need[key] = (s, v)
                for key, (s, v) in need.items():
                    if waited.get(key, 0) >= v:
                        continue
                    eng.wait_ge(s, v)
                    waited[key] = v
                r = ins["fn"](eng)
                if ins["chan"] is not None:
                    r.then_inc(csem[ins["chan"]], 16)
                elif ins["signal"]:
                    r.then_inc(esem[e], 1)
            if e == "pool":
                for c in final_wait_chans:
                    if ccount.get(c, 0) > 0:
                        eng.wait_ge(csem[c], ccount[c])

        @block.tensor
        def _(eng):
            run("pe", eng)

        @block.scalar
        def _(eng):
            run("act", eng)

        @block.vector
        def _(eng):
            run("dve", eng)

        @block.gpsimd
        def _(eng):
            run("pool", eng)

        @block.sync
        def _(eng):
            run("sp", eng)


class _Stop(Exception):
    pass


def _ck(name):
    return None


def build(jobs):
    NJ = len(jobs)
    nc = bass.Bass("TRN2", target_bir_lowering=False)
    P = Prog()

    def din(name, shape, dtype=F32):
        return nc.dram_tensor(name, shape, dtype, kind="ExternalInput").ap()

    def dscr(name, shape, dtype=BF16):
        return nc.dram_tensor(name, shape, dtype, kind="Internal").ap()

    xj = [din(f"xin{j}", [jobs[j][0] * 512, D]) for j in range(NJ)]
    ropej = [din(f"rope{j}", [2, 128, jobs[j][0] * 512]) for j in range(NJ)]
    yj = [nc.dram_tensor(f"yout{j}", [jobs[j][1] * 512, D], F32, kind="ExternalOutput").ap() for j in range(NJ)]
    cT_d = din("cTin", [128, 8, NJ])
    wada_d = din("wada", [24, 128, 8, 256])
    bcol_d = din("bcolin", [128, 48])
    wshapes = dict(win1=[2, 128, 8, 640], win2=[2, 128, 8, 512], wab=[4, 128, 4, 512], wgab=[4, 128, 8, 512],
                   wo=[2, 128, 8, 512], wgu=[11, 128, 8, 512], wdn=[4, 128, 11, 512])
    wf = {k: din(k, v) for k, v in wshapes.items()}
    wb16 = {k: dscr(k + "_b", v) for k, v in wshapes.items()}
    ident_d = din("identc", [128, 128])
    pt_d = din("ptm", [128, 128])
    mask_d = din("maskc", [128, 2, 128])
    maskn_d = din("masknc", [128, 2, 128])
    sink_d = din("sink", [1, 8])
    lamv_d = din("lamv", [1, 256])
    subw_d = din("subwin", [1, 128])
    subwc_d = din("subwcol", [128, 1])
    lnrow_d = din("lnrow", [4, D])
    lncol_d = din("lncolin", [128, 2, 8])

    NKT = max(j[0] for j in jobs)
    SM = NKT * 512
    NBM = NKT * 4

    with ExitStack() as st:
        def sb(name, shape, dtype):
            return st.enter_context(nc.sbuf_tensor(name, shape, dtype))

        KdT = sb("KdT", [128, 4, SM], BF16)
        Vd = sb("Vd", [128, NBM, 516], BF16)
        KaT = sb("KaT", [128, SM], BF16)
        Va = sb("Va", [128, NBM, 192], BF16)
        NRING = 3
        ring = [sb(f"ring{i}", [128, 5632], BF16) for i in range(NRING)]
        x1 = sb("x1", [128, 4, D], F32)
        xs = [sb(f"xs{i}", [128, D], F32) for i in range(2)]
        cs = sb("cs", [128, 2, 512], F32)
        lnv = [sb(f"lnv{i}", [128, D], F32) for i in range(2)]
        lamt = lnv[0][:, 0:256]
        gm = sb("gm", [128, D], BF16)
        gf = sb("gf", [128, D], BF16)
        arA = sb("arA", [128, 11264], BF16)
        arB = sb("arB", [128, 4096], BF16)
        arC = sb("arC", [128, 4096], BF16)
        arD = sb("arD", [128, 4096], BF16)
        ident = sb("ident", [128, 128], F32)
        ones = sb("ones", [128, 128], F32)
        PT = sb("PT", [128, 128], BF16)
        esink = sb("esink", [128, 8], F32)
        lsm = sb("lsm", [128, 8], F32)
        mhalf = sb("mhalf", [128, 16], F32)
        cT = sb("cT", [128, 8, NJ], F32)
        siluT = sb("siluT", [128, 8, NJ], F32)
        modT = sb("modT", [128, 48, NJ], F32)
        bcol = sb("bcol", [128, 48], F32)
        lncol = sb("lncol", [128, 2, 8], F32)
        cols = sb("cols", [128, NJ, 4, 8], F32)
        onesP = sb("onesP", [128, 192], BF16)
        esink2 = sb("esink2", [128, 4], F32)
        ones_bf = sb("ones_bf", [128, 128], BF16)
        ident_bf = sb("ident_bf", [128, 128], BF16)
        maskn = sb("maskn", [128, 2, 128], BF16)
        subwc = sb("subwc", [128, 1], F32)
        stt = sb("stt", [128, 64], F32)
        bst = sb("bst", [128, 2, 6], F32)
        psum = [st.enter_context(nc.psum_tensor(f"ps{i}", [128, 2, 512], F32)) for i in range(4)]

        def bank(b):
            return psum[b // 2][:, b % 2, :]

        def bkey(b):
            return f"ps{b}"

        def pg(ar, name, p0, n, dtype=BF16):
            ap = ar[:, p0 * 512:(p0 + n) * 512]
            if dtype == F32:
                ap = ap.bitcast(F32)
            return ap, [(name, p) for p in range(p0, p0 + n)]

        actT = lambda f: pg(arA, "A", f, 1)
        QdT = lambda c: pg(arA, "A", c, 1)
        QaT_all = arA[:, 4 * 512:8 * 512].rearrange("p (c t) -> p c t", c=4)
        QaT = lambda c: pg(arA, "A", 4 + c, 1)
        Oatmp = pg(arA, "A", 8, 2, F32)
        Odtmp = pg(arA, "A", 10, 2, F32)
        OaT = lambda k: pg(arA, "A", 12 + k, 1)
        OdT = lambda k: pg(arA, "A", 16 + k, 1)
        hT_all = arB[:, :].rearrange("p (k t) -> p k t", k=8)
        hkeys = lambda k, s=None: [("B", k, ss) for ss in (range(4) if s is None else [s])]
        mergedT = lambda oc: pg(arC, "C", oc, 1)
        Ed = lambda i: pg(arC, "C", 2 * i, 2)
        Ew = lambda i: pg(arC, "C", 6 + i, 1)
        tgm = lambda i: pg(arC, "C", 2 * i, 2, F32)
        wm = lambda i: pg(arC, "C", 4 + 2 * i, 2, F32)
        qb_ = pg(arD, "D", 0, 1)
        t1_ = pg(arD, "D", 1, 2, F32)
        t2_ = pg(arD, "D", 3, 2, F32)
        ta_ = pg(arD, "D", 0, 2, F32)
        tb_ = pg(arD, "D", 2, 2, F32)
        m1_ = pg(arD, "D", 4, 2, F32)
        m2_ = pg(arD, "D", 6, 2, F32)
        tmp_ = pg(arD, "D", 0, 4, F32)
        xn_ = pg(arD, "D", 4, 4, F32)

        sched = []
        ring_state = dict(issued=0, used=0)
        released = set()

        def ring_pump():
            while ring_state["issued"] < len(sched):
                n = ring_state["issued"]
                if n >= NRING and (n - NRING) not in released:
                    break
                slot = n % NRING
                wname, wi = sched[n]
                src = wb16[wname][wi].rearrange("p k n -> p (k n)")
                ne = src.shape[1]
                P.op("sp", lambda e, slot=slot, src=src, ne=ne: e.dma_start(out=ring[slot][:, 0:ne], in_=src),
                     r=[("scr", wname, wi)], w=[("ring", slot)], chan=("ring", slot))
                ring_state["issued"] += 1

        def ring_next():
            i = ring_state["used"]
            ring_state["used"] += 1
            ring_pump()
            assert ring_state["issued"] > i, "ring deadlock: too many pieces held"
            return ring[i % NRING], ("ring", i % NRING), i

        def ring_rel(i):
            released.add(i)
            ring_pump()

        def piece(name, i):
            return (name, i)

        for j in range(NJ):
            nkt, nqt = jobs[j]
            sched += [piece("win1", 0), piece("win1", 1)]
            for t in range(nqt):
                sched += [piece("win2", 0), piece("win2", 1)]
                for q in range(4):
                    sched += [piece("wab", q), piece("wgab", q)]
                sched += [piece("wo", 0), piece("wo", 1)]
                sched += [piece("wgu", i) for i in range(11)]
                sched += [piece("wdn", i) for i in range(4)]

        def cast_weights(names, extra_r=()):
            for k in names:
                for i in range(wshapes[k][0]):
                    P.op("pool", lambda e, k=k, i=i: e.dma_start(out=wb16[k][i], in_=wf[k][i]), r=list(extra_r), w=[("scr", k, i)], chan=("cast", k, i))
        cast_weights(("win1", "win2", "wab", "wgab", "wo", "wgu", "wdn"))
        cnum = dict(n=0)

        def ld(out, in_, keys):
            cnum["n"] += 1
            P.op("sp", lambda e: e.dma_start(out=out, in_=in_), w=keys, chan=("const", cnum["n"]))
        ld(ident[:], ident_d, ["ident"])
        ld(cT[:], cT_d, ["cT"])
        ld(bcol[:], bcol_d, ["bcol"])
        ld(lncol[:], lncol_d, ["lncol"])
        ld(esink[:], sink_d.partition_broadcast(128), ["esink"])
        ld(lamt, lamv_d.partition_broadcast(128), [("lnv", 0)])
        ld(subwc[:], subwc_d, ["subwc"])
        P.op("pool", lambda e: e.dma_start(out=PT[:], in_=pt_d), w=["PT"], chan=("constc", 0))
        P.op("pool", lambda e: e.dma_start(out=ident_bf[:], in_=ident_d), w=["ident_bf"], chan=("constc", 1))
        P.op("pool", lambda e: e.dma_start(out=maskn[:], in_=maskn_d), w=["maskn"], chan=("constc", 2))
        P.op("pool", lambda e: e.memset(ones[:], 1.0), w=["ones"])
        P.op("pool", lambda e: e.memset(onesP[:], 1.0), w=["onesP"])
        P.op("pool", lambda e: e.memset(onesP[:, 64:128], 0.0), r=["onesP"], w=["onesP"])
        P.op("pool", lambda e: e.memset(ones_bf[:], 1.0), w=["ones_bf"])
        P.op("pool", lambda e: e.memset(mhalf[:], -0.5), w=["mhalf"])
        P.op("pool", lambda e: e.memset(Vd[:], 1.0), w=["Vd"])
        P.op("pool", lambda e: e.memset(Va[:], 0.0), w=["Va"])
        P.op("act", lambda e: e.activation(out=esink[:], in_=esink[:], func=AF.Exp), r=["esink"], w=["esink"])
        P.op("dve", lambda e: e.tensor_copy(out=esink2[0:64, :], in_=esink[0:64, 0:4]), r=["esink"], w=["esink2"])
        P.op("dve", lambda e: e.tensor_copy(out=esink2[64:128, :], in_=esink[64:128, 4:8]), r=["esink", "esink2"], w=["esink2"])
        P.op("dve", lambda e: e.tensor_tensor(out=lamt[:, 0:64], in0=lamt[:, 0:64], in1=lamt[:, 64:128], op=ALU.mult), r=[("lnv", 0)], w=[("lnv", 0)])
        P.op("dve", lambda e: e.tensor_tensor(out=lamt[:, 128:192], in0=lamt[:, 128:192], in1=lamt[:, 192:256], op=ALU.mult), r=[("lnv", 0)], w=[("lnv", 0)])
        P.op("act", lambda e: e.activation(out=lamt[:, 64:128], in_=lamt[:, 0:64], func=AF.Identity, accum_out=lsm[:, 0:1]), r=[("lnv", 0)], w=[("lnv", 0), "lsm"])
        P.op("act", lambda e: e.activation(out=lamt[:, 192:256], in_=lamt[:, 128:192], func=AF.Identity, accum_out=lsm[:, 1:2]), r=[("lnv", 0)], w=[("lnv", 0), "lsm"])
        P.op("act", lambda e: e.activation(out=lsm[:, 2:4], in_=lsm[:, 0:2], func=AF.Exp), r=["lsm"], w=["lsm"])
        P.op("dve", lambda e: e.tensor_tensor(out=lsm[:, 4:5], in0=lsm[:, 3:4], in1=lsm[:, 2:3], op=ALU.subtract), r=["lsm"], w=["lsm"])
        P.op("dve", lambda e: e.tensor_scalar_add(out=lsm[:, 5:6], in0=lsm[:, 4:5], scalar1=-LAM_INIT), r=["lsm"], w=["lsm"])
        neglam = lsm[:, 5:6]
        P.op("dve", lambda e: e.tensor_scalar_mul(out=subwc[:], in0=subwc[:], scalar1=1.0 - LAM_INIT), r=["subwc"], w=["subwc"])
        P.op("act", lambda e: e.activation(out=siluT[:], in_=cT[:], func=AF.Tanh, scale=0.5), r=["cT"], w=["siluT"])
        P.op("dve", lambda e: e.scalar_tensor_tensor(out=siluT[:], in0=siluT[:], scalar=1.0, in1=cT[:], op0=ALU.add, op1=ALU.mult),
             r=["siluT", "cT"], w=["siluT"])
        P.op("dve", lambda e: e.tensor_scalar_mul(out=siluT[:], in0=siluT[:], scalar1=0.5), r=["siluT"], w=["siluT"])
        wstage = [arA[:, i * 4096:(i + 1) * 4096].bitcast(F32).rearrange("p (k n) -> p k n", k=8) for i in range(2)]
        wstk = [[("A", p) for p in range(8 * i, 8 * i + 8)] for i in range(2)]
        mps = psum[0][:, 0, 0:48 * NJ].rearrange("p (c j) -> p c j", j=NJ)
        for pc in range(24):
            i = pc % 2
            P.op("sp", lambda e, pc=pc, i=i: e.dma_start(out=wstage[i], in_=wada_d[pc]), w=wstk[i], chan=("wst", i))
            for cc in range(2):
                ch = pc * 2 + cc
                for kk in range(8):
                    P.op("pe", lambda e, i=i, cc=cc, kk=kk, ch=ch: e.matmul(mps[:, ch, :], lhsT=wstage[i][:, kk, cc * 128:(cc + 1) * 128],
                                                                          rhs=siluT[:, kk, :], start=(kk == 0), stop=(kk == 7)),
                         r=wstk[i] + ["siluT"], w=[bkey(0)])
        for j in range(NJ):
            P.op("dve", lambda e, j=j: e.tensor_tensor(out=modT[:, :, j], in0=mps[:, :, j], in1=bcol[:], op=ALU.add),
                 r=[bkey(0), "bcol"], w=["modT"])
        for j in range(NJ):
            P.op("dve", lambda e, j=j: e.tensor_scalar_add(out=cols[:, j, 0, :], in0=modT[:, 8:16, j], scalar1=1.0), r=["modT"], w=["cols"])
            P.op("dve", lambda e, j=j: e.tensor_copy(out=cols[:, j, 1, :], in_=modT[:, 0:8, j]), r=["modT"], w=["cols"])
            P.op("dve", lambda e, j=j: e.scalar_tensor_tensor(out=cols[:, j, 2, :], in0=modT[:, 32:40, j], scalar=1.0, in1=lncol[:, 0, :],
                                                              op0=ALU.add, op1=ALU.mult), r=["modT", "lncol"], w=["cols"])
            P.op("dve", lambda e, j=j: e.scalar_tensor_tensor(out=cols[:, j, 3, :], in0=modT[:, 32:40, j], scalar=1.0, in1=lncol[:, 1, :],
                                                              op0=ALU.add, op1=ALU.mult), r=["modT", "lncol"], w=["cols"])
            P.op("dve", lambda e, j=j: e.tensor_tensor(out=cols[:, j, 3, :], in0=cols[:, j, 3, :], in1=modT[:, 24:32, j], op=ALU.add),
                 r=["modT", "cols"], w=["cols"])

        rr = dict(ps=0)
        rope_rr = dict(n=0)
        acc_rr = dict(n=0)

        def next_bank(lo=0, hi=8):
            b = lo + rr["ps"] % (hi - lo)
            rr["ps"] += 1
            return b

        alt = dict(n=0)

        def evac_eng():
            alt["n"] += 1
            return "act" if alt["n"] % 2 else "dve"

        def load_x_sub(j, row0, slot):
            P.op("sp", lambda e: e.dma_start(out=xs[slot][:], in_=xj[j][row0:row0 + 128, :]), w=[("xs", slot)], chan=("xs", slot))

        xs_rr = dict(n=0)
        xpre = {}

        def prefetch_x(j, t):
            for s in range(2):
                slot = xs_rr["n"] % 2
                xs_rr["n"] += 1
                load_x_sub(j, t * 512 + s * 128, slot)
                xpre[(j, t, s)] = slot

        def make_hT(j, t):
            for s in range(4):
                if (j, t, s) in xpre:
                    slot = xpre.pop((j, t, s))
                else:
                    slot = xs_rr["n"] % 2
                    xs_rr["n"] += 1
                    load_x_sub(j, t * 512 + s * 128, slot)
                _ck("p1_x")
                for kg in range(2):
                    b = next_bank()
                    for kk in range(4):
                        k = kg * 4 + kk
                        if kk == 1:
                            _ck("p1_tr")
                        P.op("pe", lambda e, b=b, kk=kk, k=k, slot=slot: e.transpose(bank(b)[:, kk * 128:(kk + 1) * 128],
                                                                                 xs[slot][:, k * 128:(k + 1) * 128], ident[:]),
                             r=[("xs", slot), "ident"], w=[bkey(b)])
                    for kk in range(4):
                        k = kg * 4 + kk
                        if kk == 1:
                            _ck("p1_ev1")
                        if kk == 2:
                            _ck("p1_ev2")
                        eng = "act" if kg == 0 else "dve"
                        out = hT_all[:, k, s * 128:(s + 1) * 128]
                        src = bank(b)[:, kk * 128:(kk + 1) * 128]
                        if eng == "act":
                            P.op("act", lambda e, out=out, src=src, k=k: e.activation(out=out, in_=src, func=AF.Identity,
                                                                                   bias=cols[:, j, 1, k:k + 1], scale=cols[:, j, 0, k:k + 1]),
                                 r=[bkey(b), "cols"], w=hkeys(k, s))
                        else:
                            P.op("dve", lambda e, out=out, src=src, k=k: e.tensor_scalar(out=out, in0=src, scalar1=cols[:, j, 0, k:k + 1],
                                                                                      scalar2=cols[:, j, 1, k:k + 1], op0=ALU.mult, op1=ALU.add),
                                 r=[bkey(b), "cols"], w=hkeys(k, s))

        def load_cs(j, t):
            P.op("sp", lambda e: e.dma_start(out=cs[:], in_=ropej[j][:, :, t * 512:(t + 1) * 512].rearrange("c p t -> p c t")),
                 w=["cs"], chan="cs")

        def mm_fm(b, wv, col0, rhs_fn, nk, rkeys):
            for k in range(nk):
                P.op("pe", lambda e, k=k: e.matmul(bank(b), lhsT=wv[:, k, col0:col0 + 128], rhs=rhs_fn(k), start=(k == 0), stop=(k == nk - 1)),
                     r=rkeys(k), w=[bkey(b)])

        rope_pend = []

        def rope_chunk(b, out_ap, out_keys):
            rope_rr["n"] += 1
            odd = rope_rr["n"] % 2
            if odd:
                qb, qbk = qb_
                t1, t1k = t1_
                t2, t2k = t2_
            else:
                qb, qbk = pg(arC, "C", 0, 1)
                t1, t1k = pg(arC, "C", 1, 2, F32)
                t2, t2k = pg(arC, "C", 3, 2, F32)
            P.op("act", lambda e: e.activation(out=qb, in_=bank(b), func=AF.Copy), r=[bkey(b)], w=qbk)

            def tail():
                b2 = next_bank()
                P.op("pe", lambda e: e.matmul(bank(b2), lhsT=PT[:], rhs=qb, start=True, stop=True), r=qbk + ["PT"], w=[bkey(b2)])
                P.op("dve", lambda e: e.tensor_tensor(out=t1, in0=bank(b), in1=cs[:, 0, :], op=ALU.mult), r=[bkey(b), "cs"] + qbk, w=t1k)
                P.op("dve", lambda e: e.tensor_tensor(out=t2, in0=bank(b2), in1=cs[:, 1, :], op=ALU.mult), r=[bkey(b2), "cs"], w=t2k)
                P.op("pool" if odd else "dve", lambda e: e.tensor_tensor(out=out_ap, in0=t1, in1=t2, op=ALU.add), r=t1k + t2k, w=out_keys)
            rope_flush()
            rope_pend.append(tail)

        def rope_flush():
            while rope_pend:
                rope_pend.pop(0)()

        def job_body(j):
            nkt, nqt = jobs[j]
            NB = nkt * 4
            _ck("prologue")
            for (v0, gt, gkey) in ((16, gm, "gm"), (40, gf, "gf")):
                for half in range(2):
                    b = next_bank()
                    for kk in range(4):
                        k = half * 4 + kk
                        dg = xs[kk % 2][:, 0:128]
                        P.op("dve", lambda e, dg=dg, k=k, v0=v0: e.tensor_scalar_mul(out=dg, in0=ident[:], scalar1=modT[:, v0 + k, j:j + 1]),
                             r=["ident", "modT"], w=[("xs", kk % 2)])
                        P.op("pe", lambda e, dg=dg, b=b, kk=kk: e.matmul(bank(b)[:, kk * 128:(kk + 1) * 128], lhsT=ones[:], rhs=dg, start=True, stop=True),
                             r=[("xs", kk % 2), "ones"], w=[bkey(b)])
                    P.op("act", lambda e, b=b, gt=gt, half=half: e.activation(out=gt[:, half * 512:(half + 1) * 512], in_=bank(b), func=AF.Identity, scale=0.5),
                         r=[bkey(b)], w=[gkey])

            _ck("gates")
            w1a, w1ak, w1ai = ring_next()
            w1b, w1bk, w1bi = ring_next()
            w1av = w1a[:, 0:5120].rearrange("p (k n) -> p k n", k=8)
            w1bv = w1b[:, 0:5120].rearrange("p (k n) -> p k n", k=8)
            _ck("p1_ring")
            for t in range(nkt):
                load_cs(j, t)
                _ck("p1_cs")
                make_hT(j, t)
                _ck("p1_hT")
                for c in range(5):
                    b = next_bank()
                    mm_fm(b, w1av, c * 128, lambda k: hT_all[:, k, :], 8, lambda k: [w1ak] + hkeys(k))
                    if c < 4:
                        rope_chunk(b, KdT[:, c, t * 512:(t + 1) * 512], [("Kd", c, t)])
                    else:
                        rope_chunk(b, KaT[:, t * 512:(t + 1) * 512], [("Ka", t)])
                rope_flush()
                _ck("p1_k")
                for s in range(4):
                    blk = t * 4 + s
                    b = next_bank()
                    for k in range(8):
                        P.op("pe", lambda e, b=b, k=k, s=s: e.matmul(bank(b), lhsT=hT_all[:, k, s * 128:(s + 1) * 128], rhs=w1bv[:, k, 0:512],
                                                                   start=(k == 0), stop=(k == 7)), r=[w1bk] + hkeys(k, s), w=[bkey(b)])
                    eng = evac_eng()
                    outv = Vd[:, blk, :].rearrange("p (h e) -> p h e", e=129)[:, :, 0:128]
                    srcv = bank(b).rearrange("p (h e) -> p h e", e=128)
                    if eng == "act":
                        P.op("act", lambda e, outv=outv, srcv=srcv: e.activation(out=outv, in_=srcv, func=AF.Copy), r=[bkey(b), "Vd"], w=[("Vd", blk)])
                    else:
                        P.op("dve", lambda e, outv=outv, srcv=srcv: e.tensor_copy(out=outv, in_=srcv), r=[bkey(b), "Vd"], w=[("Vd", blk)])
                    b2 = next_bank()
                    for k in range(8):
                        P.op("pe", lambda e, b2=b2, k=k, s=s: e.matmul(bank(b2)[:, 0:128], lhsT=hT_all[:, k, s * 128:(s + 1) * 128], rhs=w1bv[:, k, 512:640],
                                                                     start=(k == 0), stop=(k == 7)), r=[w1bk] + hkeys(k, s), w=[bkey(b2)])
                    for g in range(2):
                        P.op("dve", lambda e, b2=b2, g=g, blk=blk: e.tensor_copy(out=Va[:, blk, g * 128:g * 128 + 64], in_=bank(b2)[:, g * 64:(g + 1) * 64]),
                             r=[bkey(b2), "Va"], w=[("Va", blk)])
            ring_rel(w1ai)
            ring_rel(w1bi)
            _ck("phase1")
            for t in range(nqt):
                load_cs(j, t)
                make_hT(j, t)
                for pi, qdst in ((0, QdT), (1, QaT)):
                    wt, wk, wi_ = ring_next()
                    wv = wt[:, 0:4096].rearrange("p (k n) -> p k n", k=8)
                    for c in range(4):
                        b = next_bank()
                        mm_fm(b, wv, c * 128, lambda k: hT_all[:, k, :], 8, lambda k: [wk] + hkeys(k))
                        oap, okeys = qdst(c)
                        rope_chunk(b, oap, okeys)
                    ring_rel(wi_)
                rope_flush()
                _ck("s2")
                oaT_all = arA[:, 12 * 512:16 * 512].rearrange("p (k t) -> p k t", k=4)
                den_sb, denk = pg(arD, "D", 0, 2, F32)
                steps = []
                for s in range(4):
                    n = t * 4 + s
                    lst = [(g, jb) for g in range(2) for jb in (n - 1, n, n + 1) if 0 <= jb < NB]
                    for li, (g, jb) in enumerate(lst):
                        steps.append((s, n, g, jb, li == 0, li == len(lst) - 1))

                def w_front(i, st_):
                    s, n, g, jb, first, last = st_
                    sbk = next_bank(0, 4)
                    P.op("pe", lambda e: e.matmul(
                        bank(sbk).rearrange("p (c q) -> p c q", c=4), lhsT=KaT[g * 64:(g + 1) * 64, jb * 128:(jb + 1) * 128],
                        rhs=QaT_all[g * 64:(g + 1) * 64, :, s * 128:(s + 1) * 128], start=True, stop=(jb == n)),
                        r=[("Ka", jb // 4)] + [("A", 4 + c) for c in range(4)], w=[bkey(sbk)])
                    if jb != n:
                        mi = 0 if jb < n else 1
                        P.op("pe", lambda e: e.matmul(
                            bank(sbk).rearrange("p (c q) -> p c q", c=4), lhsT=ident_bf[:],
                            rhs=maskn[:, mi:mi + 1, :].to_broadcast([128, 4, 128]), start=False, stop=True),
                            r=["ident_bf", "maskn"], w=[bkey(sbk)])
                    ew, ewk = Ew(i % 2)
                    P.op("act", lambda e: e.activation(out=ew, in_=bank(sbk), func=AF.Exp, scale=0.125), r=[bkey(sbk)], w=ewk)
                    return ew, ewk

                def w_back(i, st_, ew, ewk):
                    s, n, g, jb, first, last = st_
                    pa = 2 + (s % 2)
                    accO, accS = psum[pa][:, 0, :], psum[pa][:, 1, :]
                    kO, kS = bkey(pa * 2), bkey(pa * 2 + 1)
                    P.op("pe", lambda e: e.matmul(accO, lhsT=Va[:, jb, g * 64:g * 64 + 128], rhs=ew, start=first, stop=last),
                         r=ewk + [("Va", jb)], w=[kO])
                    P.op("pe", lambda e: e.matmul(accS, lhsT=onesP[:, g * 64:g * 64 + 128], rhs=ew, start=first, stop=last),
                         r=ewk + ["onesP"], w=[kS])
                    if last:
                        dv = den_sb.rearrange("p (h q) -> p h q", h=4)
                        P.op("dve", lambda e: e.tensor_tensor(out=dv, in0=accS.rearrange("p (h q) -> p h q", h=4),
                                                              in1=esink2[:, :].unsqueeze(2).to_broadcast([128, 4, 128]), op=ALU.add),
                             r=[kS, "esink2"], w=denk)
                        P.op("dve", lambda e: e.reciprocal(out=den_sb, in_=den_sb), r=denk, w=denk)
                        P.op("dve", lambda e: e.tensor_tensor(out=oaT_all[:, :, s * 128:(s + 1) * 128], in0=accO.rearrange("p (h q) -> p h q", h=4),
                                                              in1=dv, op=ALU.mult),
                             r=[kO] + denk, w=[("A", 12 + k) for k in range(4)])

                prev = None
                for i in range(len(steps) + 1):
                    cur = None
                    if i < len(steps):
                        ew, ewk = w_front(i, steps[i])
                        cur = (i, steps[i], ew, ewk)
                    if prev is not None:
                        w_back(*prev)
                    prev = cur
                odt, odtk = Odtmp
                rcb, rck = pg(arD, "D", 0, 2, F32)
                tt_, ttk = pg(arD, "D", 2, 2, F32)
                sqb, sqk = pg(arD, "D", 4, 2, F32)
                veb, vek = pg(arD, "D", 6, 2, F32)
                npair = NB // 2
                items = [(h, c, jp) for h in range(4) for c in range(2) for jp in range(npair)]
                grp = {}

                def d_group(h, c):
                    if (h, c) not in grp:
                        pa = 2 + (acc_rr["n"] % 2)
                        acc_rr["n"] += 1
                        grp[(h, c)] = (psum[pa][:, 0, :], psum[pa][:, 1, :], bkey(pa * 2), bkey(pa * 2 + 1))
                    return grp[(h, c)]

                def d_front(idx):
                    h, c, jp = items[idx]
                    sp_ = idx % 2
                    for jj in range(2):
                        jb = 2 * jp + jj
                        P.op("pe", lambda e, jj=jj, jb=jb: e.matmul(
                            psum[sp_][:, jj, :], lhsT=KdT[c * 64:(c + 1) * 64, h, jb * 128:(jb + 1) * 128],
                            rhs=arA[c * 64:(c + 1) * 64, h * 512:(h + 1) * 512], start=True, stop=True),
                            r=[("Kd", h, jb // 4), ("A", h)], w=[bkey(sp_ * 2 + jj)])
                    ed, edk = Ed(idx % 3)
                    P.op("act", lambda e: e.activation(out=ed.rearrange("p (a q) -> p a q", a=2), in_=psum[sp_][:, :, :], func=AF.Exp, scale=0.125),
                         r=[bkey(sp_ * 2), bkey(sp_ * 2 + 1)], w=edk)
                    return ed, edk

                def d_rms(h):
                    P.op("act", lambda e: e.activation(out=sqb, in_=odt, func=AF.Square), r=odtk, w=sqk)
                    b = next_bank(0, 4)
                    P.op("pe", lambda e: e.matmul(bank(b), lhsT=ones[:], rhs=sqb, start=True, stop=True), r=sqk + ["ones"], w=[bkey(b)])
                    P.op("dve", lambda e: e.tensor_scalar(out=veb, in0=bank(b), scalar1=1.0 / 128.0, scalar2=EPS, op0=ALU.mult, op1=ALU.add),
                         r=[bkey(b)], w=vek)
                    I32 = mybir.dt.int32
                    yb, ybk = rcb, rck
                    P.op("dve", lambda e: e.tensor_single_scalar(out=yb.bitcast(I32), in_=veb.bitcast(I32), scalar=1, op=ALU.arith_shift_right), r=vek, w=ybk)
                    P.op("dve", lambda e: e.tensor_scalar(out=yb.bitcast(I32), in0=yb.bitcast(I32), scalar1=-1, scalar2=0x5f3759df, op0=ALU.mult, op1=ALU.add),
                         r=ybk, w=ybk)
                    for it in range(2):
                        P.op("dve", lambda e: e.tensor_tensor(out=tt_, in0=veb, in1=yb, op=ALU.mult), r=vek + ybk, w=ttk)
                        P.op("dve", lambda e: e.tensor_tensor(out=tt_, in0=tt_, in1=yb, op=ALU.mult), r=ttk + ybk, w=ttk)
                        P.op("dve", lambda e: e.tensor_scalar(out=tt_, in0=tt_, scalar1=-0.5, scalar2=1.5, op0=ALU.mult, op1=ALU.add), r=ttk, w=ttk)
                        P.op("dve", lambda e: e.tensor_tensor(out=yb, in0=yb, in1=tt_, op=ALU.mult), r=ttk + ybk, w=ybk)
                    P.op("dve", lambda e: e.tensor_tensor(out=sqb, in0=odt, in1=yb, op=ALU.mult), r=odtk + ybk, w=sqk)
                    oap, oak = OdT(h)
                    P.op("dve", lambda e: e.tensor_scalar_mul(out=oap, in0=sqb, scalar1=subwc[:, 0:1]), r=sqk + ["subwc"], w=oak)

                esum_buf = [pg(arC, "C", 6, 1), pg(arC, "C", 7, 1), pg(arA, "A", 8, 1)]
                sum_pend = []

                def d_sum_flush(keep=0):
                    while len(sum_pend) > keep:
                        es, esk, accS, kS, first, last = sum_pend.pop(0)
                        P.op("pe", lambda e: e.matmul(accS, lhsT=ones_bf[:], rhs=es, start=first, stop=last), r=esk + ["ones_bf"], w=[kS])

                def d_back(idx, ed, edk):
                    h, c, jp = items[idx]
                    accO, accS, kO, kS = d_group(h, c)
                    es, esk = esum_buf[idx % 3]
                    d_sum_flush(keep=1)
                    P.op("pool", lambda e: e.tensor_tensor(out=es, in0=ed[:, 0:512], in1=ed[:, 512:1024], op=ALU.add), r=edk, w=esk)
                    for jj in range(2):
                        jb = 2 * jp + jj
                        P.op("pe", lambda e, jj=jj, jb=jb: e.matmul(accO, lhsT=Vd[:, jb, h * 129:h * 129 + 128], rhs=ed[:, jj * 512:(jj + 1) * 512],
                                                                  start=(jb == 0), stop=(jb == NB - 1)), r=edk + [("Vd", jb)], w=[kO])
                    sum_pend.append((es, esk, accS, kS, jp == 0, jp == npair - 1))
                    if jp == npair - 1:
                        d_sum_flush()
                    if jp == npair - 1:
                        P.op("dve", lambda e: e.reciprocal(out=rcb, in_=accS), r=[kS], w=rck)
                        if c == 0:
                            P.op("dve", lambda e: e.tensor_tensor(out=odt, in0=accO, in1=rcb, op=ALU.mult), r=[kO] + rck, w=odtk)
                        else:
                            P.op("dve", lambda e: e.tensor_tensor(out=tt_, in0=accO, in1=rcb, op=ALU.mult), r=[kO] + rck, w=ttk)
                            P.op("dve", lambda e: e.scalar_tensor_tensor(out=odt, in0=tt_, scalar=neglam, in1=odt, op0=ALU.mult, op1=ALU.add),
                                 r=ttk + odtk + ["lsm"], w=odtk)
                            return h
                    return None

                dpend = None
                rms_q = []
                for idx in range(len(items) + 1):
                    cur = None
                    if idx < len(items):
                        ed, edk = d_front(idx)
                        cur = (idx, ed, edk)
                    while rms_q and rms_q[0][0] <= idx:
                        d_rms(rms_q.pop(0)[1])
                    if dpend is not None:
                        hdone = d_back(*dpend)
                        if hdone is not None:
                            rms_q.append((idx + min(6, npair - 1), hdone))
                    dpend = cur
                while rms_q:
                    d_rms(rms_q.pop(0)[1])
                _ck("s4")
                for qtr in range(4):
                    wabt, wabk, wabi = ring_next()
                    gabt, gabk, gabi = ring_next()
                    wabv = wabt[:, 0:2048].rearrange("p (k n) -> p k n", k=4)
                    gabv = gabt[:, 0:4096].rearrange("p (k n) -> p k n", k=8)
                    for cc in range(2):
                        oc = qtr * 2 + cc
                        bA = next_bank()
                        mm_fm(bA, wabv, cc * 128, lambda k: OaT(k)[0], 4, lambda k: [wabk] + OaT(k)[1])
                        bB = next_bank()
                        mm_fm(bB, wabv, 256 + cc * 128, lambda k: OdT(k)[0], 4, lambda k: [wabk] + OdT(k)[1])
                        bGa = next_bank()
                        mm_fm(bGa, gabv, cc * 128, lambda k: hT_all[:, k, :], 8, lambda k: [gabk] + hkeys(k))
                        bGb = next_bank()
                        mm_fm(bGb, gabv, 256 + cc * 128, lambda k: hT_all[:, k, :], 8, lambda k: [gabk] + hkeys(k))
                        ta, tak = ta_
                        tb, tbk = tb_
                        m1, m1k = m1_
                        m2, m2k = m2_
                        P.op("act", lambda e, bGa=bGa: e.activation(out=ta, in_=bank(bGa), func=AF.Tanh, scale=0.5), r=[bkey(bGa)], w=tak)
                        P.op("act", lambda e, bGb=bGb: e.activation(out=tb, in_=bank(bGb), func=AF.Tanh, scale=0.5), r=[bkey(bGb)], w=tbk)
                        P.op("dve", lambda e, bA=bA: e.scalar_tensor_tensor(out=m1, in0=ta, scalar=1.0, in1=bank(bA), op0=ALU.add, op1=ALU.mult),
                             r=tak + [bkey(bA)], w=m1k)
                        P.op("dve", lambda e, bB=bB: e.scalar_tensor_tensor(out=m2, in0=tb, scalar=1.0, in1=bank(bB), op0=ALU.add, op1=ALU.mult),
                             r=tbk + [bkey(bB)], w=m2k)
                        mo, mok = mergedT(oc)
                        P.op("pool", lambda e, mo=mo: e.tensor_tensor(out=mo, in0=m1, in1=m2, op=ALU.add), r=m1k + m2k, w=mok)
                    ring_rel(wabi)
                    ring_rel(gabi)
                _ck("s6")
                wo0, wo0k, wo0i = ring_next()
                wo1, wo1k, wo1i = ring_next()
                wov = [wo0[:, 0:4096].rearrange("p (k n) -> p k n", k=8), wo1[:, 0:4096].rearrange("p (k n) -> p k n", k=8)]
                wok = [wo0k, wo1k]
                P.op("sp", lambda e: e.dma_start(out=lnv[0][:], in_=lnrow_d[0:1, :].partition_broadcast(128)), w=[("lnv", 0)], chan=("lnv", 0))
                P.op("sp", lambda e: e.dma_start(out=lnv[1][:], in_=lnrow_d[1:2, :].partition_broadcast(128)), w=[("lnv", 1)], chan=("lnv", 1))
                tmps = [pg(arD, "D", 0, 4, F32), pg(arD, "D", 4, 4, F32)]
                xns = [pg(arA, "A", 0, 4, F32), pg(arA, "A", 4, 4, F32)]

                def s7_front(s):
                    tmp, tmpk = tmps[s % 2]
                    xn, xnk = xns[s % 2]
                    slot = xs_rr["n"] % 2
                    xs_rr["n"] += 1
                    load_x_sub(j, t * 512 + s * 128, slot)
                    for half in range(2):
                        b = next_bank()
                        for k in range(8):
                            P.op("pe", lambda e, b=b, k=k, s=s, half=half: e.matmul(bank(b), lhsT=arC[:, k * 512 + s * 128:k * 512 + (s + 1) * 128],
                                                                                  rhs=wov[half][:, k, :], start=(k == 0), stop=(k == 7)),
                                 r=[wok[half], ("C", k)], w=[bkey(b)])
                        P.op("dve", lambda e, b=b, half=half, tmp=tmp: e.tensor_tensor(out=tmp[:, half * 512:(half + 1) * 512], in0=bank(b),
                                                                            in1=gm[:, half * 512:(half + 1) * 512], op=ALU.mult),
                             r=[bkey(b), "gm"], w=tmpk)
                    P.op("dve", lambda e, s=s, slot=slot, tmp=tmp: e.scalar_tensor_tensor(out=x1[:, s, :], in0=xs[slot][:], scalar=ALPHA, in1=tmp,
                                                                                  op0=ALU.mult, op1=ALU.add), r=[("xs", slot)] + tmpk, w=[("x1", s)])
                    for hh in range(2):
                        P.op("dve", lambda e, s=s, hh=hh: e.bn_stats(out=bst[:, hh, :], in_=x1[:, s, hh * 512:(hh + 1) * 512]), r=[("x1", s)], w=["bst"])
                    P.op("dve", lambda e: e.bn_aggr(out=stt[:, 32:34], in_=bst[:]), r=["bst"], w=["stt_mv"])
                    P.op("pool", lambda e: e.tensor_scalar_add(out=stt[:, 34:35], in0=stt[:, 33:34], scalar1=EPS), r=["stt_mv"], w=["stt_ve2"])
                    P.op("pool", lambda e: e.tensor_tensor(out=stt[:, 35:36], in0=stt[:, 34:35], in1=mhalf[:, 0:1], op=ALU.pow), r=["stt_ve2", "mhalf"], w=["stt_rs2"])
                    P.op("dve", lambda e, s=s, xn=xn: e.tensor_scalar(out=xn, in0=x1[:, s, :], scalar1=stt[:, 32:33], scalar2=stt[:, 35:36],
                                                               op0=ALU.subtract, op1=ALU.mult), r=[("x1", s), "stt_mv", "stt_rs2"], w=xnk)
                    P.op("pool", lambda e, s=s, xn=xn: e.tensor_tensor(out=x1[:, s, :], in0=xn, in1=lnv[0][:], op=ALU.mult), r=xnk + [("lnv", 0)], w=[("x1", s)])
                    P.op("pool", lambda e, s=s: e.tensor_tensor(out=x1[:, s, :], in0=x1[:, s, :], in1=lnv[1][:], op=ALU.add), r=[("x1", s), ("lnv", 1)], w=[("x1", s)])

                def s7_back(s):
                    xn, xnk = xns[s % 2]
                    for kg in range(2):
                        b = next_bank()
                        for kk in range(4):
                            k = kg * 4 + kk
                            P.op("pe", lambda e, b=b, kk=kk, k=k, xn=xn: e.transpose(bank(b)[:, kk * 128:(kk + 1) * 128], xn[:, k * 128:(k + 1) * 128], ident[:]),
                                 r=xnk + ["ident"], w=[bkey(b)])
                        for kk in range(4):
                            k = kg * 4 + kk
                            out = hT_all[:, k, s * 128:(s + 1) * 128]
                            src = bank(b)[:, kk * 128:(kk + 1) * 128]
                            if kg == 0:
                                P.op("act", lambda e, out=out, src=src, k=k: e.activation(out=out, in_=src, func=AF.Identity,
                                                                                       bias=cols[:, j, 3, k:k + 1], scale=cols[:, j, 2, k:k + 1]),
                                     r=[bkey(b), "cols"], w=hkeys(k, s))
                            else:
                                P.op("dve", lambda e, out=out, src=src, k=k: e.tensor_scalar(out=out, in0=src, scalar1=cols[:, j, 2, k:k + 1],
                                                                                          scalar2=cols[:, j, 3, k:k + 1], op0=ALU.mult, op1=ALU.add),
                                     r=[bkey(b), "cols"], w=hkeys(k, s))

                for s in range(5):
                    if s < 4:
                        s7_front(s)
                    if s >= 1:
                        s7_back(s - 1)
                ring_rel(wo0i)
                ring_rel(wo1i)
                _ck("s7")
                if t + 1 < nqt:
                    prefetch_x(j, t + 1)
                for pi in range(11):
                    wt, wk, wi_ = ring_next()
                    wv = wt[:, 0:4096].rearrange("p (k n) -> p k n", k=8)
                    for pp in range(2):
                        fi = 2 * pi + pp
                        bG = next_bank()
                        mm_fm(bG, wv, (2 * pp) * 128, lambda k: hT_all[:, k, :], 8, lambda k: [wk] + hkeys(k))
                        bU = next_bank()
                        mm_fm(bU, wv, (2 * pp + 1) * 128, lambda k: hT_all[:, k, :], 8, lambda k: [wk] + hkeys(k))
                        tg, tgk = tgm(fi % 2)
                        ww, wwk = wm(fi % 2)
                        P.op("act", lambda e, bG=bG, tg=tg: e.activation(out=tg, in_=bank(bG), func=AF.Tanh, scale=0.5), r=[bkey(bG)], w=tgk)
                        P.op("dve", lambda e, bG=bG, tg=tg, ww=ww: e.scalar_tensor_tensor(out=ww, in0=tg, scalar=1.0, in1=bank(bG), op0=ALU.add, op1=ALU.mult),
                             r=tgk + [bkey(bG)], w=wwk)
                        ao, aok = actT(fi)
                        P.op("dve", lambda e, bU=bU, ww=ww, ao=ao: e.tensor_tensor(out=ao, in0=ww, in1=bank(bU), op=ALU.mult), r=wwk + [bkey(bU)], w=aok)
                    ring_rel(wi_)
                _ck("s8")
                P.op("sp", lambda e: e.dma_start(out=lnv[0][:], in_=lnrow_d[2:3, :].partition_broadcast(128)), w=[("lnv", 0)], chan=("lnv", 0))
                P.op("sp", lambda e: e.dma_start(out=lnv[1][:], in_=lnrow_d[3:4, :].partition_broadcast(128)), w=[("lnv", 1)], chan=("lnv", 1))
                for half in range(2):
                    bs = [4 * half + s for s in range(4)]
                    for kp in range(2):
                        wt, wk, wi_ = ring_next()
                        wv = wt[:, 0:5632].rearrange("p (k n) -> p k n", k=11)
                        for s in range(4):
                            for kk in range(11):
                                f = kp * 11 + kk
                                P.op("pe", lambda e, s=s, kk=kk, f=f, wv=wv, bs=bs: e.matmul(bank(bs[s]), lhsT=arA[:, f * 512 + s * 128:f * 512 + (s + 1) * 128],
                                                                                      rhs=wv[:, kk, :], start=(f == 0), stop=(f == NFF - 1)),
                                     r=[wk, ("A", f)], w=[bkey(bs[s])])
                        ring_rel(wi_)
                    for s in range(4):
                        m1, m1k = m1_
                        P.op("dve", lambda e, s=s, half=half, bs=bs: e.tensor_tensor(out=m1, in0=bank(bs[s]), in1=gf[:, half * 512:(half + 1) * 512], op=ALU.mult),
                             r=[bkey(bs[s]), "gf"], w=m1k)
                        P.op("dve", lambda e, s=s, half=half: e.scalar_tensor_tensor(out=x1[:, s, half * 512:(half + 1) * 512], in0=x1[:, s, half * 512:(half + 1) * 512],
                                                                                      scalar=ALPHA, in1=m1, op0=ALU.mult, op1=ALU.add), r=[("x1", s)] + m1k, w=[("x1", s)])
                for s in range(4):
                    for hh in range(2):
                        P.op("dve", lambda e, s=s, hh=hh: e.bn_stats(out=bst[:, hh, :], in_=x1[:, s, hh * 512:(hh + 1) * 512]), r=[("x1", s)], w=["bst"])
                    P.op("dve", lambda e: e.bn_aggr(out=stt[:, 32:34], in_=bst[:]), r=["bst"], w=["stt_mv"])
                    P.op("pool", lambda e: e.tensor_scalar_add(out=stt[:, 34:35], in0=stt[:, 33:34], scalar1=EPS), r=["stt_mv"], w=["stt_ve2"])
                    P.op("pool", lambda e: e.tensor_tensor(out=stt[:, 35:36], in0=stt[:, 34:35], in1=mhalf[:, 0:1], op=ALU.pow), r=["stt_ve2", "mhalf"], w=["stt_rs2"])
                    P.op("dve", lambda e, s=s: e.tensor_scalar(out=x1[:, s, :], in0=x1[:, s, :], scalar1=stt[:, 32:33], scalar2=stt[:, 35:36],
                                                               op0=ALU.subtract, op1=ALU.mult), r=[("x1", s), "stt_mv", "stt_rs2"], w=[("x1", s)])
                    P.op("pool", lambda e, s=s: e.tensor_tensor(out=x1[:, s, :], in0=x1[:, s, :], in1=lnv[0][:], op=ALU.mult), r=[("x1", s), ("lnv", 0)], w=[("x1", s)])
                    P.op("pool", lambda e, s=s: e.tensor_tensor(out=x1[:, s, :], in0=x1[:, s, :], in1=lnv[1][:], op=ALU.add), r=[("x1", s), ("lnv", 1)], w=[("x1", s)])
                    P.op("pool", lambda e, s=s: e.dma_start(out=yj[j][t * 512 + s * 128:t * 512 + (s + 1) * 128, :], in_=x1[:, s, :]),
                         r=[("x1", s)], chan=("st", s))
        try:
            for j in range(NJ):
                job_body(j)
        except _Stop:
            pass
        P.emit(nc, st, final_wait_chans=[("st", s) for s in range(4)])
    return nc


def _tile_w(w, kc, ncols):
    K, N = w.shape
    return np.ascontiguousarray(w.reshape(K // 128, 128, N // ncols, ncols).transpose(2, 1, 0, 3))


def _rope_tables(S, reverse=False):
    inv = 1.0 / (10000.0 ** (np.arange(0, 64, 2, dtype=np.float32) / 64.0))
    pos = np.arange(S, dtype=np.float32)
    if reverse:
        pos = pos[::-1]
    ang = pos[None, :].astype(np.float32) * inv[:, None].astype(np.float32)
    ang = ang.astype(np.float32)
    idx = (np.arange(128) % 64) % 32
    return np.ascontiguousarray(np.stack([np.cos(ang)[idx], np.sin(ang)[idx]], 0).astype(np.float32))


def _const_inputs():
    ident = np.eye(128, dtype=np.float32)
    pt = np.zeros((128, 128), np.float32)
    for m in range(128):
        if (m % 64) < 32:
            pt[m + 32, m] = -1.0
        else:
            pt[m - 32, m] = 1.0
    ki = np.arange(128)[:, None]
    qi = np.arange(128)[None, :]
    masks = np.stack([(qi <= ki), (ki <= qi)], 1).astype(np.float32)
    return dict(identc=ident, ptm=pt, maskc=np.ascontiguousarray(masks), masknc=np.ascontiguousarray((masks - 1.0) * 30000.0))


def _weight_inputs(w_ada, b_ada, w_in, sink_logit, lam_q1, lam_k1, lam_q2, lam_k2, subln_w, w_a, w_b, w_o,
                   ln1_g, ln1_b, w_gu, w_down, ln2_g, ln2_b):
    w_in = w_in[0]
    qa, ka, va = w_in[:, 0:512], w_in[:, 512:640], w_in[:, 640:768]
    qd, kd, vd = w_in[:, 768:1280], w_in[:, 1280:1792], w_in[:, 1792:2304]
    ga, gb = w_in[:, 2304:3328], w_in[:, 3328:4352]
    perm = np.concatenate([np.concatenate([np.arange(c * 64, c * 64 + 64), np.arange((c + 4) * 64, (c + 4) * 64 + 64)]) for c in range(4)])
    qa_p = qa[:, perm]
    win1 = np.stack([_tile_w(np.concatenate([kd, ka], 1), 8, 640)[0], _tile_w(np.concatenate([vd, va], 1), 8, 640)[0]], 0)
    win2 = np.concatenate([_tile_w(qd, 8, 512), _tile_w(qa_p, 8, 512)], 0)
    wa_p = w_a[0][perm, :]
    wab = _tile_w(np.concatenate([np.concatenate([wa_p[:, q * 256:(q + 1) * 256], w_b[0][:, q * 256:(q + 1) * 256]], 1) for q in range(4)], 1), 4, 512)
    wgab = _tile_w(np.concatenate([np.concatenate([ga[:, q * 256:(q + 1) * 256], gb[:, q * 256:(q + 1) * 256]], 1) for q in range(4)], 1), 8, 512)
    wo = _tile_w(w_o[0], 8, 512)
    g, u = w_gu[0][:, :DFF], w_gu[0][:, DFF:]
    gu = np.concatenate([np.concatenate([g[:, f * 128:(f + 1) * 128], u[:, f * 128:(f + 1) * 128]], 1) for f in range(NFF)], 1)
    wgu = _tile_w(gu, 8, 512)
    wd = w_down[0]
    wdn = np.stack([np.ascontiguousarray(wd[kp * 1408:(kp + 1) * 1408, half * 512:(half + 1) * 512].reshape(11, 128, 512).transpose(1, 0, 2))
                    for half in range(2) for kp in range(2)], 0)
    wada = _tile_w(w_ada[0], 8, 256)
    bcol = np.ascontiguousarray(b_ada[0].reshape(48, 128).T)
    lamv = np.concatenate([lam_q1[0], lam_k1[0], lam_q2[0], lam_k2[0]])[None, :]
    lnrow = np.stack([ln1_g[0], ln1_b[0], ln2_g[0], ln2_b[0]], 0)
    lncol = np.ascontiguousarray(np.stack([ln1_g[0].reshape(8, 128).T, ln1_b[0].reshape(8, 128).T], 1))
    f = lambda a: np.ascontiguousarray(a, dtype=np.float32)
    return dict(win1=f(win1), win2=f(win2), wab=f(wab), wgab=f(wgab), wo=f(wo), wgu=f(wgu), wdn=f(wdn), wada=f(wada), bcolin=f(bcol),
                sink=f(sink_logit), lamv=f(lamv), subwin=f(subln_w), subwcol=f(subln_w[0][:, None]), lnrow=f(lnrow), lncolin=f(lncol))


_NC_CACHE = {}


def run_jobs(core_jobs, jobs_cfg, weights):
    key = tuple(jobs_cfg)
    if key not in _NC_CACHE:
        _NC_CACHE[key] = build(list(jobs_cfg))
    nc = _NC_CACHE[key]
    shared = dict(weights)
    shared.update(_const_inputs())
    in_maps = []
    for cj in core_jobs:
        m = dict(shared)
        cs_ = np.stack([c for (_, c, _) in cj], 0)
        m["cTin"] = np.ascontiguousarray(cs_.reshape(len(cj), 8, 128).transpose(2, 1, 0)).astype(np.float32)
        for j, (x, c, rev) in enumerate(cj):
            m[f"xin{j}"] = np.ascontiguousarray(x, dtype=np.float32)
            m[f"rope{j}"] = _rope_tables(x.shape[0], rev)
        in_maps.append(m)
    res = run_bass_kernel_spmd(nc, in_maps, core_ids=list(range(len(core_jobs))))
    return [[r[f"yout{j}"] for j in range(len(jobs_cfg))] for r in res.results]


def kernel(x_prompt, x_sample, c_prompt, c_sample, w_ada, b_ada, w_in, sink_logit, lam_q1, lam_k1, lam_q2, lam_k2,
           subln_w, w_a, w_b, w_o, ln1_g, ln1_b, w_gu, w_down, ln2_g, ln2_b):
    a = lambda v: np.asarray(v)
    weights = _weight_inputs(a(w_ada), a(b_ada), a(w_in), a(sink_logit), a(lam_q1), a(lam_k1), a(lam_q2), a(lam_k2), a(subln_w),
                             a(w_a), a(w_b), a(w_o), a(ln1_g), a(ln1_b), a(w_gu), a(w_down), a(ln2_g), a(ln2_b))
    xs_ = [a(x_prompt)[i] for i in range(16)] + [a(x_sample)[i] for i in range(4)]
    cs_ = [a(c_prompt)[i] for i in range(16)] + [a(c_sample)[i] for i in range(4)]
    core_jobs = []
    plan = []
    for c in range(8):
        halves = list(range(5 * c, 5 * c + 5))
        seqs = sorted(set(h // 2 for h in halves))
        full = [s for s in seqs if (2 * s in halves and 2 * s + 1 in halves)]
        part = [h for h in halves if (h // 2) not in full]
        assert len(full) == 2 and len(part) == 1
        hp = part[0]
        rev = (hp % 2 == 1)
        sp_ = hp // 2
        xp = xs_[sp_][::-1] if rev else xs_[sp_]
        core_jobs.append([(xs_[full[0]], cs_[full[0]], False), (xs_[full[1]], cs_[full[1]], False), (xp, cs_[sp_], rev)])
        plan.append((full, sp_, rev))
    outs = run_jobs(core_jobs, ((8, 8), (8, 8), (8, 4)), weights)
    y = np.zeros((20, 4096, D), np.float32)
    for c in range(8):
        full, sp_, rev = plan[c]
        y[full[0]] = outs[c][0]
        y[full[1]] = outs[c][1]
        if rev:
            y[sp_, 2048:] = outs[c][2][::-1]
        else:
            y[sp_, :2048] = outs[c][2]
    return (y[:16], y[16:])
```

```python
from contextlib import ExitStack
import types
import numpy as np
import concourse.bass as bass
import concourse.mybir as mybir
from concourse.bass_utils import run_bass_kernel_spmd

F32 = mybir.dt.float32
BF16 = mybir.dt.bfloat16
AF = mybir.ActivationFunctionType
ALU = mybir.AluOpType

D = 1024
DFF = 2816
NFF = 22
ALPHA = 2.0 ** 0.25
LAM_INIT = 0.2
EPS = 1e-5


def _freeze(fn):
    if fn.__closure__ is None:
        return fn
    cells = []
    for c in fn.__closure__:
        try:
            cells.append(types.CellType(c.cell_contents))
        except ValueError:
            cells.append(c)
    return types.FunctionType(fn.__code__, fn.__globals__, fn.__name__, fn.__defaults__, tuple(cells))


class Prog:
    ENG = ("pe", "act", "dve", "pool", "sp")

    def __init__(self):
        self.q = {e: [] for e in self.ENG}
        self.last_w = {}
        self.readers = {}
        self.chans = []

    def op(self, eng, fn, r=(), w=(), chan=None):
        idx = len(self.q[eng])
        deps = set()
        for b in r:
            lw = self.last_w.get(b)
            if lw is not None:
                deps.add(lw)
        for b in w:
            lw = self.last_w.get(b)
            if lw is not None:
                deps.add(lw)
            rd = self.readers.get(b)
            if rd:
                deps.update(rd.values())
        if chan is not None and chan not in self.chans:
            self.chans.append(chan)
        self.q[eng].append(dict(fn=_freeze(fn), deps=deps, chan=chan, signal=False, ticket=None))
        me = (eng, idx)
        for b in r:
            key = eng if chan is None else ("dma", chan)
            self.readers.setdefault(b, {})[key] = me
        for b in w:
            self.last_w[b] = me
            self.readers[b] = {}
        return me

    def emit(self, nc, stack, final_wait_chans=()):
        q = self.q
        for e in self.ENG:
            for ins in q[e]:
                for (e2, i2) in ins["deps"]:
                    q[e2][i2]["signal"] = True
        esem = {e: stack.enter_context(nc.semaphore("s_" + e)) for e in self.ENG if e != "sp"}
        csem = {c: stack.enter_context(nc.semaphore("c_" + str(c))) for c in self.chans}
        ccount = {c: 0 for c in self.chans}
        for e in self.ENG:
            cnt = 0
            for ins in q[e]:
                if ins["chan"] is not None:
                    ccount[ins["chan"]] += 16
                    ins["ticket"] = ccount[ins["chan"]]
                elif ins["signal"]:
                    cnt += 1
                    ins["ticket"] = cnt
        block = stack.enter_context(nc.Block())

        def run(e, eng):
            waited = {}
            for ins in q[e]:
                need = {}
                for (e2, i2) in ins["deps"]:
                    d = q[e2][i2]
                    if d["chan"] is not None:
                        s = csem[d["chan"]]
                        key = ("c", d["chan"])
                    else:
                        if e2 == e and e == "pe":
                            continue
                        s = esem[e2]
                        key = ("e", e2)
                    v = d["ticket"]
                    if v > need.get(key, (None, 0))[1]:
                        need[key] = (s, v)
                for key, (s, v) in need.items():
                    if waited.get(key, 0) >= v:
                        continue
                    eng.wait_ge(s, v)
                    waited[key] = v
                r = ins["fn"](eng)
                if ins["chan"] is not None:
                    r.then_inc(csem[ins["chan"]], 16)
                elif ins["signal"]:
                    r.then_inc(esem[e], 1)
            if e == "pool":
                for c in final_wait_chans:
                    if ccount.get(c, 0) > 0:
                        eng.wait_ge(csem[c], ccount[c])

        @block.tensor
        def _(eng):
            run("pe", eng)

        @block.scalar
        def _(eng):
            run("act", eng)

        @block.vector
        def _(eng):
            run("dve", eng)

        @block.gpsimd
        def _(eng):
            run("pool", eng)

        @block.sync
        def _(eng):
            run("sp", eng)


class _Stop(Exception):
    pass


def _ck(name):
    return None


def build(jobs):
    NJ = len(jobs)
    nc = bass.Bass("TRN2", target_bir_lowering=False)
    P = Prog()

    def din(name, shape, dtype=F32):
        return nc.dram_tensor(name, shape, dtype, kind="ExternalInput").ap()

    def dscr(name, shape, dtype=BF16):
        return nc.dram_tensor(name, shape, dtype, kind="Internal").ap()

    xj = [din(f"xin{j}", [jobs[j][0] * 512, D]) for j in range(NJ)]
    ropej = [din(f"rope{j}", [2, 128, jobs[j][0] * 512]) for j in range(NJ)]
    yj = [nc.dram_tensor(f"yout{j}", [jobs[j][1] * 512, D], F32, kind="ExternalOutput").ap() for j in range(NJ)]
    cT_d = din("cTin", [128, 8, NJ])
    wada_d = din("wada", [24, 128, 8, 256])
    bcol_d = din("bcolin", [128, 48])
    wshapes = dict(win1=[2, 128, 8, 640], win2=[2, 128, 8, 512], wab=[4, 128, 4, 512], wgab=[4, 128, 8, 512],
                   wo=[2, 128, 8, 512], wgu=[11, 128, 8, 512], wdn=[4, 128, 11, 512])
    wf = {k: din(k, v) for k, v in wshapes.items()}
    wb16 = {k: dscr(k + "_b", v) for k, v in wshapes.items()}
    ident_d = din("identc", [128, 128])
    pt_d = din("ptm", [128, 128])
    mask_d = din("maskc", [128, 2, 128])
    maskn_d = din("masknc", [128, 2, 128])
    sink_d = din("sink", [1, 8])
    lamv_d = din("lamv", [1, 256])
    subw_d = din("subwin", [1, 128])
    subwc_d = din("subwcol", [128, 1])
    lnrow_d = din("lnrow", [4, D])
    lncol_d = din("lncolin", [128, 2, 8])

    NKT = max(j[0] for j in jobs)
    SM = NKT * 512
    NBM = NKT * 4

    with ExitStack() as st:
        def sb(name, shape, dtype):
            return st.enter_context(nc.sbuf_tensor(name, shape, dtype))

        KdT = sb("KdT", [128, 4, SM], BF16)
        Vd = sb("Vd", [128, NBM, 516], BF16)
        KaT = sb("KaT", [128, SM], BF16)
        Va = sb("Va", [128, NBM, 192], BF16)
        NRING = 3
        ring = [sb(f"ring{i}", [128, 5632], BF16) for i in range(NRING)]
        x1 = sb("x1", [128, 4, D], F32)
        xs = [sb(f"xs{i}", [128, D], F32) for i in range(2)]
        cs = sb("cs", [128, 2, 512], F32)
        lnv = [sb(f"lnv{i}", [128, D], F32) for i in range(2)]
        lamt = lnv[0][:, 0:256]
        gm = sb("gm", [128, D], BF16)
        gf = sb("gf", [128, D], BF16)
        arA = sb("arA", [128, 11264], BF16)
        arB = sb("arB", [128, 4096], BF16)
        arC = sb("arC", [128, 4096], BF16)
        arD = sb("arD", [128, 4096], BF16)
        ident = sb("ident", [128, 128], F32)
        ones = sb("ones", [128, 128], F32)
        PT = sb("PT", [128, 128], BF16)
        esink = sb("esink", [128, 8], F32)
        lsm = sb("lsm", [128, 8], F32)
        mhalf = sb("mhalf", [128, 16], F32)
        cT = sb("cT", [128, 8, NJ], F32)
        siluT = sb("siluT", [128, 8, NJ], F32)
        modT = sb("modT", [128, 48, NJ], F32)
        bcol = sb("bcol", [128, 48], F32)
        lncol = sb("lncol", [128, 2, 8], F32)
        cols = sb("cols", [128, NJ, 4, 8], F32)
        onesP = sb("onesP", [128, 192], BF16)
        esink2 = sb("esink2", [128, 4], F32)
        ones_bf = sb("ones_bf", [128, 128], BF16)
        ident_bf = sb("ident_bf", [128, 128], BF16)
        maskn = sb("maskn", [128, 2, 128], BF16)
        subwc = sb("subwc", [128, 1], F32)
        stt = sb("stt", [128, 64], F32)
        bst = sb("bst", [128, 2, 6], F32)
        psum = [st.enter_context(nc.psum_tensor(f"ps{i}", [128, 2, 512], F32)) for i in range(4)]

        def bank(b):
            return psum[b // 2][:, b % 2, :]

        def bkey(b):
            return f"ps{b}"

        def pg(ar, name, p0, n, dtype=BF16):
            ap = ar[:, p0 * 512:(p0 + n) * 512]
            if dtype == F32:
                ap = ap.bitcast(F32)
            return ap, [(name, p) for p in range(p0, p0 + n)]

        actT = lambda f: pg(arA, "A", f, 1)
        QdT = lambda c: pg(arA, "A", c, 1)
        QaT_all = arA[:, 4 * 512:8 * 512].rearrange("p (c t) -> p c t", c=4)
        QaT = lambda c: pg(arA, "A", 4 + c, 1)
        Oatmp = pg(arA, "A", 8, 2, F32)
        Odtmp = pg(arA, "A", 10, 2, F32)
        OaT = lambda k: pg(arA, "A", 12 + k, 1)
        OdT = lambda k: pg(arA, "A", 16 + k, 1)
        hT_all = arB[:, :].rearrange("p (k t) -> p k t", k=8)
        hkeys = lambda k, s=None: [("B", k, ss) for ss in (range(4) if s is None else [s])]
        mergedT = lambda oc: pg(arC, "C", oc, 1)
        Ed = lambda i: pg(arC, "C", 2 * i, 2)
        Ew = lambda i: pg(arC, "C", 6 + i, 1)
        tgm = lambda i: pg(arC, "C", 2 * i, 2, F32)
        wm = lambda i: pg(arC, "C", 4 + 2 * i, 2, F32)
        qb_ = pg(arD, "D", 0, 1)
        t1_ = pg(arD, "D", 1, 2, F32)
        t2_ = pg(arD, "D", 3, 2, F32)
        ta_ = pg(arD, "D", 0, 2, F32)
        tb_ = pg(arD, "D", 2, 2, F32)
        m1_ = pg(arD, "D", 4, 2, F32)
        m2_ = pg(arD, "D", 6, 2, F32)
        tmp_ = pg(arD, "D", 0, 4, F32)
        xn_ = pg(arD, "D", 4, 4, F32)

        sched = []
        ring_state = dict(issued=0, used=0)
        released = set()

        def ring_pump():
            while ring_state["issued"] < len(sched):
                n = ring_state["issued"]
                if n >= NRING and (n - NRING) not in released:
                    break
                slot = n % NRING
                wname, wi = sched[n]
                src = wb16[wname][wi].rearrange("p k n -> p (k n)")
                ne = src.shape[1]
                P.op("sp", lambda e, slot=slot, src=src, ne=ne: e.dma_start(out=ring[slot][:, 0:ne], in_=src),
                     r=[("scr", wname, wi)], w=[("ring", slot)], chan=("ring", slot))
                ring_state["issued"] += 1

        def ring_next():
            i = ring_state["used"]
            ring_state["used"] += 1
            ring_pump()
            assert ring_state["issued"] > i, "ring deadlock: too many pieces held"
            return ring[i % NRING], ("ring", i % NRING), i

        def ring_rel(i):
            released.add(i)
            ring_pump()

        def piece(name, i):
            return (name, i)

        for j in range(NJ):
            nkt, nqt = jobs[j]
            sched += [piece("win1", 0), piece("win1", 1)]
            for t in range(nqt):
                sched += [piece("win2", 0), piece("win2", 1)]
                for q in range(4):
                    sched += [piece("wab", q), piece("wgab", q)]
                sched += [piece("wo", 0), piece("wo", 1)]
                sched += [piece("wgu", i) for i in range(11)]
                sched += [piece("wdn", i) for i in range(4)]

        def cast_weights(names, extra_r=()):
            for k in names:
                for i in range(wshapes[k][0]):
                    P.op("pool", lambda e, k=k, i=i: e.dma_start(out=wb16[k][i], in_=wf[k][i]), r=list(extra_r), w=[("scr", k, i)], chan=("cast", k, i))
        cast_weights(("win1", "win2", "wab", "wgab", "wo", "wgu", "wdn"))
        cnum = dict(n=0)

        def ld(out, in_, keys):
            cnum["n"] += 1
            P.op("sp", lambda e: e.dma_start(out=out, in_=in_), w=keys, chan=("const", cnum["n"]))
        ld(ident[:], ident_d, ["ident"])
        ld(cT[:], cT_d, ["cT"])
        ld(bcol[:], bcol_d, ["bcol"])
        ld(lncol[:], lncol_d, ["lncol"])
        ld(esink[:], sink_d.partition_broadcast(128), ["esink"])
        ld(lamt, lamv_d.partition_broadcast(128), [("lnv", 0)])
        ld(subwc[:], subwc_d, ["subwc"])
        P.op("pool", lambda e: e.dma_start(out=PT[:], in_=pt_d), w=["PT"], chan=("constc", 0))
        P.op("pool", lambda e: e.dma_start(out=ident_bf[:], in_=ident_d), w=["ident_bf"], chan=("constc", 1))
        P.op("pool", lambda e: e.dma_start(out=maskn[:], in_=maskn_d), w=["maskn"], chan=("constc", 2))
        P.op("pool", lambda e: e.memset(ones[:], 1.0), w=["ones"])
        P.op("pool", lambda e: e.memset(onesP[:], 1.0), w=["onesP"])
        P.op("pool", lambda e: e.memset(onesP[:, 64:128], 0.0), r=["onesP"], w=["onesP"])
        P.op("pool", lambda e: e.memset(ones_bf[:], 1.0), w=["ones_bf"])
        P.op("pool", lambda e: e.memset(mhalf[:], -0.5), w=["mhalf"])
        P.op("pool", lambda e: e.memset(Vd[:], 1.0), w=["Vd"])
        P.op("pool", lambda e: e.memset(Va[:], 0.0), w=["Va"])
        P.op("act", lambda e: e.activation(out=esink[:], in_=esink[:], func=AF.Exp), r=["esink"], w=["esink"])
        P.op("dve", lambda e: e.tensor_copy(out=esink2[0:64, :], in_=esink[0:64, 0:4]), r=["esink"], w=["esink2"])
        P.op("dve", lambda e: e.tensor_copy(out=esink2[64:128, :], in_=esink[64:128, 4:8]), r=["esink", "esink2"], w=["esink2"])
        P.op("dve", lambda e: e.tensor_tensor(out=lamt[:, 0:64], in0=lamt[:, 0:64], in1=lamt[:, 64:128], op=ALU.mult), r=[("lnv", 0)], w=[("lnv", 0)])
        P.op("dve", lambda e: e.tensor_tensor(out=lamt[:, 128:192], in0=lamt[:, 128:192], in1=lamt[:, 192:256], op=ALU.mult), r=[("lnv", 0)], w=[("lnv", 0)])
        P.op("act", lambda e: e.activation(out=lamt[:, 64:128], in_=lamt[:, 0:64], func=AF.Identity, accum_out=lsm[:, 0:1]), r=[("lnv", 0)], w=[("lnv", 0), "lsm"])
        P.op("act", lambda e: e.activation(out=lamt[:, 192:256], in_=lamt[:, 128:192], func=AF.Identity, accum_out=lsm[:, 1:2]), r=[("lnv", 0)], w=[("lnv", 0), "lsm"])
        P.op("act", lambda e: e.activation(out=lsm[:, 2:4], in_=lsm[:, 0:2], func=AF.Exp), r=["lsm"], w=["lsm"])
        P.op("dve", lambda e: e.tensor_tensor(out=lsm[:, 4:5], in0=lsm[:, 3:4], in1=lsm[:, 2:3], op=ALU.subtract), r=["lsm"], w=["lsm"])
        P.op("dve", lambda e: e.tensor_scalar_add(out=lsm[:, 5:6], in0=lsm[:, 4:5], scalar1=-LAM_INIT), r=["lsm"], w=["lsm"])
        neglam = lsm[:, 5:6]
        P.op("dve", lambda e: e.tensor_scalar_mul(out=subwc[:], in0=subwc[:], scalar1=1.0 - LAM_INIT), r=["subwc"], w=["subwc"])
        P.op("act", lambda e: e.activation(out=siluT[:], in_=cT[:], func=AF.Tanh, scale=0.5), r=["cT"], w=["siluT"])
        P.op("dve", lambda e: e.scalar_tensor_tensor(out=siluT[:], in0=siluT[:], scalar=1.0, in1=cT[:], op0=ALU.add, op1=ALU.mult),
             r=["siluT", "cT"], w=["siluT"])
        P.op("dve", lambda e: e.tensor_scalar_mul(out=siluT[:], in0=siluT[:], scalar1=0.5), r=["siluT"], w=["siluT"])
        wstage = [arA[:, i * 4096:(i + 1) * 4096].bitcast(F32).rearrange("p (k n) -> p k n", k=8) for i in range(2)]
        wstk = [[("A", p) for p in range(8 * i, 8 * i + 8)] for i in range(2)]
        mps = psum[0][:, 0, 0:48 * NJ].rearrange("p (c j) -> p c j", j=NJ)
        for pc in range(24):
            i = pc % 2
            P.op("sp", lambda e, pc=pc, i=i: e.dma_start(out=wstage[i], in_=wada_d[pc]), w=wstk[i], chan=("wst", i))
            for cc in range(2):
                ch = pc * 2 + cc
                for kk in range(8):
                    P.op("pe", lambda e, i=i, cc=cc, kk=kk, ch=ch: e.matmul(mps[:, ch, :], lhsT=wstage[i][:, kk, cc * 128:(cc + 1) * 128],
                                                                          rhs=siluT[:, kk, :], start=(kk == 0), stop=(kk == 7)),
                         r=wstk[i] + ["siluT"], w=[bkey(0)])
        for j in range(NJ):
            P.op("dve", lambda e, j=j: e.tensor_tensor(out=modT[:, :, j], in0=mps[:, :, j], in1=bcol[:], op=ALU.add),
                 r=[bkey(0), "bcol"], w=["modT"])
        for j in range(NJ):
            P.op("dve", lambda e, j=j: e.tensor_scalar_add(out=cols[:, j, 0, :], in0=modT[:, 8:16, j], scalar1=1.0), r=["modT"], w=["cols"])
            P.op("dve", lambda e, j=j: e.tensor_copy(out=cols[:, j, 1, :], in_=modT[:, 0:8, j]), r=["modT"], w=["cols"])
            P.op("dve", lambda e, j=j: e.scalar_tensor_tensor(out=cols[:, j, 2, :], in0=modT[:, 32:40, j], scalar=1.0, in1=lncol[:, 0, :],
                                                              op0=ALU.add, op1=ALU.mult), r=["modT", "lncol"], w=["cols"])
            P.op("dve", lambda e, j=j: e.scalar_tensor_tensor(out=cols[:, j, 3, :], in0=modT[:, 32:40, j], scalar=1.0, in1=lncol[:, 1, :],
                                                              op0=ALU.add, op1=ALU.mult), r=["modT", "lncol"], w=["cols"])
            P.op("dve", lambda e, j=j: e.tensor_tensor(out=cols[:, j, 3, :], in0=cols[:, j, 3, :], in1=modT[:, 24:32, j], op=ALU.add),
                 r=["modT", "cols"], w=["cols"])

        rr = dict(ps=0)
        rope_rr = dict(n=0)
        acc_rr = dict(n=0)

        def next_bank(lo=0, hi=8):
            b = lo + rr["ps"] % (hi - lo)
            rr["ps"] += 1
            return b

        alt = dict(n=0)

        def evac_eng():
            alt["n"] += 1
            return "act" if alt["n"] % 2 else "dve"

        def load_x_sub(j, row0, slot):
            P.op("sp", lambda e: e.dma_start(out=xs[slot][:], in_=xj[j][row0:row0 + 128, :]), w=[("xs", slot)], chan=("xs", slot))

        xs_rr = dict(n=0)
        xpre = {}

        def prefetch_x(j, t):
            for s in range(2):
                slot = xs_rr["n"] % 2
                xs_rr["n"] += 1
                load_x_sub(j, t * 512 + s * 128, slot)
                xpre[(j, t, s)] = slot

        def make_hT(j, t):
            for s in range(4):
                if (j, t, s) in xpre:
                    slot = xpre.pop((j, t, s))
                else:
                    slot = xs_rr["n"] % 2
                    xs_rr["n"] += 1
                    load_x_sub(j, t * 512 + s * 128, slot)
                _ck("p1_x")
                for kg in range(2):
                    b = next_bank()
                    for kk in range(4):
                        k = kg * 4 + kk
                        if kk == 1:
                            _ck("p1_tr")
                        P.op("pe", lambda e, b=b, kk=kk, k=k, slot=slot: e.transpose(bank(b)[:, kk * 128:(kk + 1) * 128],
                                                                                 xs[slot][:, k * 128:(k + 1) * 128], ident[:]),
                             r=[("xs", slot), "ident"], w=[bkey(b)])
                    for kk in range(4):
                        k = kg * 4 + kk
                        if kk == 1:
                            _ck("p1_ev1")
                        if kk == 2:
                            _ck("p1_ev2")
                        eng = "act"
                        out = hT_all[:, k, s * 128:(s + 1) * 128]
                        src = bank(b)[:, kk * 128:(kk + 1) * 128]
                        if eng == "act":
                            P.op("act", lambda e, out=out, src=src, k=k: e.activation(out=out, in_=src, func=AF.Identity,
                                                                                   bias=cols[:, j, 1, k:k + 1], scale=cols[:, j, 0, k:k + 1]),
                                 r=[bkey(b), "cols"], w=hkeys(k, s))
                        else:
                            P.op("dve", lambda e, out=out, src=src, k=k: e.tensor_scalar(out=out, in0=src, scalar1=cols[:, j, 0, k:k + 1],
                                                                                      scalar2=cols[:, j, 1, k:k + 1], op0=ALU.mult, op1=ALU.add),
                                 r=[bkey(b), "cols"], w=hkeys(k, s))

        def load_cs(j, t):
            P.op("sp", lambda e: e.dma_start(out=cs[:], in_=ropej[j][:, :, t * 512:(t + 1) * 512].rearrange("c p t -> p c t")),
                 w=["cs"], chan="cs")

        def mm_fm(b, wv, col0, rhs_fn, nk, rkeys):
            for k in range(nk):
                P.op("pe", lambda e, k=k: e.matmul(bank(b), lhsT=wv[:, k, col0:col0 + 128], rhs=rhs_fn(k), start=(k == 0), stop=(k == nk - 1)),
                     r=rkeys(k), w=[bkey(b)])

        rope_pend = []

        def rope_chunk(b, out_ap, out_keys):
            rope_rr["n"] += 1
            odd = rope_rr["n"] % 2
            if odd:
                qb, qbk = qb_
                t1, t1k = t1_
                t2, t2k = t2_
            else:
                qb, qbk = pg(arC, "C", 0, 1)
                t1, t1k = pg(arC, "C", 1, 2, F32)
                t2, t2k = pg(arC, "C", 3, 2, F32)
            P.op("act", lambda e: e.activation(out=qb, in_=bank(b), func=AF.Copy), r=[bkey(b)], w=qbk)

            def tail():
                b2 = next_bank()
                P.op("pe", lambda e: e.matmul(bank(b2), lhsT=PT[:], rhs=qb, start=True, stop=True), r=qbk + ["PT"], w=[bkey(b2)])
                P.op("dve", lambda e: e.tensor_tensor(out=t1, in0=bank(b), in1=cs[:, 0, :], op=ALU.mult), r=[bkey(b), "cs"] + qbk, w=t1k)
                P.op("dve", lambda e: e.tensor_tensor(out=t2, in0=bank(b2), in1=cs[:, 1, :], op=ALU.mult), r=[bkey(b2), "cs"], w=t2k)
                P.op("pool" if odd else "dve", lambda e: e.tensor_tensor(out=out_ap, in0=t1, in1=t2, op=ALU.add), r=t1k + t2k, w=out_keys)
            rope_flush()
            rope_pend.append(tail)

        def rope_flush():
            while rope_pend:
                rope_pend.pop(0)()

        def job_body(j):
            nkt, nqt = jobs[j]
            NB = nkt * 4
            _ck("prologue")
            for (v0, gt, gkey) in ((16, gm, "gm"), (40, gf, "gf")):
                for half in range(2):
                    b = next_bank()
                    for kk in range(4):
                        k = half * 4 + kk
                        dg = xs[kk % 2][:, 0:128]
                        P.op("dve", lambda e, dg=dg, k=k, v0=v0: e.tensor_scalar_mul(out=dg, in0=ident[:], scalar1=modT[:, v0 + k, j:j + 1]),
                             r=["ident", "modT"], w=[("xs", kk % 2)])
                        P.op("pe", lambda e, dg=dg, b=b, kk=kk: e.matmul(bank(b)[:, kk * 128:(kk + 1) * 128], lhsT=ones[:], rhs=dg, start=True, stop=True),
                             r=[("xs", kk % 2), "ones"], w=[bkey(b)])
                    P.op("act", lambda e, b=b, gt=gt, half=half: e.activation(out=gt[:, half * 512:(half + 1) * 512], in_=bank(b), func=AF.Identity, scale=0.5),
                         r=[bkey(b)], w=[gkey])

            _ck("gates")
            w1a, w1ak, w1ai = ring_next()
            w1b, w1bk, w1bi = ring_next()
            w1av = w1a[:, 0:5120].rearrange("p (k n) -> p k n", k=8)
            w1bv = w1b[:, 0:5120].rearrange("p (k n) -> p k n", k=8)
            _ck("p1_ring")
            for t in range(nkt):
                load_cs(j, t)
                _ck("p1_cs")
                make_hT(j, t)
                _ck("p1_hT")
                for c in range(5):
                    b = next_bank()
                    mm_fm(b, w1av, c * 128, lambda k: hT_all[:, k, :], 8, lambda k: [w1ak] + hkeys(k))
                    if c < 4:
                        rope_chunk(b, KdT[:, c, t * 512:(t + 1) * 512], [("Kd", c, t)])
                    else:
                        rope_chunk(b, KaT[:, t * 512:(t + 1) * 512], [("Ka", t)])
                rope_flush()
                _ck("p1_k")
                for s in range(4):
                    blk = t * 4 + s
                    b = next_bank()
                    for k in range(8):
                        P.op("pe", lambda e, b=b, k=k, s=s: e.matmul(bank(b), lhsT=hT_all[:, k, s * 128:(s + 1) * 128], rhs=w1bv[:, k, 0:512],
                                                                   start=(k == 0), stop=(k == 7)), r=[w1bk] + hkeys(k, s), w=[bkey(b)])
                    eng = evac_eng()
                    outv = Vd[:, blk, :].rearrange("p (h e) -> p h e", e=129)[:, :, 0:128]
                    srcv = bank(b).rearrange("p (h e) -> p h e", e=128)
                    if eng == "act":
                        P.op("act", lambda e, outv=outv, srcv=srcv: e.activation(out=outv, in_=srcv, func=AF.Copy), r=[bkey(b), "Vd"], w=[("Vd", blk)])
                    else:
                        P.op("dve", lambda e, outv=outv, srcv=srcv: e.tensor_copy(out=outv, in_=srcv), r=[bkey(b), "Vd"], w=[("Vd", blk)])
                    b2 = next_bank()
                    for k in range(8):
                        P.op("pe", lambda e, b2=b2, k=k, s=s: e.matmul(bank(b2)[:, 0:128], lhsT=hT_all[:, k, s * 128:(s + 1) * 128], rhs=w1bv[:, k, 512:640],
                                                                     start=(k == 0), stop=(k == 7)), r=[w1bk] + hkeys(k, s), w=[bkey(b2)])
                    for g in range(2):
                        P.op("dve", lambda e, b2=b2, g=g, blk=blk: e.tensor_copy(out=Va[:, blk, g * 128:g * 128 + 64], in_=bank(b2)[:, g * 64:(g + 1) * 64]),
                             r=[bkey(b2), "Va"], w=[("Va", blk)])
            ring_rel(w1ai)
            ring_rel(w1bi)
            _ck("phase1")
            for t in range(nqt):
                load_cs(j, t)
                make_hT(j, t)
                for pi, qdst in ((0, QdT), (1, QaT)):
                    wt, wk, wi_ = ring_next()
                    wv = wt[:, 0:4096].rearrange("p (k n) -> p k n", k=8)
                    for c in range(4):
                        b = next_bank()
                        mm_fm(b, wv, c * 128, lambda k: hT_all[:, k, :], 8, lambda k: [wk] + hkeys(k))
                        oap, okeys = qdst(c)
                        rope_chunk(b, oap, okeys)
                    ring_rel(wi_)
                rope_flush()
                _ck("s2")
                oaT_all = arA[:, 12 * 512:16 * 512].rearrange("p (k t) -> p k t", k=4)
                den_sb, denk = pg(arD, "D", 0, 2, F32)
                steps = []
                for s in range(4):
                    n = t * 4 + s
                    lst = [(g, jb) for g in range(2) for jb in (n - 1, n, n + 1) if 0 <= jb < NB]
                    for li, (g, jb) in enumerate(lst):
                        steps.append((s, n, g, jb, li == 0, li == len(lst) - 1))

                def w_front(i, st_):
                    s, n, g, jb, first, last = st_
                    sbk = next_bank(0, 4)
                    P.op("pe", lambda e: e.matmul(
                        bank(sbk).rearrange("p (c q) -> p c q", c=4), lhsT=KaT[g * 64:(g + 1) * 64, jb * 128:(jb + 1) * 128],
                        rhs=QaT_all[g * 64:(g + 1) * 64, :, s * 128:(s + 1) * 128], start=True, stop=(jb == n)),
                        r=[("Ka", jb // 4)] + [("A", 4 + c) for c in range(4)], w=[bkey(sbk)])
                    if jb != n:
                        mi = 0 if jb < n else 1
                        P.op("pe", lambda e: e.matmul(
                            bank(sbk).rearrange("p (c q) -> p c q", c=4), lhsT=ident_bf[:],
                            rhs=maskn[:, mi:mi + 1, :].to_broadcast([128, 4, 128]), start=False, stop=True),
                            r=["ident_bf", "maskn"], w=[bkey(sbk)])
                    ew, ewk = Ew(i % 2)
                    P.op("act", lambda e: e.activation(out=ew, in_=bank(sbk), func=AF.Exp, scale=0.125), r=[bkey(sbk)], w=ewk)
                    return ew, ewk

                def w_back(i, st_, ew, ewk):
                    s, n, g, jb, first, last = st_
                    pa = 2 + (s % 2)
                    accO, accS = psum[pa][:, 0, :], psum[pa][:, 1, :]
                    kO, kS = bkey(pa * 2), bkey(pa * 2 + 1)
                    P.op("pe", lambda e: e.matmul(accO, lhsT=Va[:, jb, g * 64:g * 64 + 128], rhs=ew, start=first, stop=last),
                         r=ewk + [("Va", jb)], w=[kO])
                    P.op("pe", lambda e: e.matmul(accS, lhsT=onesP[:, g * 64:g * 64 + 128], rhs=ew, start=first, stop=last),
                         r=ewk + ["onesP"], w=[kS])
                    if last:
                        dv = den_sb.rearrange("p (h q) -> p h q", h=4)
                        P.op("dve", lambda e: e.tensor_tensor(out=dv, in0=accS.rearrange("p (h q) -> p h q", h=4),
                                                              in1=esink2[:, :].unsqueeze(2).to_broadcast([128, 4, 128]), op=ALU.add),
                             r=[kS, "esink2"], w=denk)
                        P.op("dve", lambda e: e.reciprocal(out=den_sb, in_=den_sb), r=denk, w=denk)
                        P.op("dve", lambda e: e.tensor_tensor(out=oaT_all[:, :, s * 128:(s + 1) * 128], in0=accO.rearrange("p (h q) -> p h q", h=4),
                                                              in1=dv, op=ALU.mult),
                             r=[kO] + denk, w=[("A", 12 + k) for k in range(4)])

                prev = None
                for i in range(len(steps) + 1):
                    cur = None
                    if i < len(steps):
                        ew, ewk = w_front(i, steps[i])
                        cur = (i, steps[i], ew, ewk)
                    if prev is not None:
                        w_back(*prev)
                    prev = cur
                odt, odtk = Odtmp
                rcb, rck = pg(arD, "D", 0, 2, F32)
                tt_, ttk = pg(arD, "D", 2, 2, F32)
                sqb, sqk = pg(arD, "D", 4, 2, F32)
                veb, vek = pg(arD, "D", 6, 2, F32)
                npair = NB // 2
                items = [(h, c, jp) for h in range(4) for c in range(2) for jp in range(npair)]
                grp = {}

                def d_group(h, c):
                    if (h, c) not in grp:
                        pa = 2 + (acc_rr["n"] % 2)
                        acc_rr["n"] += 1
                        grp[(h, c)] = (psum[pa][:, 0, :], psum[pa][:, 1, :], bkey(pa * 2), bkey(pa * 2 + 1))
                    return grp[(h, c)]

                def d_front(idx):
                    h, c, jp = items[idx]
                    sp_ = idx % 2
                    for jj in range(2):
                        jb = 2 * jp + jj
                        P.op("pe", lambda e, jj=jj, jb=jb: e.matmul(
                            psum[sp_][:, jj, :], lhsT=KdT[c * 64:(c + 1) * 64, h, jb * 128:(jb + 1) * 128],
                            rhs=arA[c * 64:(c + 1) * 64, h * 512:(h + 1) * 512], start=True, stop=True),
                            r=[("Kd", h, jb // 4), ("A", h)], w=[bkey(sp_ * 2 + jj)])
                    ed, edk = Ed(idx % 3)
                    P.op("act", lambda e: e.activation(out=ed.rearrange("p (a q) -> p a q", a=2), in_=psum[sp_][:, :, :], func=AF.Exp, scale=0.125),
                         r=[bkey(sp_ * 2), bkey(sp_ * 2 + 1)], w=edk)
                    return ed, edk

                def d_rms(h):
                    P.op("act", lambda e: e.activation(out=sqb, in_=odt, func=AF.Square), r=odtk, w=sqk)
                    b = next_bank(0, 4)
                    P.op("pe", lambda e: e.matmul(bank(b), lhsT=ones[:], rhs=sqb, start=True, stop=True), r=sqk + ["ones"], w=[bkey(b)])
                    P.op("dve", lambda e: e.tensor_scalar(out=veb, in0=bank(b), scalar1=1.0 / 128.0, scalar2=EPS, op0=ALU.mult, op1=ALU.add),
                         r=[bkey(b)], w=vek)
                    I32 = mybir.dt.int32
                    yb, ybk = rcb, rck
                    P.op("dve", lambda e: e.tensor_single_scalar(out=yb.bitcast(I32), in_=veb.bitcast(I32), scalar=1, op=ALU.arith_shift_right), r=vek, w=ybk)
                    P.op("dve", lambda e: e.tensor_scalar(out=yb.bitcast(I32), in0=yb.bitcast(I32), scalar1=-1, scalar2=0x5f3759df, op0=ALU.mult, op1=ALU.add),
                         r=ybk, w=ybk)
                    for it in range(2):
                        P.op("dve", lambda e: e.tensor_tensor(out=tt_, in0=veb, in1=yb, op=ALU.mult), r=vek + ybk, w=ttk)
                        P.op("dve", lambda e: e.tensor_tensor(out=tt_, in0=tt_, in1=yb, op=ALU.mult), r=ttk + ybk, w=ttk)
                        P.op("dve", lambda e: e.tensor_scalar(out=tt_, in0=tt_, scalar1=-0.5, scalar2=1.5, op0=ALU.mult, op1=ALU.add), r=ttk, w=ttk)
                        P.op("dve", lambda e: e.tensor_tensor(out=yb, in0=yb, in1=tt_, op=ALU.mult), r=ttk + ybk, w=ybk)
                    P.op("dve", lambda e: e.tensor_tensor(out=sqb, in0=odt, in1=yb, op=ALU.mult), r=odtk + ybk, w=sqk)
                    oap, oak = OdT(h)
                    P.op("dve", lambda e: e.tensor_scalar_mul(out=oap, in0=sqb, scalar1=subwc[:, 0:1]), r=sqk + ["subwc"], w=oak)

                esum_buf = [pg(arC, "C", 6, 1), pg(arC, "C", 7, 1), pg(arA, "A", 8, 1)]
                sum_pend = []

                def d_sum_flush(keep=0):
                    while len(sum_pend) > keep:
                        es, esk, accS, kS, first, last = sum_pend.pop(0)
                        P.op("pe", lambda e: e.matmul(accS, lhsT=ones_bf[:], rhs=es, start=first, stop=last), r=esk + ["ones_bf"], w=[kS])

                def d_back(idx, ed, edk):
                    h, c, jp = items[idx]
                    accO, accS, kO, kS = d_group(h, c)
                    es, esk = esum_buf[idx % 3]
                    d_sum_flush(keep=1)
                    P.op("pool", lambda e: e.tensor_tensor(out=es, in0=ed[:, 0:512], in1=ed[:, 512:1024], op=ALU.add), r=edk, w=esk)
                    for jj in range(2):
                        jb = 2 * jp + jj
                        P.op("pe", lambda e, jj=jj, jb=jb: e.matmul(accO, lhsT=Vd[:, jb, h * 129:h * 129 + 128], rhs=ed[:, jj * 512:(jj + 1) * 512],
                                                                  start=(jb == 0), stop=(jb == NB - 1)), r=edk + [("Vd", jb)], w=[kO])
                    sum_pend.append((es, esk, accS, kS, jp == 0, jp == npair - 1))
                    if jp == npair - 1:
                        d_sum_flush()
                    if jp == npair - 1:
                        P.op("dve", lambda e: e.reciprocal(out=rcb, in_=accS), r=[kS], w=rck)
                        if c == 0:
                            P.op("dve", lambda e: e.tensor_tensor(out=odt, in0=accO, in1=rcb, op=ALU.mult), r=[kO] + rck, w=odtk)
                        else:
                            P.op("dve", lambda e: e.tensor_tensor(out=tt_, in0=accO, in1=rcb, op=ALU.mult), r=[kO] + rck, w=ttk)
                            P.op("dve", lambda e: e.scalar_tensor_tensor(out=odt, in0=tt_, scalar=neglam, in1=odt, op0=ALU.mult, op1=ALU.add),
                                 r=ttk + odtk + ["lsm"], w=odtk)
                            return h
                    return None

                dpend = None
                rms_q = []
                for idx in range(len(items) + 1):
                    cur = None
                    if idx < len(items):
                        ed, edk = d_front(idx)
                        cur = (idx, ed, edk)
                    while rms_q and rms_q[0][0] <= idx:
                        d_rms(rms_q.pop(0)[1])
                    if dpend is not None:
                        hdone = d_back(*dpend)
                        if hdone is not None:
                            rms_q.append((idx + min(6, npair - 1), hdone))
                    dpend = cur
                while rms_q:
                    d_rms(rms_q.pop(0)[1])
                _ck("s4")
                for qtr in range(4):
                    wabt, wabk, wabi = ring_next()
                    gabt, gabk, gabi = ring_next()
                    wabv = wabt[:, 0:2048].rearrange("p (k n) -> p k n", k=4)
                    gabv = gabt[:, 0:4096].rearrange("p (k n) -> p k n", k=8)
                    for cc in range(2):
                        oc = qtr * 2 + cc
                        bA = next_bank()
                        mm_fm(bA, wabv, cc * 128, lambda k: OaT(k)[0], 4, lambda k: [wabk] + OaT(k)[1])
                        bB = next_bank()
                        mm_fm(bB, wabv, 256 + cc * 128, lambda k: OdT(k)[0], 4, lambda k: [wabk] + OdT(k)[1])
                        bGa = next_bank()
                        mm_fm(bGa, gabv, cc * 128, lambda k: hT_all[:, k, :], 8, lambda k: [gabk] + hkeys(k))
                        bGb = next_bank()
                        mm_fm(bGb, gabv, 256 + cc * 128, lambda k: hT_all[:, k, :], 8, lambda k: [gabk] + hkeys(k))
                        ta, tak = ta_
                        tb, tbk = tb_
                        m1, m1k = m1_
                        m2, m2k = m2_
                        P.op("act", lambda e, bGa=bGa: e.activation(out=ta, in_=bank(bGa), func=AF.Tanh, scale=0.5), r=[bkey(bGa)], w=tak)
                        P.op("act", lambda e, bGb=bGb: e.activation(out=tb, in_=bank(bGb), func=AF.Tanh, scale=0.5), r=[bkey(bGb)], w=tbk)
                        P.op("dve", lambda e, bA=bA: e.scalar_tensor_tensor(out=m1, in0=ta, scalar=1.0, in1=bank(bA), op0=ALU.add, op1=ALU.mult),
                             r=tak + [bkey(bA)], w=m1k)
                        P.op("dve", lambda e, bB=bB: e.scalar_tensor_tensor(out=m2, in0=tb, scalar=1.0, in1=bank(bB), op0=ALU.add, op1=ALU.mult),
                             r=tbk + [bkey(bB)], w=m2k)
                        mo, mok = mergedT(oc)
                        P.op("pool", lambda e, mo=mo: e.tensor_tensor(out=mo, in0=m1, in1=m2, op=ALU.add), r=m1k + m2k, w=mok)
                    ring_rel(wabi)
                    ring_rel(gabi)
                _ck("s6")
                wo0, wo0k, wo0i = ring_next()
                wo1, wo1k, wo1i = ring_next()
                wov = [wo0[:, 0:4096].rearrange("p (k n) -> p k n", k=8), wo1[:, 0:4096].rearrange("p (k n) -> p k n", k=8)]
                wok = [wo0k, wo1k]
                P.op("sp", lambda e: e.dma_start(out=lnv[0][:], in_=lnrow_d[0:1, :].partition_broadcast(128)), w=[("lnv", 0)], chan=("lnv", 0))
                P.op("sp", lambda e: e.dma_start(out=lnv[1][:], in_=lnrow_d[1:2, :].partition_broadcast(128)), w=[("lnv", 1)], chan=("lnv", 1))
                tmps = [pg(arD, "D", 0, 4, F32), pg(arD, "D", 4, 4, F32)]
                xns = [pg(arA, "A", 0, 4, F32), pg(arA, "A", 4, 4, F32)]

                def s7_front(s):
                    tmp, tmpk = tmps[s % 2]
                    xn, xnk = xns[s % 2]
                    slot = xs_rr["n"] % 2
                    xs_rr["n"] += 1
                    load_x_sub(j, t * 512 + s * 128, slot)
                    for half in range(2):
                        b = next_bank()
                        for k in range(8):
                            P.op("pe", lambda e, b=b, k=k, s=s, half=half: e.matmul(bank(b), lhsT=arC[:, k * 512 + s * 128:k * 512 + (s + 1) * 128],
                                                                                  rhs=wov[half][:, k, :], start=(k == 0), stop=(k == 7)),
                                 r=[wok[half], ("C", k)], w=[bkey(b)])
                        P.op("dve", lambda e, b=b, half=half, tmp=tmp: e.tensor_tensor(out=tmp[:, half * 512:(half + 1) * 512], in0=bank(b),
                                                                            in1=gm[:, half * 512:(half + 1) * 512], op=ALU.mult),
                             r=[bkey(b), "gm"], w=tmpk)
                    P.op("dve", lambda e, s=s, slot=slot, tmp=tmp: e.scalar_tensor_tensor(out=x1[:, s, :], in0=xs[slot][:], scalar=ALPHA, in1=tmp,
                                                                                  op0=ALU.mult, op1=ALU.add), r=[("xs", slot)] + tmpk, w=[("x1", s)])
                    for hh in range(2):
                        P.op("dve", lambda e, s=s, hh=hh: e.bn_stats(out=bst[:, hh, :], in_=x1[:, s, hh * 512:(hh + 1) * 512]), r=[("x1", s)], w=["bst"])
                    P.op("dve", lambda e: e.bn_aggr(out=stt[:, 32:34], in_=bst[:]), r=["bst"], w=["stt_mv"])
                    P.op("pool", lambda e: e.tensor_scalar_add(out=stt[:, 34:35], in0=stt[:, 33:34], scalar1=EPS), r=["stt_mv"], w=["stt_ve2"])
                    P.op("pool", lambda e: e.tensor_tensor(out=stt[:, 35:36], in0=stt[:, 34:35], in1=mhalf[:, 0:1], op=ALU.pow), r=["stt_ve2", "mhalf"], w=["stt_rs2"])
                    P.op("dve", lambda e, s=s, xn=xn: e.tensor_scalar(out=xn, in0=x1[:, s, :], scalar1=stt[:, 32:33], scalar2=stt[:, 35:36],
                                                               op0=ALU.subtract, op1=ALU.mult), r=[("x1", s), "stt_mv", "stt_rs2"], w=xnk)
                    P.op("pool", lambda e, s=s, xn=xn: e.tensor_tensor(out=x1[:, s, :], in0=xn, in1=lnv[0][:], op=ALU.mult), r=xnk + [("lnv", 0)], w=[("x1", s)])
                    P.op("pool", lambda e, s=s: e.tensor_tensor(out=x1[:, s, :], in0=x1[:, s, :], in1=lnv[1][:], op=ALU.add), r=[("x1", s), ("lnv", 1)], w=[("x1", s)])

                def s7_back(s):
                    xn, xnk = xns[s % 2]
                    for kg in range(2):
                        b = next_bank()
                        for kk in range(4):
                            k = kg * 4 + kk
                            P.op("pe", lambda e, b=b, kk=kk, k=k, xn=xn: e.transpose(bank(b)[:, kk * 128:(kk + 1) * 128], xn[:, k * 128:(k + 1) * 128], ident[:]),
                                 r=xnk + ["ident"], w=[bkey(b)])
                        for kk in range(4):
                            k = kg * 4 + kk
                            out = hT_all[:, k, s * 128:(s + 1) * 128]
                            src = bank(b)[:, kk * 128:(kk + 1) * 128]
                            if True:
                                P.op("act", lambda e, out=out, src=src, k=k: e.activation(out=out, in_=src, func=AF.Identity,
                                                                                       bias=cols[:, j, 3, k:k + 1], scale=cols[:, j, 2, k:k + 1]),
                                     r=[bkey(b), "cols"], w=hkeys(k, s))
                            else:
                                P.op("dve", lambda e, out=out, src=src, k=k: e.tensor_scalar(out=out, in0=src, scalar1=cols[:, j, 2, k:k + 1],
                                                                                          scalar2=cols[:, j, 3, k:k + 1], op0=ALU.mult, op1=ALU.add),
                                     r=[bkey(b), "cols"], w=hkeys(k, s))

                for s in range(5):
                    if s < 4:
                        s7_front(s)
                    if s >= 1:
                        s7_back(s - 1)
                ring_rel(wo0i)
                ring_rel(wo1i)
                _ck("s7")
                if t + 1 < nqt:
                    prefetch_x(j, t + 1)
                for pi in range(11):
                    wt, wk, wi_ = ring_next()
                    wv = wt[:, 0:4096].rearrange("p (k n) -> p k n", k=8)
                    for pp in range(2):
                        fi = 2 * pi + pp
                        bG = next_bank()
                        mm_fm(bG, wv, (2 * pp) * 128, lambda k: hT_all[:, k, :], 8, lambda k: [wk] + hkeys(k))
                        bU = next_bank()
                        mm_fm(bU, wv, (2 * pp + 1) * 128, lambda k: hT_all[:, k, :], 8, lambda k: [wk] + hkeys(k))
                        tg, tgk = tgm(fi % 2)
                        ww, wwk = wm(fi % 2)
                        P.op("act", lambda e, bG=bG, tg=tg: e.activation(out=tg, in_=bank(bG), func=AF.Tanh, scale=0.5), r=[bkey(bG)], w=tgk)
                        P.op("dve", lambda e, bG=bG, tg=tg, ww=ww: e.scalar_tensor_tensor(out=ww, in0=tg, scalar=1.0, in1=bank(bG), op0=ALU.add, op1=ALU.mult),
                             r=tgk + [bkey(bG)], w=wwk)
                        ao, aok = actT(fi)
                        P.op("dve", lambda e, bU=bU, ww=ww, ao=ao: e.tensor_tensor(out=ao, in0=ww, in1=bank(bU), op=ALU.mult), r=wwk + [bkey(bU)], w=aok)
                    ring_rel(wi_)
                _ck("s8")
                P.op("sp", lambda e: e.dma_start(out=lnv[0][:], in_=lnrow_d[2:3, :].partition_broadcast(128)), w=[("lnv", 0)], chan=("lnv", 0))
                P.op("sp", lambda e: e.dma_start(out=lnv[1][:], in_=lnrow_d[3:4, :].partition_broadcast(128)), w=[("lnv", 1)], chan=("lnv", 1))
                for half in range(2):
                    bs = [4 * half + s for s in range(4)]
                    for kp in range(2):
                        wt, wk, wi_ = ring_next()
                        wv = wt[:, 0:5632].rearrange("p (k n) -> p k n", k=11)
                        for s in range(4):
                            for kk in range(11):
                                f = kp * 11 + kk
                                P.op("pe", lambda e, s=s, kk=kk, f=f, wv=wv, bs=bs: e.matmul(bank(bs[s]), lhsT=arA[:, f * 512 + s * 128:f * 512 + (s + 1) * 128],
                                                                                      rhs=wv[:, kk, :], start=(f == 0), stop=(f == NFF - 1)),
                                     r=[wk, ("A", f)], w=[bkey(bs[s])])
                        ring_rel(wi_)
                    for s in range(4):
                        m1, m1k = m1_
                        P.op("dve", lambda e, s=s, half=half, bs=bs: e.tensor_tensor(out=m1, in0=bank(bs[s]), in1=gf[:, half * 512:(half + 1) * 512], op=ALU.mult),
                             r=[bkey(bs[s]), "gf"], w=m1k)
                        P.op("dve", lambda e, s=s, half=half: e.scalar_tensor_tensor(out=x1[:, s, half * 512:(half + 1) * 512], in0=x1[:, s, half * 512:(half + 1) * 512],
                                                                                      scalar=ALPHA, in1=m1, op0=ALU.mult, op1=ALU.add), r=[("x1", s)] + m1k, w=[("x1", s)])
                for s in range(4):
                    for hh in range(2):
                        P.op("dve", lambda e, s=s, hh=hh: e.bn_stats(out=bst[:, hh, :], in_=x1[:, s, hh * 512:(hh + 1) * 512]), r=[("x1", s)], w=["bst"])
                    P.op("dve", lambda e: e.bn_aggr(out=stt[:, 32:34], in_=bst[:]), r=["bst"], w=["stt_mv"])
                    P.op("pool", lambda e: e.tensor_scalar_add(out=stt[:, 34:35], in0=stt[:, 33:34], scalar1=EPS), r=["stt_mv"], w=["stt_ve2"])
                    P.op("pool", lambda e: e.tensor_tensor(out=stt[:, 35:36], in0=stt[:, 34:35], in1=mhalf[:, 0:1], op=ALU.pow), r=["stt_ve2", "mhalf"], w=["stt_rs2"])
                    P.op("dve", lambda e, s=s: e.tensor_scalar(out=x1[:, s, :], in0=x1[:, s, :], scalar1=stt[:, 32:33], scalar2=stt[:, 35:36],
                                                               op0=ALU.subtract, op1=ALU.mult), r=[("x1", s), "stt_mv", "stt_rs2"], w=[("x1", s)])
                    P.op("pool", lambda e, s=s: e.tensor_tensor(out=x1[:, s, :], in0=x1[:, s, :], in1=lnv[0][:], op=ALU.mult), r=[("x1", s), ("lnv", 0)], w=[("x1", s)])
                    P.op("pool", lambda e, s=s: e.tensor_tensor(out=x1[:, s, :], in0=x1[:, s, :], in1=lnv[1][:], op=ALU.add), r=[("x1", s), ("lnv", 1)], w=[("x1", s)])
                    P.op("pool", lambda e, s=s: e.dma_start(out=yj[j][t * 512 + s * 128:t * 512 + (s + 1) * 128, :], in_=x1[:, s, :]),
                         r=[("x1", s)], chan=("st", s))
        try:
            for j in range(NJ):
                job_body(j)
        except _Stop:
            pass
        P.emit(nc, st, final_wait_chans=[("st", s) for s in range(4)])
    return nc


def _tile_w(w, kc, ncols):
    K, N = w.shape
    return np.ascontiguousarray(w.reshape(K // 128, 128, N // ncols, ncols).transpose(2, 1, 0, 3))


def _rope_tables(S, reverse=False):
    inv = 1.0 / (10000.0 ** (np.arange(0, 64, 2, dtype=np.float32) / 64.0))
    pos = np.arange(S, dtype=np.float32)
    if reverse:
        pos = pos[::-1]
    ang = pos[None, :].astype(np.float32) * inv[:, None].astype(np.float32)
    ang = ang.astype(np.float32)
    idx = (np.arange(128) % 64) % 32
    return np.ascontiguousarray(np.stack([np.cos(ang)[idx], np.sin(ang)[idx]], 0).astype(np.float32))


def _const_inputs():
    ident = np.eye(128, dtype=np.float32)
    pt = np.zeros((128, 128), np.float32)
    for m in range(128):
        if (m % 64) < 32:
            pt[m + 32, m] = -1.0
        else:
            pt[m - 32, m] = 1.0
    ki = np.arange(128)[:, None]
    qi = np.arange(128)[None, :]
    masks = np.stack([(qi <= ki), (ki <= qi)], 1).astype(np.float32)
    return dict(identc=ident, ptm=pt, maskc=np.ascontiguousarray(masks), masknc=np.ascontiguousarray((masks - 1.0) * 30000.0))


def _weight_inputs(w_ada, b_ada, w_in, sink_logit, lam_q1, lam_k1, lam_q2, lam_k2, subln_w, w_a, w_b, w_o,
                   ln1_g, ln1_b, w_gu, w_down, ln2_g, ln2_b):
    w_in = w_in[0]
    qa, ka, va = w_in[:, 0:512], w_in[:, 512:640], w_in[:, 640:768]
    qd, kd, vd = w_in[:, 768:1280], w_in[:, 1280:1792], w_in[:, 1792:2304]
    ga, gb = w_in[:, 2304:3328], w_in[:, 3328:4352]
    perm = np.concatenate([np.concatenate([np.arange(c * 64, c * 64 + 64), np.arange((c + 4) * 64, (c + 4) * 64 + 64)]) for c in range(4)])
    qa_p = qa[:, perm]
    win1 = np.stack([_tile_w(np.concatenate([kd, ka], 1), 8, 640)[0], _tile_w(np.concatenate([vd, va], 1), 8, 640)[0]], 0)
    win2 = np.concatenate([_tile_w(qd, 8, 512), _tile_w(qa_p, 8, 512)], 0)
    wa_p = w_a[0][perm, :]
    wab = _tile_w(np.concatenate([np.concatenate([wa_p[:, q * 256:(q + 1) * 256], w_b[0][:, q * 256:(q + 1) * 256]], 1) for q in range(4)], 1), 4, 512)
    wgab = _tile_w(np.concatenate([np.concatenate([ga[:, q * 256:(q + 1) * 256], gb[:, q * 256:(q + 1) * 256]], 1) for q in range(4)], 1), 8, 512)
    wo = _tile_w(w_o[0], 8, 512)
    g, u = w_gu[0][:, :DFF], w_gu[0][:, DFF:]
    gu = np.concatenate([np.concatenate([g[:, f * 128:(f + 1) * 128], u[:, f * 128:(f + 1) * 128]], 1) for f in range(NFF)], 1)
    wgu = _tile_w(gu, 8, 512)
    wd = w_down[0]
    wdn = np.stack([np.ascontiguousarray(wd[kp * 1408:(kp + 1) * 1408, half * 512:(half + 1) * 512].reshape(11, 128, 512).transpose(1, 0, 2))
                    for half in range(2) for kp in range(2)], 0)
    wada = _tile_w(w_ada[0], 8, 256)
    bcol = np.ascontiguousarray(b_ada[0].reshape(48, 128).T)
    lamv = np.concatenate([lam_q1[0], lam_k1[0], lam_q2[0], lam_k2[0]])[None, :]
    lnrow = np.stack([ln1_g[0], ln1_b[0], ln2_g[0], ln2_b[0]], 0)
    lncol = np.ascontiguousarray(np.stack([ln1_g[0].reshape(8, 128).T, ln1_b[0].reshape(8, 128).T], 1))
    f = lambda a: np.ascontiguousarray(a, dtype=np.float32)
    return dict(win1=f(win1), win2=f(win2), wab=f(wab), wgab=f(wgab), wo=f(wo), wgu=f(wgu), wdn=f(wdn), wada=f(wada), bcolin=f(bcol),
                sink=f(sink_logit), lamv=f(lamv), subwin=f(subln_w), subwcol=f(subln_w[0][:, None]), lnrow=f(lnrow), lncolin=f(lncol))


_NC_CACHE = {}


def run_jobs(core_jobs, jobs_cfg, weights):
    key = tuple(jobs_cfg)
    if key not in _NC_CACHE:
        _NC_CACHE[key] = build(list(jobs_cfg))
    nc = _NC_CACHE[key]
    shared = dict(weights)
    shared.update(_const_inputs())
    in_maps = []
    for cj in core_jobs:
        m = dict(shared)
        cs_ = np.stack([c for (_, c, _) in cj], 0)
        m["cTin"] = np.ascontiguousarray(cs_.reshape(len(cj), 8, 128).transpose(2, 1, 0)).astype(np.float32)
        for j, (x, c, rev) in enumerate(cj):
            m[f"xin{j}"] = np.ascontiguousarray(x, dtype=np.float32)
            m[f"rope{j}"] = _rope_tables(x.shape[0], rev)
        in_maps.append(m)
    res = run_bass_kernel_spmd(nc, in_maps, core_ids=list(range(len(core_jobs))))
    return [[r[f"yout{j}"] for j in range(len(jobs_cfg))] for r in res.results]


def kernel(x_prompt, x_sample, c_prompt, c_sample, w_ada, b_ada, w_in, sink_logit, lam_q1, lam_k1, lam_q2, lam_k2,
           subln_w, w_a, w_b, w_o, ln1_g, ln1_b, w_gu, w_down, ln2_g, ln2_b):
    a = lambda v: np.asarray(v)
    weights = _weight_inputs(a(w_ada), a(b_ada), a(w_in), a(sink_logit), a(lam_q1), a(lam_k1), a(lam_q2), a(lam_k2), a(subln_w),
                             a(w_a), a(w_b), a(w_o), a(ln1_g), a(ln1_b), a(w_gu), a(w_down), a(ln2_g), a(ln2_b))
    xs_ = [a(x_prompt)[i] for i in range(16)] + [a(x_sample)[i] for i in range(4)]
    cs_ = [a(c_prompt)[i] for i in range(16)] + [a(c_sample)[i] for i in range(4)]
    core_jobs = []
    plan = []
    for c in range(8):
        halves = list(range(5 * c, 5 * c + 5))
        seqs = sorted(set(h // 2 for h in halves))
        full = [s for s in seqs if (2 * s in halves and 2 * s + 1 in halves)]
        part = [h for h in halves if (h // 2) not in full]
        assert len(full) == 2 and len(part) == 1
        hp = part[0]
        rev = (hp % 2 == 1)
        sp_ = hp // 2
        xp = xs_[sp_][::-1] if rev else xs_[sp_]
        core_jobs.append([(xs_[full[0]], cs_[full[0]], False), (xs_[full[1]], cs_[full[1]], False), (xp, cs_[sp_], rev)])
        plan.append((full, sp_, rev))
    outs = run_jobs(core_jobs, ((8, 8), (8, 8), (8, 4)), weights)
    y = np.zeros((20, 4096, D), np.float32)
    for c in range(8):
        full, sp_, rev = plan[c]
        y[full[0]] = outs[c][0]
        y[full[1]] = outs[c][1]
        if rev:
            y[sp_, 2048:] = outs[c][2][::-1]
        else:
            y[sp_, :2048] = outs[c][2]
    return (y[:16], y[16:])
```
